# Optimizing a Trainium2 kernel written in Bass

```python
import math
import jax, jax.numpy as jnp
from jax import lax
import numpy as np

D_MODEL = 1024
BATCH = 8
SEQ = 4096
DEPTH = 2

HEAD_DIM = 64
BLOCK = 128
NEG_INF = -1e30
A_HEADS = D_MODEL // (2 * HEAD_DIM)
A_KV_HEADS = 2
A_WINDOW = 128
A_Q = A_HEADS * HEAD_DIM
A_KV = A_KV_HEADS * HEAD_DIM
B_HEADS = D_MODEL // (2 * HEAD_DIM)
B_Q_RANK = 3 * D_MODEL // 8
B_KV_RANK = D_MODEL // 4
B_NOPE_DIM = 64
B_ROPE_DIM = 32
B_V_DIM = 64
ROPE_BASE = 10000.0
C_HEADS = D_MODEL // HEAD_DIM
GRID_W = 64
C_WIN_H = 8
C_WIN_W = 16
N_EXPERTS = 16
EC_CAPACITY_FACTOR = 2
D_EXPERT = 2 * D_MODEL
LN_EPS = 1e-5
RMS_EPS = 1e-6
DN_ALPHA = (2 * DEPTH) ** 0.25
DN_BETA = (8 * DEPTH) ** -0.25

EVEN_IN_WIDTH = A_Q + 2 * A_KV + B_Q_RANK + B_KV_RANK + B_ROPE_DIM
EVEN_MIX_WIDTH = A_Q + B_HEADS * B_V_DIM
ODD_MIX_WIDTH = C_HEADS * HEAD_DIM

kernel_name = "hybrid_swa_mla_natten_ecmoe_encoder"


def layer_norm(x, g, b):
    xf = x.astype(jnp.float32)
    mu = jnp.mean(xf, -1, keepdims=True)
    var = jnp.mean(jnp.square(xf - mu), -1, keepdims=True)
    return ((xf - mu) * lax.rsqrt(var + LN_EPS) * g.astype(jnp.float32) + b.astype(jnp.float32)).astype(x.dtype)


def rms_norm(x, g):
    xf = x.astype(jnp.float32)
    return (xf * lax.rsqrt(jnp.mean(jnp.square(xf), -1, keepdims=True) + RMS_EPS) * g.astype(jnp.float32)).astype(x.dtype)


def rope(x, pos):
    half = x.shape[-1] // 2
    inv = ROPE_BASE ** (-jnp.arange(half, dtype=jnp.float32) / half)
    ang = pos.astype(jnp.float32)[:, None] * inv[None, :]
    cos = jnp.cos(ang)[None, :, None, :]
    sin = jnp.sin(ang)[None, :, None, :]
    xf = x.astype(jnp.float32)
    x1, x2 = xf[..., :half], xf[..., half:]
    return jnp.concatenate([x1 * cos - x2 * sin, x1 * sin + x2 * cos], -1).astype(x.dtype)


def alibi_slopes(n):
    return 2.0 ** (-8.0 * (jnp.arange(n, dtype=jnp.float32) + 1.0) / n)


def windowed_gqa_sink(q, k, v, sink):
    B, S = q.shape[0], q.shape[1]
    nb = S // BLOCK
    G = A_KV_HEADS
    R = A_HEADS // G
    q = q.reshape(B, S, G, R, HEAD_DIM)
    pad = ((0, 0), (A_WINDOW, A_WINDOW), (0, 0), (0, 0))
    kp = jnp.pad(k, pad)
    vp = jnp.pad(v, pad)
    span = BLOCK + 2 * A_WINDOW
    slopes = alibi_slopes(A_HEADS).reshape(G, R)
    sink_gr = sink.astype(jnp.float32).reshape(G, R)
    scale = HEAD_DIM ** -0.5

    def one_block(i):
        start = i * BLOCK
        qb = lax.dynamic_slice_in_dim(q, start, BLOCK, axis=1)
        kb = lax.dynamic_slice_in_dim(kp, start, span, axis=1)
        vb = lax.dynamic_slice_in_dim(vp, start, span, axis=1)
        s = jnp.einsum('bqgrd,bkgd->bgrqk', qb, kb, preferred_element_type=jnp.float32) * scale
        t = start + jnp.arange(BLOCK)
        src = start - A_WINDOW + jnp.arange(span)
        dist = jnp.abs(t[:, None] - src[None, :])
        valid = (dist <= A_WINDOW) & (src[None, :] >= 0) & (src[None, :] < S)
        s = s - slopes[:, :, None, None] * dist.astype(jnp.float32)
        s = jnp.where(valid, s, NEG_INF)
        m = jnp.maximum(jnp.max(s, -1), sink_gr[:, :, None])
        p = jnp.exp(s - m[..., None])
        denom = jnp.sum(p, -1) + jnp.exp(sink_gr[:, :, None] - m)
        o = jnp.einsum('bgrqk,bkgd->bqgrd', p, vb.astype(jnp.float32))
        o = o / jnp.transpose(denom, (0, 3, 1, 2))[..., None]
        return o.reshape(B, BLOCK, A_HEADS * HEAD_DIM).astype(k.dtype)

    out = lax.map(one_block, jnp.arange(nb))
    return jnp.transpose(out, (1, 0, 2, 3)).reshape(B, S, A_Q)


def mla(c_q, c_kv, k_rope, pos, q_norm_g, w_q_up, kv_norm_g, w_kv_up):
    B, S = c_q.shape[0], c_q.shape[1]
    nb = S // BLOCK
    q = (rms_norm(c_q, q_norm_g) @ w_q_up).reshape(B, S, B_HEADS, B_NOPE_DIM + B_ROPE_DIM)
    q_nope = q[..., :B_NOPE_DIM]
    q_pe = rope(q[..., B_NOPE_DIM:], pos)
    kv = (rms_norm(c_kv, kv_norm_g) @ w_kv_up).reshape(B, S, B_HEADS, B_NOPE_DIM + B_V_DIM)
    k_nope = kv[..., :B_NOPE_DIM]
    v = kv[..., B_NOPE_DIM:]
    k_pe = rope(k_rope[:, :, None, :], pos)[:, :, 0, :]
    scale = (B_NOPE_DIM + B_ROPE_DIM) ** -0.5

    def one_block(i):
        start = i * BLOCK
        qn = lax.dynamic_slice_in_dim(q_nope, start, BLOCK, axis=1)
        qp = lax.dynamic_slice_in_dim(q_pe, start, BLOCK, axis=1)
        s = (jnp.einsum('bqhd,bkhd->bhqk', qn, k_nope, preferred_element_type=jnp.float32)
             + jnp.einsum('bqhd,bkd->bhqk', qp, k_pe, preferred_element_type=jnp.float32)) * scale
        p = jax.nn.softmax(s, axis=-1)
        o = jnp.einsum('bhqk,bkhd->bqhd', p, v.astype(jnp.float32))
        return o.reshape(B, BLOCK, B_HEADS * B_V_DIM).astype(c_q.dtype)

    out = lax.map(one_block, jnp.arange(nb))
    return jnp.transpose(out, (1, 0, 2, 3)).reshape(B, S, B_HEADS * B_V_DIM)


def neighbourhood_attention(q, k, v, rpb):
    B, S = q.shape[0], q.shape[1]
    rows = S // GRID_W
    wh = min(C_WIN_H, rows)
    qg = q.reshape(B, rows, GRID_W, C_HEADS, HEAD_DIM)
    kg = k.reshape(B, rows, GRID_W, C_HEADS, HEAD_DIM)
    vg = v.reshape(B, rows, GRID_W, C_HEADS, HEAD_DIM)
    cols = jnp.arange(GRID_W)
    col_start = jnp.clip(cols - C_WIN_W // 2, 0, GRID_W - C_WIN_W)
    col_valid = (cols[None, :] >= col_start[:, None]) & (cols[None, :] < col_start[:, None] + C_WIN_W)
    dc_idx = jnp.clip(cols[None, :] - cols[:, None] + C_WIN_W - 1, 0, 2 * C_WIN_W - 2)
    scale = HEAD_DIM ** -0.5

    def one_row(r):
        rs = jnp.clip(r - wh // 2, 0, rows - wh)
        qr = lax.dynamic_index_in_dim(qg, r, axis=1, keepdims=False)
        kb = lax.dynamic_slice_in_dim(kg, rs, wh, axis=1)
        vb = lax.dynamic_slice_in_dim(vg, rs, wh, axis=1)
        s = jnp.einsum('bchd,bwxhd->bhcwx', qr, kb, preferred_element_type=jnp.float32) * scale
        dr_idx = rs + jnp.arange(wh) - r + C_WIN_H - 1
        bias = rpb[:, dr_idx[:, None, None], dc_idx[None, :, :]]
        s = s + jnp.transpose(bias, (0, 2, 1, 3))[None].astype(jnp.float32)
        s = jnp.where(col_valid[None, None, :, None, :], s, NEG_INF)
        p = jax.nn.softmax(s.reshape(B, C_HEADS, GRID_W, wh * GRID_W), axis=-1).reshape(s.shape)
        o = jnp.einsum('bhcwx,bwxhd->bchd', p, vb.astype(jnp.float32))
        return o.reshape(B, GRID_W, C_HEADS * HEAD_DIM).astype(q.dtype)

    out = lax.map(one_row, jnp.arange(rows))
    return jnp.transpose(out, (1, 0, 2, 3)).reshape(B, S, C_HEADS * HEAD_DIM)


def expert_choice_moe(h, w_router, w_gate, w_up, w_down):
    B, S, D = h.shape
    cap = EC_CAPACITY_FACTOR * S // N_EXPERTS
    logits = jnp.einsum('bsd,de->bse', h, w_router, preferred_element_type=jnp.float32)
    aff = jax.nn.softmax(logits, axis=-1)
    g, idx = lax.top_k(jnp.transpose(aff, (0, 2, 1)), cap)
    xin = jax.vmap(lambda hb, ib: hb[ib])(h, idx)
    hid = jax.nn.silu(jnp.einsum('becd,edf->becf', xin, w_gate)) * jnp.einsum('becd,edf->becf', xin, w_up)
    y = jnp.einsum('becf,efd->becd', hid, w_down) * g[..., None].astype(h.dtype)
    return jax.vmap(lambda yb, ib: jnp.zeros((S, D), yb.dtype).at[ib.reshape(-1)].add(yb.reshape(-1, D)))(y, idx)


def even_mixer(x, w_in, a_sink, mla_q_norm, w_q_up, mla_kv_norm, w_kv_up, w_out):
    B, S, _ = x.shape
    pos = jnp.arange(S, dtype=jnp.int32)
    proj = x @ w_in
    o1 = A_Q
    o2 = o1 + A_KV
    o3 = o2 + A_KV
    o4 = o3 + B_Q_RANK
    o5 = o4 + B_KV_RANK
    qa, ka, va = proj[..., :o1], proj[..., o1:o2], proj[..., o2:o3]
    c_q, c_kv, k_rope = proj[..., o3:o4], proj[..., o4:o5], proj[..., o5:]
    out_a = windowed_gqa_sink(qa.reshape(B, S, A_HEADS, HEAD_DIM),
                              ka.reshape(B, S, A_KV_HEADS, HEAD_DIM),
                              va.reshape(B, S, A_KV_HEADS, HEAD_DIM), a_sink)
    out_b = mla(c_q, c_kv, k_rope, pos, mla_q_norm, w_q_up, mla_kv_norm, w_kv_up)
    return jnp.concatenate([out_a, out_b], axis=-1) @ w_out


def odd_mixer(x, w_qkv, na_rpb, w_out):
    B, S, _ = x.shape
    qkv = x @ w_qkv
    q = qkv[..., :ODD_MIX_WIDTH].reshape(B, S, C_HEADS, HEAD_DIM)
    k = qkv[..., ODD_MIX_WIDTH:2 * ODD_MIX_WIDTH].reshape(B, S, C_HEADS, HEAD_DIM)
    v = qkv[..., 2 * ODD_MIX_WIDTH:].reshape(B, S, C_HEADS, HEAD_DIM)
    return neighbourhood_attention(q, k, v, na_rpb) @ w_out


def setup_inputs(seed: int = 0) -> dict:
    key = jax.random.key(seed)
    ks = jax.random.split(key, 32)
    f32 = jnp.float32

    def nrm(k, shape, scale):
        return jax.random.normal(k, shape, f32) * scale

    D = D_MODEL
    return {
        "x": nrm(ks[0], (BATCH, SEQ, D), 1.0),
        "w_in0": nrm(ks[1], (D, EVEN_IN_WIDTH), D ** -0.5),
        "a_sink": nrm(ks[2], (A_HEADS,), 0.5),
        "mla_q_norm": 1.0 + nrm(ks[3], (B_Q_RANK,), 0.01),
        "w_q_up": nrm(ks[4], (B_Q_RANK, B_HEADS * (B_NOPE_DIM + B_ROPE_DIM)), B_Q_RANK ** -0.5),
        "mla_kv_norm": 1.0 + nrm(ks[5], (B_KV_RANK,), 0.01),
        "w_kv_up": nrm(ks[6], (B_KV_RANK, B_HEADS * (B_NOPE_DIM + B_V_DIM)), B_KV_RANK ** -0.5),
        "w_out0": nrm(ks[7], (EVEN_MIX_WIDTH, D), DN_BETA * EVEN_MIX_WIDTH ** -0.5),
        "ln0a_g": 1.0 + nrm(ks[8], (D,), 0.01),
        "ln0a_b": nrm(ks[9], (D,), 0.01),
        "router0": nrm(ks[10], (D, N_EXPERTS), D ** -0.5),
        "w_gate0": nrm(ks[11], (N_EXPERTS, D, D_EXPERT), D ** -0.5),
        "w_up0": nrm(ks[12], (N_EXPERTS, D, D_EXPERT), D ** -0.5),
        "w_down0": nrm(ks[13], (N_EXPERTS, D_EXPERT, D), DN_BETA * D_EXPERT ** -0.5),
        "ln0b_g": 1.0 + nrm(ks[14], (D,), 0.01),
        "ln0b_b": nrm(ks[15], (D,), 0.01),
        "w_qkv1": nrm(ks[16], (D, 3 * ODD_MIX_WIDTH), D ** -0.5),
        "na_rpb": nrm(ks[17], (C_HEADS, 2 * C_WIN_H - 1, 2 * C_WIN_W - 1), 0.02),
        "w_out1": nrm(ks[18], (ODD_MIX_WIDTH, D), DN_BETA * ODD_MIX_WIDTH ** -0.5),
        "ln1a_g": 1.0 + nrm(ks[19], (D,), 0.01),
        "ln1a_b": nrm(ks[20], (D,), 0.01),
        "router1": nrm(ks[21], (D, N_EXPERTS), D ** -0.5),
        "w_gate1": nrm(ks[22], (N_EXPERTS, D, D_EXPERT), D ** -0.5),
        "w_up1": nrm(ks[23], (N_EXPERTS, D, D_EXPERT), D ** -0.5),
        "w_down1": nrm(ks[24], (N_EXPERTS, D_EXPERT, D), DN_BETA * D_EXPERT ** -0.5),
        "ln1b_g": 1.0 + nrm(ks[25], (D,), 0.01),
        "ln1b_b": nrm(ks[26], (D,), 0.01),
    }


def reference(x, w_in0, a_sink, mla_q_norm, w_q_up, mla_kv_norm, w_kv_up, w_out0, ln0a_g, ln0a_b,
              router0, w_gate0, w_up0, w_down0, ln0b_g, ln0b_b,
              w_qkv1, na_rpb, w_out1, ln1a_g, ln1a_b,
              router1, w_gate1, w_up1, w_down1, ln1b_g, ln1b_b):
    even_params = [(w_in0, a_sink, mla_q_norm, w_q_up, mla_kv_norm, w_kv_up, w_out0)]
    odd_params = [(w_qkv1, na_rpb, w_out1)]
    mix_norms = [(ln0a_g, ln0a_b), (ln1a_g, ln1a_b)]
    moe_params = [(router0, w_gate0, w_up0, w_down0), (router1, w_gate1, w_up1, w_down1)]
    moe_norms = [(ln0b_g, ln0b_b), (ln1b_g, ln1b_b)]
    h = x
    for layer in range(DEPTH):
        if layer % 2 == 0:
            mix = even_mixer(h, *even_params[layer // 2])
        else:
            mix = odd_mixer(h, *odd_params[layer // 2])
        h = layer_norm(DN_ALPHA * h + mix, *mix_norms[layer])
        h = layer_norm(DN_ALPHA * h + expert_choice_moe(h, *moe_params[layer]), *moe_norms[layer])
    return h
```

```python
import math
import numpy as np
import ml_dtypes
import concourse.bass as bass
import concourse.mybir as mybir
from concourse.bass_utils import run_bass_kernel_spmd

F32 = mybir.dt.float32
BF16 = mybir.dt.bfloat16
F16 = mybir.dt.float16
I32 = mybir.dt.int32
U8 = mybir.dt.uint8
ALU = mybir.AluOpType
AF = mybir.ActivationFunctionType
DSZ = {F32: 4, BF16: 2, F16: 2, I32: 4, U8: 1}

S = 4096
D = 1024
NT = 32
NEG = -1.0e30
ALPHA = 4.0 ** 0.25
N_CORES = 8
ENGS = ['pe', 'act', 'dve', 'pool', 'sp']
N_DSEM = 48


class Sched:
    def __init__(self, nc):
        self.nc = nc
        self.ops = []

    def add(self, eng, fn, r=(), w=(), dma=False):
        self.ops.append(dict(eng=eng, fn=fn, r=list(r), w=list(w), dma=dma, sig=dma, bar=False))

    def barrier(self):
        self.ops.append(dict(eng=None, bar=True))

    def mm(self, out, lhsT, rhs, start, stop, r, w):
        self.add('pe', lambda e: e.matmul(out, lhsT, rhs, start=start, stop=stop), r, w)

    def tr(self, out, in_, ident, r, w):
        self.add('pe', lambda e: e.transpose(out, in_, ident), r, w)

    def act(self, out, in_, func, r, w, bias=None, scale=None, accum=None):
        kw = {}
        if bias is not None:
            kw['bias'] = bias
        if scale is not None:
            kw['scale'] = scale
        if accum is not None:
            kw['accum_out'] = accum
        self.add('act', lambda e: e.activation(out, in_, func, **kw), r, w)

    def ts(self, eng, out, in0, s1, s2, op0, op1, r, w, accum=None):
        if accum is not None:
            self.add(eng, lambda e: e.tensor_scalar(out, in0, s1, s2, op0, op1, accum_out=accum), r, w)
        elif op1 is None:
            self.add(eng, lambda e: e.tensor_scalar(out, in0, s1, None, op0), r, w)
        else:
            self.add(eng, lambda e: e.tensor_scalar(out, in0, s1, s2, op0, op1), r, w)

    def tt(self, eng, out, in0, in1, op, r, w):
        self.add(eng, lambda e: e.tensor_tensor(out, in0, in1, op), r, w)

    def stt(self, eng, out, in0, scalar, in1, op0, op1, r, w):
        self.add(eng, lambda e: e.scalar_tensor_tensor(out, in0, scalar, in1, op0, op1), r, w)

    def copy(self, eng, out, in_, r, w):
        if eng == 'act':
            self.add('act', lambda e: e.copy(out, in_), r, w)
        else:
            self.add(eng, lambda e: e.tensor_copy(out, in_), r, w)

    def recip(self, out, in_, r, w):
        self.add('dve', lambda e: e.reciprocal(out, in_), r, w)

    def memset(self, eng, ap, val, w):
        self.add(eng, lambda e: e.memset(ap, val), (), w)

    def dma(self, eng, out, in_, r, w, **kw):
        self.add(eng, lambda e: e.dma_start(out=out, in_=in_, **kw), r, w, dma=True)

    def idma(self, out, out_off, in_, in_off, r, w, **kw):
        self.add('pool', lambda e: e.indirect_dma_start(out=out, out_offset=out_off, in_=in_,
                                                        in_offset=in_off, **kw), r, w, dma=True)

    def finalize(self):
        nc = self.nc
        ops = self.ops
        last_w, readers = {}, {}
        eng_last = {e: None for e in ENGS}
        outstanding = []
        for i, op in enumerate(ops):
            if op['bar']:
                op['deps'] = [v for v in eng_last.values() if v is not None] + outstanding
                outstanding = []
                last_w.clear()
                readers.clear()
                for d in op['deps']:
                    ops[d]['sig'] = True
                continue
            deps = {}
            for k in op['r']:
                if k in last_w:
                    deps[last_w[k]] = 'raw'
            for k in op['w']:
                if k in last_w:
                    deps.setdefault(last_w[k], 'waw')
                for rr in readers.get(k, ()):
                    deps.setdefault(rr, 'war')
            deps.pop(i, None)
            keep = []
            for d, kind in deps.items():
                p = ops[d]
                if p['eng'] == op['eng'] and not p['dma']:
                    if op['eng'] == 'pe':
                        continue
                keep.append(d)
            op['deps'] = keep
            for d in keep:
                ops[d]['sig'] = True
            for k in op['r']:
                lst = readers.setdefault(k, [])
                if not op['dma']:
                    lst[:] = [q for q in lst if ops[q]['dma'] or ops[q]['eng'] != op['eng']]
                lst.append(i)
            for k in op['w']:
                last_w[k] = i
                readers[k] = []
            if op['dma']:
                outstanding.append(i)
            else:
                eng_last[op['eng']] = i
        cnt = {e: 0 for e in ENGS}
        dcnt = [0] * N_DSEM
        ndq = [0, 0]
        for op in ops:
            if op['bar']:
                continue
            if op['dma']:
                half = N_DSEM // 2
                qi = 0 if op['eng'] == 'sp' else 1
                d = qi * half + (ndq[qi] % half)
                ndq[qi] += 1
                op['dsem'] = d
                op['dprev'] = dcnt[d]
                dcnt[d] += 16
                op['dval'] = dcnt[d]
            elif op['sig']:
                cnt[op['eng']] += 1
                op['sval'] = cnt[op['eng']]
        import contextlib
        with contextlib.ExitStack() as st:
            esem = {e: st.enter_context(nc.semaphore("s_" + e)) for e in ENGS}
            dsem = [st.enter_context(nc.semaphore("d_%d" % i)) for i in range(N_DSEM)]
            block = st.enter_context(nc.Block())

            def waits_for(deps):
                need = {}
                for d in deps:
                    p = ops[d]
                    if p['dma']:
                        key, val = ('d', p['dsem']), p['dval']
                    else:
                        key, val = ('e', p['eng']), p['sval']
                    if need.get(key, 0) < val:
                        need[key] = val
                return need

            def emit(eng, e):
                seen = {}

                def do_waits(need):
                    for key, val in need.items():
                        if seen.get(key, 0) >= val:
                            continue
                        seen[key] = val
                        sem = dsem[key[1]] if key[0] == 'd' else esem[key[1]]
                        e.wait_ge(sem, val)

                for op in ops:
                    if op['bar']:
                        do_waits(waits_for(op['deps']))
                        continue
                    if op['eng'] != eng:
                        continue
                    need = waits_for(op['deps'])
                    if op['dma'] and op['dprev'] > 0:
                        key = ('d', op['dsem'])
                        if need.get(key, 0) < op['dprev']:
                            need[key] = op['dprev']
                    do_waits(need)
                    ins = op['fn'](e)
                    if op['dma']:
                        ins.then_inc(dsem[op['dsem']], 16)
                    elif op['sig']:
                        ins.then_inc(esem[eng], 1)
                if eng == 'sp':
                    for d in range(N_DSEM):
                        if dcnt[d] > 0 and seen.get(('d', d), 0) < dcnt[d]:
                            e.wait_ge(dsem[d], dcnt[d])
                    for en in ENGS:
                        if cnt[en] > 0 and seen.get(('e', en), 0) < cnt[en]:
                            e.wait_ge(esem[en], cnt[en])

            @block.tensor
            def _(e):
                emit('pe', e)

            @block.scalar
            def _(e):
                emit('act', e)

            @block.vector
            def _(e):
                emit('dve', e)

            @block.gpsimd
            def _(e):
                emit('pool', e)

            @block.sync
            def _(e):
                emit('sp', e)


class Arena:
    def __init__(self, nc, nbytes):
        self.t = nc.alloc_sbuf_tensor("arena", [128, nbytes], U8)
        self.cap = nbytes
        self.off = 0

    def alloc(self, shape, dtype):
        n = int(np.prod(shape)) * DSZ[dtype]
        n_al = (n + 63) // 64 * 64
        assert self.off + n_al <= self.cap, ("arena overflow", self.off, n_al, self.cap)
        ap = self.t[:, self.off:self.off + n].bitcast(dtype)
        self.off += n_al
        if len(shape) == 2:
            ap = ap.rearrange("p (a b) -> p a b", a=shape[0])
        elif len(shape) == 3:
            ap = ap.rearrange("p (a b c) -> p a b c", a=shape[0], b=shape[1])
        return ap

    def view(self, off, shape, dtype):
        n = int(np.prod(shape)) * DSZ[dtype]
        assert off + n <= self.cap
        ap = self.t[:, off:off + n].bitcast(dtype)
        if len(shape) == 2:
            ap = ap.rearrange("p (a b) -> p a b", a=shape[0])
        return ap

    def mark(self):
        return self.off

    def release(self, m):
        self.off = m


def build_program(stop_after=None, debug=False):
    nc = bass.Bass("TRN2", target_bir_lowering=False)
    sc = Sched(nc)

    def din(name, shape, dt=F32):
        return nc.dram_tensor(name, list(shape), dt, kind="ExternalInput").ap()

    def dscr(name, shape, dt):
        return nc.dram_tensor(name, list(shape), dt, kind="Internal").ap()

    x = din("x", [S, D])
    w0f = din("w0f", [D, 1664])
    w0v = din("w0v", [D, 128])
    a_sink = din("a_sink", [1, 8])
    gq_d = din("gq", [128, 3])
    gkv_d = din("gkv", [128, 2])
    wq_d = din("wq", [384, 768])
    wqs_d = din("wqs", [384, 768])
    wkv_d = din("wkv", [256, 1024])
    wo_d = [din("wo0", [D, D]), din("wo1", [D, D])]
    lnp = {n: din(n, [1, D]) for n in
           ["ln0a_g", "ln0a_b", "ln0b_g", "ln0b_b", "ln1a_g", "ln1a_b", "ln1b_g", "ln1b_b"]}
    router_d = [din("router0", [D, 16]), din("router1", [D, 16])]
    NE_ = 1 if stop_after in ('INIT', 'L0A', 'L0W', 'L0M', 'L0O') else 16
    wg_d = [din("w_gate0", [NE_, D, 2048]), din("w_gate1", [NE_, D, 2048])]
    wu_d = [din("w_up0", [NE_, D, 2048]), din("w_up1", [NE_, D, 2048])]
    wd_d = [din("w_down0", [NE_, 2048, D]), din("w_down1", [NE_, 2048, D])]
    wqkv_d = din("w_qkv1", [D, 3072])
    rpbT_d = din("rpbT", [16, 128, 16, 64])
    identb_d = din("ident_bf", [128, 128], BF16)
    identf_d = din("ident_f", [128, 128])
    cos_d = din("cos_t", [32, S])
    sin_d = din("sin_t", [32, S])
    biasw_d = din("biasw", [128, 8, 384])
    namask_d = din("namask", [128, 16, 64])
    cvals_d = din("cvals", [128, 4])
    out_d = nc.dram_tensor("out", [S, D], F32, kind="ExternalOutput").ap()

    Hb = dscr("Hb", [S, D], BF16)
    Yd = dscr("Yd", [S, D], F32)
    R1 = dscr("R1", [S, D], F32)
    affd = dscr("affd", [S, 16], F32)
    cumd = dscr("cumd", [16, S], F16)
    mixd = dscr("mixd", [8, 128, S], BF16)
    dbg = {}
    if debug:
        for n, shp, dt in [("dbg_h1", [S, D], F32), ("dbg_mix", [8, 128, S], F32), ("dbg_aff", [S, 16], F32),
                           ("dbg_y", [S, D], F32), ("dbg_cum", [16, S], F32)]:
            dbg[n] = nc.dram_tensor(n, shp, dt, kind="ExternalOutput").ap()

    ar = Arena(nc, 200 * 1024)
    psb = [nc.alloc_psum_tensor("psb%d" % i, [128, 512], F32) for i in range(8)]

    def PS(b):
        return psb[b][:, :]

    def PSB(b):
        return psb[b][:, :].bitcast(BF16)

    def pk(b):
        return "ps%d" % b

    identb = ar.alloc([128], BF16)
    identf = ar.alloc([128], F32)
    onesb = ar.alloc([128], BF16)
    esink = ar.alloc([8], F32)
    cvals = ar.alloc([4], F32)
    eps_rms = ar.alloc([1], F32)
    eps_ln = ar.alloc([1], F32)
    afft = ar.alloc([NT, 16], F32)
    sc.dma('sp', identb, identb_d, (), ['identb'])
    sc.dma('sp', identf, identf_d, (), ['identf'])
    sc.dma('sp', cvals, cvals_d, (), ['cvals'])
    sc.dma('sp', esink, a_sink.partition_broadcast(128), (), ['esink'])
    sc.memset('pool', onesb, 1.0, ['onesb'])
    sc.memset('pool', eps_rms, 1e-6, ['eps'])
    sc.memset('pool', eps_ln, 1e-5, ['eps'])
    sc.act(esink, esink, AF.Exp, ['esink'], ['esink'])
    base_mark = ar.mark()
    if stop_after == 'INIT':
        sc.finalize()
        return nc

    def load_bcast(dst, name, key):
        sc.dma('sp', dst, lnp[name].partition_broadcast(128), (), [key])

    def ln_chunk(src, srck, gB, bB, dst, dstk, st, mv, sd, sfx='', pool_affine=True):
        for hh in range(2):
            sc.add('dve', (lambda a, b: (lambda e: e.bn_stats(a, b)))(st[:, hh, :], src[:, hh * 512:(hh + 1) * 512]),
                   [srck], ['lnst' + sfx])
        sc.add('dve', lambda e: e.bn_aggr(mv, st.rearrange("p a b -> p (a b)")), ['lnst' + sfx], ['lnmv' + sfx])
        sc.act(sd, mv[:, 1:2], AF.Ln, ['lnmv' + sfx, 'eps'], ['lnsd' + sfx], bias=eps_ln)
        sc.act(sd, sd, AF.Exp, ['lnsd' + sfx], ['lnsd' + sfx], scale=-0.5)
        sc.stt('dve', dst, src, mv[:, 0:1], gB, ALU.subtract, ALU.mult, [srck, 'lnmv' + sfx, 'lng'], [dstk])
        sc.stt('dve', dst, dst, sd[:, 0:1], bB, ALU.mult, ALU.add, [dstk, 'lnsd' + sfx, 'lnb'], [dstk])

    def post_h_moe(h, hk, tc, layer, bufs, sfx=''):
        ahb, hbf, hT, ex, ssum, rtr, affT = bufs
        rows = slice(tc * 128, (tc + 1) * 128)
        sc.act(ahb, h, AF.Copy, [hk], ['ahb' + sfx], scale=ALPHA)
        sc.dma('sp', Yd[rows, :], ahb, ['ahb' + sfx], ['Yd'])
        sc.copy('act', hbf, h, [hk], ['hbf' + sfx])
        sc.dma('sp', Hb[rows, :], hbf, ['hbf' + sfx], ['Hb'])
        tb = 4 + (tc % 2)
        for k in range(8):
            sc.tr(PSB(tb)[:, k * 128:(k + 1) * 128], hbf[:, k * 128:(k + 1) * 128], identb, ['hbf' + sfx, 'identb'], [pk(tb)])
        sc.copy('act', hT, PSB(tb).rearrange("p (k t) -> p k t", k=8), [pk(tb)], ['hT' + sfx])
        for k in range(8):
            sc.mm(PS(6)[:, 0:16], hT[:, k, :], rtr[:, k, :], k == 0, k == 7, ['hT' + sfx, 'rtr'], [pk(6)])
        sc.act(ex, PS(6)[:, 0:16], AF.Exp, [pk(6)], ['ex' + sfx, 'ssum' + sfx], accum=ssum)
        sc.recip(ssum, ssum, ['ssum' + sfx], ['ssum' + sfx])
        sc.ts('dve', afft[:, tc, :], ex, ssum[:, 0:1], None, ALU.mult, None, ['ex' + sfx, 'ssum' + sfx], ['afft%d' % tc])
        sc.tr(PS(7)[0:16, (tc % 4) * 128:(tc % 4 + 1) * 128], afft[:, tc, :], identf, ['afft%d' % tc, 'identf'], [pk(7)])
        if tc % 4 == 3:
            g4 = tc // 4
            sc.copy('act', affT[0:16, g4 * 512:(g4 + 1) * 512], PS(7)[0:16, :], [pk(7)], ['affT'])

    def moe_route(affT, rbufs):
        junk, ones16, maskT, cum, lo, mid, cntt, step = rbufs
        sc.memset('pool', ones16, 1.0, ['ones16'])
        sc.memset('dve', lo, 0.0, ['lo'])
        for it in range(27):
            wk = 2.0 ** -(it + 1)
            sc.ts('dve', mid, lo, wk, None, ALU.add, None, ['lo'], ['mid'])
            sc.ts('dve', junk, affT, mid[:, 0:1], 0.0, ALU.is_ge, ALU.add, ['affT', 'mid'], ['junk', 'cnt'], accum=cntt)
            sc.ts('dve', step, cntt, 511.5, wk, ALU.is_ge, ALU.mult, ['cnt'], ['step'])
            sc.tt('dve', lo, lo, step, ALU.add, ['lo', 'step'], ['lo'])
        sc.ts('dve', maskT, affT, lo[:, 0:1], None, ALU.is_ge, None, ['affT', 'lo'], ['maskT'])
        sc.add('dve', lambda e: e.tensor_tensor_scan(cum, ones16, maskT, 0.0, ALU.mult, ALU.add),
               ['ones16', 'maskT'], ['cum'])
        sc.dma('sp', cumd, cum, ['cum'], ['cumd'])
        if debug:
            sc.dma('sp', dbg["dbg_cum"], maskT, ['maskT'], ['dbg_cum'])

    def moe_experts(layer, mb):
        (wring, cumB, idxf, idxi, xg, gg, xgT, hidT, ysb, sg, junkc) = mb
        NSL = len(wring)
        pieces = []
        for e in range(16):
            for fq in range(4):
                pieces.append((e, 'gu', fq))
            for dh in range(2):
                pieces.append((e, 'd', dh))
        state = dict(next_load=0)

        def load_piece(n):
            e, kind, q = pieces[n]
            slot = wring[n % NSL]
            key = 'w%d' % (n % NSL)
            if kind == 'gu':
                sc.dma('pool', slot[:, 0:8, :], wg_d[layer][e][:, q * 512:(q + 1) * 512].rearrange("(k p) f -> p k f", p=128),
                       (), [key])
                sc.dma('pool', slot[:, 8:16, :], wu_d[layer][e][:, q * 512:(q + 1) * 512].rearrange("(k p) f -> p k f", p=128),
                       (), [key])
            else:
                sc.dma('pool', slot, wd_d[layer][e][:, q * 512:(q + 1) * 512].rearrange("(f p) d -> p f d", p=128),
                       (), [key])

        def prefetch(upto):
            while state['next_load'] < min(upto, len(pieces)):
                load_piece(state['next_load'])
                state['next_load'] += 1

        def prep_a(e):
            b = e % 2
            b3 = e % 4
            g3 = e % 3
            sc.dma('sp', cumB[b], cumd[e:e + 1, :].partition_broadcast(128), ['cumd'], ['cumB%d' % b])
            for j in range(4):
                sc.ts('dve', junkc, cumB[b], cvals[:, j:j + 1], 0.0, ALU.is_le, ALU.add,
                      ['cumB%d' % b, 'cvals'], ['junkc', 'idxf%d' % b3], accum=idxf[b3][:, j:j + 1])
            sc.ts('dve', idxf[b3], idxf[b3], float(S - 1), None, ALU.min, None, ['idxf%d' % b3], ['idxf%d' % b3])
            sc.copy('dve', idxi[b3], idxf[b3], ['idxf%d' % b3], ['idxi%d' % b3])

        def prep_a2(e):
            b = e % 2
            b3 = e % 4
            g3 = e % 3
            for j in range(4):
                sc.idma(xg[b][:, j, :], None, Hb[:, :], bass.IndirectOffsetOnAxis(ap=idxi[b3][:, j:j + 1], axis=0),
                        ['idxi%d' % b3, 'Hb'], ['xg%d' % b])
                sc.idma(gg[g3][:, j, :], None, affd[:, :], bass.IndirectOffsetOnAxis(ap=idxi[b3][:, j:j + 1], axis=0),
                        ['idxi%d' % b3, 'affd'], ['gg%d' % g3])

        def prep_b(e):
            b = e % 2
            for j in range(4):
                tb = 0 if j % 2 == 0 else 7
                for k in range(8):
                    sc.tr(PSB(tb)[:, k * 128:(k + 1) * 128], xg[b][:, j, k * 128:(k + 1) * 128], identb,
                          ['xg%d' % b, 'identb'], [pk(tb)])
                sc.copy('act' if j % 2 == 0 else 'dve', xgT[b][:, :, j * 128:(j + 1) * 128],
                        PSB(tb).rearrange("p (k t) -> p k t", k=8), [pk(tb)], ['xgT%d' % b])

        prefetch(NSL)
        prep_a(0)
        prep_a2(0)
        prep_a(1)
        prep_a2(1)
        prep_b(0)
        n = 0
        for e in range(16):
            b = e % 2
            b3 = e % 4
            g3 = e % 3
            if e + 2 < 16:
                prep_a(e + 2)
            for fq in range(4):
                if fq == 2 and e + 2 < 16:
                    prep_a2(e + 2)
                if fq == 3 and e + 1 < 16:
                    prep_b(e + 1)
                slot = wring[n % NSL]
                wkey = 'w%d' % (n % NSL)
                for fcl in range(4):
                    fc = fq * 4 + fcl
                    bg, bu = 1 + (fc % 2), 3 + (fc % 2)
                    for k in range(8):
                        sc.mm(PS(bg), slot[:, k, fcl * 128:(fcl + 1) * 128], xgT[b][:, k, :], k == 0, k == 7,
                              [wkey, 'xgT%d' % b], [pk(bg)])
                    for k in range(8):
                        sc.mm(PS(bu), slot[:, 8 + k, fcl * 128:(fcl + 1) * 128], xgT[b][:, k, :], k == 0, k == 7,
                              [wkey, 'xgT%d' % b], [pk(bu)])
                    sc.act(sg[fc % 2], PS(bg), AF.Silu, [pk(bg)], ['sg%d' % (fc % 2)])
                    sc.tt('dve', hidT[:, fc, :], sg[fc % 2], PS(bu), ALU.mult, ['sg%d' % (fc % 2), pk(bu)], ['hid%d' % fc])
                n += 1
                prefetch(n + NSL)
            for dh in range(2):
                slot = wring[n % NSL]
                wkey = 'w%d' % (n % NSL)
                for j in range(4):
                    bd = 5 + (j % 2)
                    for fc in range(16):
                        sc.mm(PS(bd), hidT[:, fc, j * 128:(j + 1) * 128], slot[:, fc, :], fc == 0, fc == 15,
                              ['hid%d' % fc, wkey], [pk(bd)])
                    sc.act(ysb[:, j, dh * 512:(dh + 1) * 512], PS(bd), AF.Copy, [pk(bd), 'gg%d' % g3], ['ysb%d' % j],
                           scale=gg[g3][:, j, e:e + 1])
                n += 1
                prefetch(n + NSL)
            for j in range(4):
                sc.idma(Yd[:, :], bass.IndirectOffsetOnAxis(ap=idxi[b3][:, j:j + 1], axis=0), ysb[:, j, :], None,
                        ['ysb%d' % j, 'idxi%d' % b3] + ['Yd_%d_%d' % ((e - 1) % 2, jj) for jj in range(4)],
                        ['Yd_%d_%d' % (e % 2, j)], compute_op=ALU.add)

    def moe_phase(layer, affT):
        m0 = ar.mark()
        wring = [ar.alloc([16, 512], BF16) for _ in range(6)]
        m1 = ar.mark()
        junk = ar.alloc([S], BF16)
        ones16 = ar.alloc([S], F32)
        maskT = ar.alloc([S], F32)
        cum = ar.alloc([S], F16)
        smalls = [ar.alloc([1], F32) for _ in range(4)]
        rb = (junk[0:16], ones16[0:16], maskT[0:16], cum[0:16]) + tuple(t[0:16] for t in smalls)
        assert ar.off <= 184 * 1024
        moe_route(affT[0:16], rb)
        sc.barrier()
        ar.release(m1)
        cumB = [ar.alloc([S], F16) for _ in range(2)]
        idxf = [ar.alloc([4], F32) for _ in range(4)]
        idxi = [ar.alloc([4], I32) for _ in range(4)]
        xg = [ar.alloc([4, D], BF16) for _ in range(2)]
        gg = [ar.alloc([4, 16], F32) for _ in range(3)]
        xgT = [ar.alloc([8, 512], BF16) for _ in range(2)]
        hidT = ar.alloc([16, 512], BF16)
        ysb = ar.alloc([4, D], F32)
        sg = [ar.alloc([512], F32) for _ in range(2)]
        junkc = ar.alloc([S], BF16)
        moe_experts(layer, (wring, cumB, idxf, idxi, xg, gg, xgT, hidT, ysb, sg, junkc))
        sc.barrier()
        ar.release(m0)

    def attn_norm_store(po_bank, nq, esk, mixrow, rd, mixt):
        if esk is not None:
            sc.ts('dve', rd[64:128, 0:nq], PS(po_bank)[64:128, 0:nq], esk, None, ALU.add, None,
                  [pk(po_bank), 'esink'], ['rdh'])
        else:
            sc.copy('dve', rd[64:128, 0:nq], PS(po_bank)[64:128, 0:nq], [pk(po_bank)], ['rdh'])
        sc.ts('dve', rd[0:64, 0:nq], rd[64:128, 0:nq], 1.0, None, ALU.mult, None, ['rdh'], ['rd'])
        sc.recip(rd[0:64, 0:nq], rd[0:64, 0:nq], ['rd'], ['rd'])
        sc.tt('dve', mixt[0:64, 0:nq], PS(po_bank)[0:64, 0:nq], rd[0:64, 0:nq], ALU.mult, [pk(po_bank), 'rd'], ['mixt'])
        sc.dma('sp', mixrow, mixt[0:64, 0:nq], ['mixt'], ['mixd'])

    def outproj_ln_phase(layer, wo_ap, gname, bname, resid_load, resid_fn):
        m0 = ar.mark()
        wo = ar.alloc([8, D], BF16)
        gB = ar.alloc([D], F32)
        bB = ar.alloc([D], F32)
        rtr = ar.alloc([8, 16], BF16)
        affT = ar.view(184 * 1024, [S], F32)
        mt = [ar.alloc([8, 128], BF16) for _ in range(2)]
        xc = [ar.alloc([D], F32) for _ in range(2)]
        rbuf = [ar.alloc([D], F32) for _ in range(2)]
        hbuf = [ar.alloc([D], F32) for _ in range(2)]
        ahb = [ar.alloc([D], F32) for _ in range(2)]
        hbf = [ar.alloc([D], BF16) for _ in range(2)]
        hT = [ar.alloc([8, 128], BF16) for _ in range(2)]
        ex = [ar.alloc([16], F32) for _ in range(2)]
        ssum = [ar.alloc([1], F32) for _ in range(2)]
        st = [ar.alloc([2, 6], F32) for _ in range(2)]
        mv = [ar.alloc([2], F32) for _ in range(2)]
        sd = [ar.alloc([1], F32) for _ in range(2)]
        sc.dma('pool', wo, wo_ap.rearrange("(k p) d -> p k d", p=128), (), ['wo'])
        sc.dma('pool', rtr, router_d[layer].rearrange("(k p) e -> p k e", p=128), (), ['rtr'])
        load_bcast(gB, gname, 'lng')
        load_bcast(bB, bname, 'lnb')
        def loads(tc):
            b = tc % 2
            sc.dma('sp', mt[b], mixd[:, :, tc * 128:(tc + 1) * 128].rearrange("f p t -> p f t"), ['mixd'], ['mt%d' % b])
            resid_load(tc, xc[b], 'xc%d' % b)

        def banks_of(tc):
            return (0, 1) if tc % 2 == 0 else (2, 3)

        def mms(tc):
            b = tc % 2
            for half, bank in zip((0, 1), banks_of(tc)):
                for f in range(8):
                    sc.mm(PS(bank), mt[b][:, f, :], wo[:, f, half * 512:(half + 1) * 512], f == 0, f == 7,
                          ['mt%d' % b, 'wo'], [pk(bank)])

        loads(0)
        loads(1)
        mms(0)
        for tc in range(NT):
            b = tc % 2
            pa, pb_ = banks_of(tc)
            if tc + 1 < NT:
                mms(tc + 1)
            sf = str(b)
            resid_fn(tc, pa, pb_, rbuf[b], 'rbuf' + sf, xc[b], 'xc%d' % b)
            if tc + 2 < NT:
                loads(tc + 2)
            ln_chunk(rbuf[b], 'rbuf' + sf, gB, bB, hbuf[b], 'hbuf' + sf, st[b], mv[b], sd[b], sf)
            post_h_moe(hbuf[b], 'hbuf' + sf, tc, layer, (ahb[b], hbf[b], hT[b], ex[b], ssum[b], rtr, affT), sf)
            if debug and layer == 0:
                sc.dma('sp', dbg["dbg_h1"][tc * 128:(tc + 1) * 128, :], hbuf[b], ['hbuf' + sf], ['dbg_h1'])
        sc.dma('sp', affd.rearrange("(c p) e -> p c e", p=128), afft, ['afft%d' % t for t in range(NT)], ['affd'])
        if debug and layer == 0:
            sc.dma('sp', dbg["dbg_aff"].rearrange("(c p) e -> p c e", p=128), afft, ['afft%d' % t for t in range(NT)], ['dbg_aff'])
        sc.barrier()
        ar.release(m0)
        return affT, m0

    qaT = ar.alloc([4, S], BF16)
    kaT2 = ar.alloc([2, S], BF16)
    va = ar.alloc([NT, 2, 128], BF16)
    cqn = ar.alloc([3, S], BF16)
    ckvn = ar.alloc([2, S], BF16)
    KT = [ar.alloc([S], BF16) for _ in range(2)]
    mA = ar.mark()
    w0 = ar.alloc([8, 1792], BF16)
    xb = ar.alloc([4, D], BF16)
    xT = ar.alloc([8, 512], BF16)
    cqg = ar.alloc([3, 512], F32)
    ckg = ar.alloc([2, 512], F32)
    sq = ar.alloc([5, 512], BF16)
    rq = ar.alloc([512], F32)
    rk = ar.alloc([512], F32)
    cs = ar.alloc([512], F32)
    sn = ar.alloc([512], F32)
    t1 = ar.alloc([512], F32)
    t2 = ar.alloc([512], F32)
    gq = ar.alloc([3], F32)
    gkv = ar.alloc([2], F32)
    sc.dma('pool', w0[:, :, 0:1664], w0f.rearrange("(k p) f -> p k f", p=128), (), ['w0'])
    sc.dma('pool', w0[:, :, 1664:1792], w0v.rearrange("(k p) f -> p k f", p=128), (), ['w0'])
    sc.dma('sp', gq, gq_d, (), ['gq'])
    sc.dma('sp', gkv, gkv_d, (), ['gkv'])
    sc.memset('pool', va[:, :, :, 64:128], 1.0, ['va_ones'])
    for i in range(8):
        tl = slice(i * 512, (i + 1) * 512)
        sc.dma('pool', xb, x[tl, :].rearrange("(c p) d -> p c d", p=128), (), ['xb'])
        sc.dma('sp', cs[64:96, :], cos_d[:, tl], (), ['cs'])
        sc.dma('sp', sn[64:96, :], sin_d[:, tl], (), ['sn'])
        for c in range(4):
            tb = c % 2
            for k in range(8):
                sc.tr(PSB(tb)[:, k * 128:(k + 1) * 128], xb[:, c, k * 128:(k + 1) * 128], identb, ['xb', 'identb'], [pk(tb)])
            sc.copy('act' if c % 2 == 0 else 'dve', xT[:, :, c * 128:(c + 1) * 128],
                    PSB(tb).rearrange("p (k t) -> p k t", k=8), [pk(tb)], ['xT'])
        for f in range(13):
            bank = 2 + (f % 3)
            for k in range(8):
                sc.mm(PS(bank), w0[:, k, f * 128:(f + 1) * 128], xT[:, k, :], k == 0, k == 7, ['w0', 'xT'], [pk(bank)])
            if f < 4:
                sc.copy('act', qaT[:, f, tl], PS(bank), [pk(bank)], ['qaT'])
            elif f < 6:
                sc.copy('dve', kaT2[:, f - 4, tl], PS(bank), [pk(bank)], ['kaT2'])
            elif f < 9:
                r = f - 6
                sc.act(sq[:, r, :], PS(bank), AF.Square, [pk(bank)], ['sq%d' % r, 'psord'])
                sc.ts('dve', cqg[:, r, :], PS(bank), gq[:, r:r + 1], None, ALU.mult, None, [pk(bank), 'gq', 'psord'], ['cqg%d' % r])
            elif f < 11:
                r = f - 9
                sc.act(sq[:, 3 + r, :], PS(bank), AF.Square, [pk(bank)], ['sq%d' % (3 + r), 'psord'])
                sc.ts('dve', ckg[:, r, :], PS(bank), gkv[:, r:r + 1], None, ALU.mult, None, [pk(bank), 'gkv', 'psord'], ['ckg%d' % r])
            elif f == 11:
                sc.tt('dve', t1[64:96, :], PS(bank)[64:96, :], cs[64:96, :], ALU.mult, [pk(bank), 'cs'], ['t1'])
            else:
                sc.tt('dve', t2[64:96, :], PS(bank)[64:96, :], sn[64:96, :], ALU.mult, [pk(bank), 'sn'], ['t2'])
                sc.tt('pool', KT[0][64:96, tl], t1[64:96, :], t2[64:96, :], ALU.add, ['t1', 't2'], ['KT0pe'])
                sc.copy('pool', KT[1][64:96, tl], KT[0][64:96, tl], ['KT0pe'], ['KT1pe'])
        for r in range(3):
            sc.mm(PS(5), onesb, sq[:, r, :], r == 0, r == 2, ['onesb', 'sq%d' % r], [pk(5)])
        for r in range(2):
            sc.mm(PS(6), onesb, sq[:, 3 + r, :], r == 0, r == 1, ['onesb', 'sq%d' % (3 + r)], [pk(6)])
        sc.act(rq, PS(5), AF.Sqrt, [pk(5), 'eps'], ['rq'], bias=eps_rms, scale=1.0 / 384.0)
        sc.recip(rq, rq, ['rq'], ['rq'])
        sc.act(rk, PS(6), AF.Sqrt, [pk(6), 'eps'], ['rk'], bias=eps_rms, scale=1.0 / 256.0)
        sc.recip(rk, rk, ['rk'], ['rk'])
        for r in range(3):
            sc.tt('dve', cqn[:, r, tl], cqg[:, r, :], rq, ALU.mult, ['cqg%d' % r, 'rq'], ['cqn'])
        for r in range(2):
            sc.tt('dve', ckvn[:, r, tl], ckg[:, r, :], rk, ALU.mult, ['ckg%d' % r, 'rk'], ['ckvn'])
        for c in range(4):
            for k in range(8):
                sc.mm(PS(7)[:, c * 128:(c + 1) * 128], xT[:, k, c * 128:(c + 1) * 128], w0[:, k, 1664:1792],
                      k == 0, k == 7, ['xT', 'w0'], [pk(7)])
        sc.copy('act', va[:, i * 4:(i + 1) * 4, :, 0:64], PS(7).rearrange("p (c g d) -> p c g d", c=4, g=2),
                [pk(7)], ['va'])
    sc.barrier()
    ar.release(mA)
    if stop_after == 'L0A':
        sc.finalize()
        return nc

    mW = ar.mark()
    biasw = ar.alloc([8, 384], F32)
    pT = [ar.alloc([384], BF16) for _ in range(4)]
    tbuf = [ar.alloc([384], F32) for _ in range(2)]
    rd = ar.alloc([512], F32)
    mixt = ar.alloc([512], BF16)
    sc.dma('sp', biasw, biasw_d, (), ['biasw'])
    scale_a = 64.0 ** -0.5
    for h in range(8):
        g = h // 4
        f = h // 2
        rbs = (h % 2) * 64
        pr = slice(rbs, rbs + 64)

        def q0_of(j):
            return max(0, (j - 1) * 128)

        def s_step(j):
            q0 = q0_of(j)
            q1 = min(S, (j + 2) * 128)
            n = q1 - q0
            off = q0 - (j - 1) * 128
            bank = j % 3
            sc.mm(PS(bank)[:, 0:n], kaT2[pr, g, j * 128:(j + 1) * 128], qaT[pr, f, q0:q1], True, True,
                  ['kaT2', 'qaT'], [pk(bank)])
            tbk = 'tb%d' % (j % 2)
            sc.stt('dve', tbuf[j % 2][:, 0:n], PS(bank)[:, 0:n], scale_a, biasw[:, h, off:off + n], ALU.mult, ALU.add,
                   [pk(bank), 'biasw'], [tbk])
            sc.act(pT[j % 4][:, 0:n], tbuf[j % 2][:, 0:n], AF.Exp, [tbk], ['pT%d' % (j % 4)])

        def pv_step(i):
            pob = 3 + ((i // 4) % 2)
            js = [jj for jj in (i - 1, i, i + 1) if 0 <= jj < NT]
            for n_, jj in enumerate(js):
                c0 = i * 128 - q0_of(jj)
                sc.mm(PS(pob)[:, (i % 4) * 128:(i % 4 + 1) * 128], va[:, jj, g, :], pT[jj % 4][:, c0:c0 + 128],
                      n_ == 0, n_ == len(js) - 1, ['va', 'va_ones', 'pT%d' % (jj % 4)], [pk(pob)])
            if i % 4 == 3:
                t0 = (i // 4) * 512
                attn_norm_store(pob, 512, esink[64:128, h:h + 1], mixd[f, pr, t0:t0 + 512], rd, mixt)

        s_step(0)
        for j in range(1, NT):
            s_step(j)
            pv_step(j - 1)
        pv_step(NT - 1)
    sc.barrier()
    ar.release(mW)
    if stop_after == 'L0W':
        sc.finalize()
        return nc

    mM = ar.mark()
    wq = ar.alloc([3, 768], BF16)
    wqs = ar.alloc([3, 768], BF16)
    wkv = ar.alloc([2, 1024], BF16)
    QT = [ar.alloc([S], BF16) for _ in range(2)]
    vm = [ar.alloc([NT, 128], BF16) for _ in range(2)]
    csq = [ar.alloc([512], F32) for _ in range(2)]
    snq = [ar.alloc([512], F32) for _ in range(2)]
    u1 = ar.alloc([512], F32)
    u2 = ar.alloc([512], F32)
    pbuf = [ar.alloc([512], BF16) for _ in range(4)]
    rd = ar.alloc([512], F32)
    mixt = ar.alloc([512], BF16)
    sc.dma('pool', wq, wq_d.rearrange("(k p) f -> p k f", p=128), (), ['wq'])
    sc.dma('pool', wqs, wqs_d.rearrange("(k p) f -> p k f", p=128), (), ['wqs'])
    sc.dma('pool', wkv, wkv_d.rearrange("(k p) f -> p k f", p=128), (), ['wkv'])
    for b in range(2):
        sc.memset('pool', vm[b][:, :, 64:128], 1.0, ['vm_ones%d' % b])
    scale_b = 96.0 ** -0.5

    def mla_proj(h):
        b = h % 2
        for i in range(8):
            tl = slice(i * 512, (i + 1) * 512)
            cb = i % 2
            sc.dma('sp', csq[cb][64:96, :], cos_d[:, tl], (), ['csq%d' % cb])
            sc.dma('sp', snq[cb][64:96, :], sin_d[:, tl], (), ['snq%d' % cb])
            for r in range(3):
                sc.mm(PS(5)[0:96, :], wq[:, r, h * 96:(h + 1) * 96], cqn[:, r, tl], r == 0, r == 2, ['wq', 'cqn'], [pk(5)])
            for r in range(3):
                sc.mm(PS(6)[0:96, :], wqs[:, r, h * 96:(h + 1) * 96], cqn[:, r, tl], r == 0, r == 2, ['wqs', 'cqn'], [pk(6)])
            sc.copy('act', QT[b][0:64, tl], PS(5)[0:64, :], [pk(5)], ['QTn%d' % b, 'psord'])
            sc.tt('dve', u1[64:96, :], PS(5)[64:96, :], csq[cb][64:96, :], ALU.mult, [pk(5), 'csq%d' % cb, 'psord'], ['u1'])
            sc.tt('dve', u2[64:96, :], PS(6)[64:96, :], snq[cb][64:96, :], ALU.mult, [pk(6), 'snq%d' % cb], ['u2'])
            sc.tt('pool', QT[b][64:96, tl], u1[64:96, :], u2[64:96, :], ALU.add, ['u1', 'u2'], ['QTp%d' % b])
            for c in range(2):
                sc.mm(PS(7)[0:64, :], wkv[:, c, h * 128:h * 128 + 64], ckvn[:, c, tl], c == 0, c == 1, ['wkv', 'ckvn'], [pk(7)])
            sc.copy('act', KT[b][0:64, tl], PS(7)[0:64, :], [pk(7)], ['KTn%d' % b])
        for g4 in range(4):
            for t8 in range(8):
                tc = g4 * 8 + t8
                for c in range(2):
                    sc.mm(PS(7)[:, t8 * 64:(t8 + 1) * 64], ckvn[:, c, tc * 128:(tc + 1) * 128],
                          wkv[:, c, h * 128 + 64:h * 128 + 128], c == 0, c == 1, ['ckvn', 'wkv'], [pk(7)])
            sc.copy('dve', vm[b][:, g4 * 8:(g4 + 1) * 8, 0:64], PS(7).rearrange("p (t d) -> p t d", t=8), [pk(7)], ['vm%d' % b])

    def mla_attn(h):
        b = h % 2
        f = 4 + h // 2
        rbs = (h % 2) * 64
        kkeys = ['KTn%d' % b, 'KT%dpe' % b]
        qkeys = ['QTn%d' % b, 'QTp%d' % b]
        cnt = 0
        for i in range(8):
            tl = slice(i * 512, (i + 1) * 512)
            pob = 3 + (i % 2)

            def s_step(kc, slot):
                bank = slot % 3
                sc.mm(PS(bank), KT[b][0:96, kc * 128:(kc + 1) * 128], QT[b][0:96, tl], True, True, kkeys + qkeys, [pk(bank)])
                sc.act(pbuf[slot % 4], PS(bank), AF.Exp, [pk(bank)], ['pb%d' % (slot % 4)], scale=scale_b)

            def pv_step(kc, slot):
                sc.mm(PS(pob), vm[b][:, kc, :], pbuf[slot % 4], kc == 0, kc == NT - 1,
                      ['vm%d' % b, 'vm_ones%d' % b, 'pb%d' % (slot % 4)], [pk(pob)])

            s_step(0, cnt)
            s_step(1, cnt + 1)
            for kc in range(NT):
                if kc + 2 < NT:
                    s_step(kc + 2, cnt + kc + 2)
                pv_step(kc, cnt + kc)
            cnt += NT
            attn_norm_store(pob, 512, None, mixd[f, rbs:rbs + 64, tl], rd, mixt)

    mla_proj(0)
    for h in range(8):
        if h + 1 < 8:
            mla_proj(h + 1)
        mla_attn(h)
    sc.barrier()
    ar.release(base_mark)

    if debug:
        mD = ar.mark()
        mtb = ar.alloc([S], BF16)
        mtf = ar.alloc([S], F32)
        for f in range(8):
            sc.dma('sp', mtb, mixd[f], ['mixd'], ['mtb'])
            sc.copy('dve', mtf, mtb, ['mtb'], ['mtf'])
            sc.dma('sp', dbg["dbg_mix"][f], mtf, ['mtf'], ['dbg_mix'])
        sc.barrier()
        ar.release(mD)

    def resid0_load(tc, xcb, xck):
        sc.dma('sp', xcb, x[tc * 128:(tc + 1) * 128, :], (), [xck])

    def resid0(tc, pa, pb_, rbuf, rbk, xcb, xck):
        for half, bank in ((0, pa), (1, pb_)):
            sc.stt('dve', rbuf[:, half * 512:(half + 1) * 512], xcb[:, half * 512:(half + 1) * 512], ALPHA, PS(bank),
                   ALU.mult, ALU.add, [xck, pk(bank)], [rbk])

    affT, _ = outproj_ln_phase(0, wo_d[0], "ln0a_g", "ln0a_b", resid0_load, resid0)
    if stop_after == 'L0O':
        sc.finalize()
        return nc

    moe_phase(0, affT)
    ar.release(base_mark)
    if debug:
        mD = ar.mark()
        yb = ar.alloc([D], F32)
        for tc in range(NT):
            sc.dma('sp', yb, Yd[tc * 128:(tc + 1) * 128, :], ['Yd'], ['yb'])
            sc.dma('sp', dbg["dbg_y"][tc * 128:(tc + 1) * 128, :], yb, ['yb'], ['dbg_y'])
        sc.barrier()
        ar.release(mD)
    if stop_after == 'MOE0':
        sc.finalize()
        return nc

    h2T = ar.alloc([8, S], BF16)
    mL1 = ar.mark()
    gB = ar.alloc([D], F32)
    bB = ar.alloc([D], F32)
    yc = [ar.alloc([D], F32) for _ in range(2)]
    hbuf = [ar.alloc([D], F32) for _ in range(2)]
    ahb = [ar.alloc([D], F32) for _ in range(2)]
    hbf = [ar.alloc([D], BF16) for _ in range(2)]
    st = [ar.alloc([2, 6], F32) for _ in range(2)]
    mv = [ar.alloc([2], F32) for _ in range(2)]
    sd = [ar.alloc([1], F32) for _ in range(2)]
    load_bcast(gB, "ln0b_g", 'lng')
    load_bcast(bB, "ln0b_b", 'lnb')
    sc.dma('sp', yc[0], Yd[0:128, :], ['Yd'], ['yc0'])
    for tc in range(NT):
        b = tc % 2
        sf = str(b)
        rows = slice(tc * 128, (tc + 1) * 128)
        if tc + 1 < NT:
            sc.dma('sp', yc[1 - b], Yd[(tc + 1) * 128:(tc + 2) * 128, :], ['Yd'], ['yc%d' % (1 - b)])
        ln_chunk(yc[b], 'yc%d' % b, gB, bB, hbuf[b], 'hbuf' + sf, st[b], mv[b], sd[b], sf)
        sc.act(ahb[b], hbuf[b], AF.Copy, ['hbuf' + sf], ['ahb' + sf], scale=ALPHA)
        sc.dma('sp', R1[rows, :], ahb[b], ['ahb' + sf], ['R1'])
        sc.copy('act', hbf[b], hbuf[b], ['hbuf' + sf], ['hbf' + sf])
        tb = tc % 2
        for k in range(8):
            sc.tr(PSB(tb)[:, k * 128:(k + 1) * 128], hbf[b][:, k * 128:(k + 1) * 128], identb, ['hbf' + sf, 'identb'], [pk(tb)])
        sc.copy('dve', h2T[:, :, rows], PSB(tb).rearrange("p (k t) -> p k t", k=8), [pk(tb)], ['h2T'])
    sc.barrier()
    ar.release(mL1)

    wp = [ar.alloc([8, 384], BF16) for _ in range(2)]
    QTp = [ar.alloc([S], BF16) for _ in range(2)]
    KTp = [ar.alloc([S], BF16) for _ in range(2)]
    Vp = [ar.alloc([NT, 2, 128], BF16) for _ in range(2)]
    TTp = [ar.alloc([2, 16, 64], F32) for _ in range(2)]
    TTb = [ar.alloc([2, 16, 64], BF16) for _ in range(2)]
    namask = ar.alloc([16, 64], F32)
    tbn = [ar.alloc([5, 64], F32) for _ in range(4)]
    pTn = [ar.alloc([5, 64], BF16) for _ in range(5)]
    rd = ar.alloc([512], F32)
    mixt = ar.alloc([512], BF16)
    sc.dma('sp', namask, namask_d, (), ['namask'])
    for b in range(2):
        sc.memset('pool', Vp[b][:, :, :, 64:128], 1.0, ['vp_ones%d' % b])
    scale_c = 64.0 ** -0.5

    def na_proj(hp):
        b = hp % 2
        for part in range(3):
            sc.dma('pool', wp[b][:, :, part * 128:(part + 1) * 128],
                   wqkv_d[:, part * 1024 + hp * 128: part * 1024 + (hp + 1) * 128].rearrange("(k p) f -> p k f", p=128),
                   (), ['wp%d' % b])
        for g in range(2):
            sc.dma('sp', TTp[b][:, g], rpbT_d[hp * 2 + g], (), ['TT%d_%d' % (b, g)])
            sc.tt('pool', TTp[b][:, g], TTp[b][:, g], namask, ALU.add, ['TT%d_%d' % (b, g), 'namask'], ['TT%d_%d' % (b, g)])
            sc.ts('pool', TTb[b][:, g], TTp[b][:, g], 1.0 / scale_c, None, ALU.mult, None, ['TT%d_%d' % (b, g)], ['TTb%d_%d' % (b, g)])
        for i in range(8):
            tl = slice(i * 512, (i + 1) * 512)
            for k in range(8):
                sc.mm(PS(5), wp[b][:, k, 0:128], h2T[:, k, tl], k == 0, k == 7, ['wp%d' % b, 'h2T'], [pk(5)])
            sc.copy('act', QTp[b][:, tl], PS(5), [pk(5)], ['QTp%d' % b])
            for k in range(8):
                sc.mm(PS(6), wp[b][:, k, 128:256], h2T[:, k, tl], k == 0, k == 7, ['wp%d' % b, 'h2T'], [pk(6)])
            sc.copy('dve', KTp[b][:, tl], PS(6), [pk(6)], ['KTp%d' % b])
            for c in range(4):
                tc = i * 4 + c
                for k in range(8):
                    sc.mm(PS(5)[:, c * 128:(c + 1) * 128], h2T[:, k, tc * 128:(tc + 1) * 128], wp[b][:, k, 256:384],
                          k == 0, k == 7, ['h2T', 'wp%d' % b], [pk(5)])
            sc.copy('act', Vp[b][:, i * 4:(i + 1) * 4, :, 0:64], PS(5).rearrange("p (c g d) -> p c g d", c=4, g=2),
                    [pk(5)], ['Vp%d' % b])

    def na_attn(hp):
        b = hp % 2
        tasks = [(g, r) for g in range(2) for r in range(64)]

        def geom(r):
            rs = min(max(r - 4, 0), 56)
            odd = rs % 2
            kr0 = rs - odd
            nch = 5 if odd else 4
            return odd, kr0, nch, kr0 - r + 8

        def s_part(t):
            g, r = tasks[t]
            pr = slice(g * 64, g * 64 + 64)
            odd, kr0, nch, u0 = geom(r)
            bank = (0, 1, 2, 7)[t % 4]
            for c in range(nch):
                kc = kr0 // 2 + c
                sc.mm(PS(bank)[:, c * 64:(c + 1) * 64], KTp[b][pr, kc * 128:(kc + 1) * 128], QTp[b][pr, r * 64:(r + 1) * 64],
                      True, False, ['KTp%d' % b, 'QTp%d' % b], [pk(bank)])
                sc.mm(PS(bank)[:, c * 64:(c + 1) * 64], identb, TTb[b][:, g, u0 + 2 * c, :],
                      False, True, ['identb', 'TTb%d_%d' % (b, g)], [pk(bank)])
            sc.act(pTn[t % 5][:, 0:nch, :], PS(bank)[:, 0:nch * 64].rearrange("p (c q) -> p c q", c=nch), AF.Exp,
                   [pk(bank)], ['pTn%d' % (t % 5)], scale=scale_c)

        def pv_part(t):
            g, r = tasks[t]
            pr = slice(g * 64, g * 64 + 64)
            odd, kr0, nch, u0 = geom(r)
            pob = 3 + ((r // 8) % 2)
            pkk = 'pTn%d' % (t % 5)
            for c in range(nch):
                kc = kr0 // 2 + c
                if odd and c == 0:
                    ps_ = slice(64, 128)
                elif odd and c == nch - 1:
                    ps_ = slice(0, 64)
                else:
                    ps_ = slice(0, 128)
                sc.mm(PS(pob)[:, (r % 8) * 64:(r % 8 + 1) * 64], Vp[b][ps_, kc, g, :], pTn[t % 5][ps_, c, :],
                      c == 0, c == nch - 1, ['Vp%d' % b, 'vp_ones%d' % b, pkk], [pk(pob)])
            if r % 8 == 7:
                t0 = (r // 8) * 512
                attn_norm_store(pob, 512, None, mixd[hp, pr, t0:t0 + 512], rd, mixt)

        LA = 3
        for t in range(min(LA, len(tasks))):
            s_part(t)
        for t in range(len(tasks)):
            if t + LA < len(tasks):
                s_part(t + LA)
            pv_part(t)

    na_proj(0)
    for hp in range(8):
        if hp + 1 < 8:
            na_proj(hp + 1)
        na_attn(hp)
    sc.barrier()
    ar.release(base_mark)

    def resid1_load(tc, xcb, xck):
        sc.dma('sp', xcb, R1[tc * 128:(tc + 1) * 128, :], ['R1'], [xck])

    def resid1(tc, pa, pb_, rbuf, rbk, xcb, xck):
        for half, bank in ((0, pa), (1, pb_)):
            sc.tt('dve', rbuf[:, half * 512:(half + 1) * 512], xcb[:, half * 512:(half + 1) * 512], PS(bank),
                  ALU.add, [xck, pk(bank)], [rbk])

    affT, _ = outproj_ln_phase(1, wo_d[1], "ln1a_g", "ln1a_b", resid1_load, resid1)
    moe_phase(1, affT)
    ar.release(base_mark)

    gB = ar.alloc([D], F32)
    bB = ar.alloc([D], F32)
    yc = [ar.alloc([D], F32) for _ in range(2)]
    hb2 = [ar.alloc([D], F32) for _ in range(2)]
    st = [ar.alloc([2, 6], F32) for _ in range(2)]
    mv = [ar.alloc([2], F32) for _ in range(2)]
    sd = [ar.alloc([1], F32) for _ in range(2)]
    load_bcast(gB, "ln1b_g", 'lng')
    load_bcast(bB, "ln1b_b", 'lnb')
    sc.dma('sp', yc[0], Yd[0:128, :], ['Yd'], ['yc0'])
    for tc in range(NT):
        b = tc % 2
        rows = slice(tc * 128, (tc + 1) * 128)
        if tc + 1 < NT:
            sc.dma('sp', yc[1 - b], Yd[(tc + 1) * 128:(tc + 2) * 128, :], ['Yd'], ['yc%d' % (1 - b)])
        ln_chunk(yc[b], 'yc%d' % b, gB, bB, hb2[b], 'hb%d' % b, st[b], mv[b], sd[b], str(b))
        sc.dma('sp', out_d[rows, :], hb2[b], ['hb%d' % b], ['out'])
    sc.finalize()
    return nc


def _consts():
    c = {}
    c["ident_bf"] = np.eye(128, dtype=np.float32).astype(ml_dtypes.bfloat16)
    c["ident_f"] = np.eye(128, dtype=np.float32)
    half = 16
    inv = (10000.0 ** (-np.arange(half, dtype=np.float32) / half)).astype(np.float32)
    ang = np.arange(S, dtype=np.float32)[None, :] * inv[:, None]
    cos = np.cos(ang).astype(np.float32)
    sin = np.sin(ang).astype(np.float32)
    c["cos_t"] = np.concatenate([cos, cos], 0)
    c["sin_t"] = np.concatenate([-sin, sin], 0)
    k = np.arange(128)[:, None]
    qp = np.arange(384)[None, :]
    dist = np.abs(qp - 128 - k).astype(np.float32)
    slopes = 2.0 ** (-8.0 * (np.arange(8, dtype=np.float32) + 1.0) / 8)
    bw = np.where(dist[:, None, :] <= 128, -slopes[None, :, None] * dist[:, None, :], NEG).astype(np.float32)
    c["biasw"] = np.ascontiguousarray(bw)
    cols = np.arange(64)
    cstart = np.clip(cols - 8, 0, 48)
    valid = (cols[None, :] >= cstart[:, None]) & (cols[None, :] < cstart[:, None] + 16)
    m = np.where(valid.T, 0.0, NEG).astype(np.float32)
    m2 = np.concatenate([m, m], 0)
    c["namask"] = np.ascontiguousarray(np.broadcast_to(m2[:, None, :], (128, 16, 64)))
    c["cvals"] = (np.arange(4)[None, :] * 128 + np.arange(128)[:, None]).astype(np.float32)
    return c


def _prep_shared(inp):
    f32 = np.float32
    d = {}
    w_in0 = np.asarray(inp["w_in0"], f32)
    qa = w_in0[:, 0:512]
    ka0, ka1 = w_in0[:, 512:576], w_in0[:, 576:640]
    vaw = w_in0[:, 640:768]
    cq = w_in0[:, 768:1152]
    ckv = w_in0[:, 1152:1408]
    kr = w_in0[:, 1408:1440]
    krs = np.concatenate([kr[:, 16:32], kr[:, 0:16]], 1)
    d["w0f"] = np.ascontiguousarray(np.concatenate(
        [qa, ka0, ka0, ka1, ka1, cq, ckv, kr, kr, kr, kr, krs, krs, krs, krs], 1))
    d["w0v"] = np.ascontiguousarray(vaw)
    d["a_sink"] = np.asarray(inp["a_sink"], f32).reshape(1, 8)
    d["gq"] = np.ascontiguousarray(np.asarray(inp["mla_q_norm"], f32).reshape(3, 128).T)
    d["gkv"] = np.ascontiguousarray(np.asarray(inp["mla_kv_norm"], f32).reshape(2, 128).T)
    wq = np.asarray(inp["w_q_up"], f32)
    d["wq"] = wq
    wq3 = wq.reshape(384, 8, 96)
    d["wqs"] = np.ascontiguousarray(
        np.concatenate([wq3[:, :, 0:64], wq3[:, :, 80:96], wq3[:, :, 64:80]], 2).reshape(384, 768))
    d["wkv"] = np.asarray(inp["w_kv_up"], f32)
    d["wo0"] = np.asarray(inp["w_out0"], f32)
    d["wo1"] = np.asarray(inp["w_out1"], f32)
    for n in ["ln0a_g", "ln0a_b", "ln0b_g", "ln0b_b", "ln1a_g", "ln1a_b", "ln1b_g", "ln1b_b"]:
        d[n] = np.asarray(inp[n], f32).reshape(1, D)
    for n in ["router0", "router1", "w_gate0", "w_gate1", "w_up0", "w_up1", "w_down0", "w_down1", "w_qkv1"]:
        d[n] = np.asarray(inp[n], f32)
    rpb = np.asarray(inp["na_rpb"], f32)
    cols = np.arange(64)
    dc = np.clip(cols[None, :] - cols[:, None] + 15, 0, 30)
    u = np.arange(16)
    out = np.empty((16, 2, 64, 16, 64), f32)
    for kr2 in range(2):
        dr = np.clip(u + kr2 - 8, -7, 7) + 7
        g_ = rpb[:, dr[:, None, None], dc[None, :, :]]
        out[:, kr2] = np.transpose(g_, (0, 3, 1, 2))
    d["rpbT"] = np.ascontiguousarray(out.reshape(16, 128, 16, 64))
    d.update(_consts())
    return d


_CACHE = {}


def kernel(**inputs):
    x = np.asarray(inputs["x"], np.float32)
    shared = _prep_shared(inputs)
    if "nc" not in _CACHE:
        _CACHE["nc"] = build_program()
    nc = _CACHE["nc"]
    in_maps = []
    for c in range(N_CORES):
        m = dict(shared)
        m["x"] = np.ascontiguousarray(x[c])
        in_maps.append(m)
    res = run_bass_kernel_spmd(nc, in_maps, core_ids=list(range(N_CORES)))
    return np.stack([np.asarray(r["out"], np.float32) for r in res.results], 0)
```

```python
import math
import numpy as np
import ml_dtypes
import concourse.bass as bass
import concourse.mybir as mybir
from concourse.bass_utils import run_bass_kernel_spmd

F32 = mybir.dt.float32
BF16 = mybir.dt.bfloat16
F16 = mybir.dt.float16
I32 = mybir.dt.int32
U8 = mybir.dt.uint8
ALU = mybir.AluOpType
AF = mybir.ActivationFunctionType
DSZ = {F32: 4, BF16: 2, F16: 2, I32: 4, U8: 1}

S = 4096
D = 1024
NT = 32
NEG = -1.0e30
ALPHA = 4.0 ** 0.25
N_CORES = 8
ENGS = ['pe', 'act', 'dve', 'pool', 'sp']
N_DSEM = 48


class Sched:
    def __init__(self, nc):
        self.nc = nc
        self.ops = []

    def add(self, eng, fn, r=(), w=(), dma=False):
        self.ops.append(dict(eng=eng, fn=fn, r=list(r), w=list(w), dma=dma, sig=dma, bar=False))

    def barrier(self):
        self.ops.append(dict(eng=None, bar=True))

    def mm(self, out, lhsT, rhs, start, stop, r, w):
        self.add('pe', lambda e: e.matmul(out, lhsT, rhs, start=start, stop=stop), r, w)

    def tr(self, out, in_, ident, r, w):
        self.add('pe', lambda e: e.transpose(out, in_, ident), r, w)

    def act(self, out, in_, func, r, w, bias=None, scale=None, accum=None):
        kw = {}
        if bias is not None:
            kw['bias'] = bias
        if scale is not None:
            kw['scale'] = scale
        if accum is not None:
            kw['accum_out'] = accum
        self.add('act', lambda e: e.activation(out, in_, func, **kw), r, w)

    def ts(self, eng, out, in0, s1, s2, op0, op1, r, w, accum=None):
        if accum is not None:
            self.add(eng, lambda e: e.tensor_scalar(out, in0, s1, s2, op0, op1, accum_out=accum), r, w)
        elif op1 is None:
            self.add(eng, lambda e: e.tensor_scalar(out, in0, s1, None, op0), r, w)
        else:
            self.add(eng, lambda e: e.tensor_scalar(out, in0, s1, s2, op0, op1), r, w)

    def tt(self, eng, out, in0, in1, op, r, w):
        self.add(eng, lambda e: e.tensor_tensor(out, in0, in1, op), r, w)

    def stt(self, eng, out, in0, scalar, in1, op0, op1, r, w):
        self.add(eng, lambda e: e.scalar_tensor_tensor(out, in0, scalar, in1, op0, op1), r, w)

    def copy(self, eng, out, in_, r, w):
        if eng == 'act':
            self.add('act', lambda e: e.copy(out, in_), r, w)
        else:
            self.add(eng, lambda e: e.tensor_copy(out, in_), r, w)

    def recip(self, out, in_, r, w):
        self.add('dve', lambda e: e.reciprocal(out, in_), r, w)

    def memset(self, eng, ap, val, w):
        self.add(eng, lambda e: e.memset(ap, val), (), w)

    def dma(self, eng, out, in_, r, w, **kw):
        self.add(eng, lambda e: e.dma_start(out=out, in_=in_, **kw), r, w, dma=True)

    def idma(self, out, out_off, in_, in_off, r, w, **kw):
        self.add('pool', lambda e: e.indirect_dma_start(out=out, out_offset=out_off, in_=in_,
                                                        in_offset=in_off, **kw), r, w, dma=True)

    def finalize(self):
        nc = self.nc
        ops = self.ops
        last_w, readers = {}, {}
        eng_last = {e: None for e in ENGS}
        outstanding = []
        for i, op in enumerate(ops):
            if op['bar']:
                op['deps'] = [v for v in eng_last.values() if v is not None] + outstanding
                outstanding = []
                last_w.clear()
                readers.clear()
                for d in op['deps']:
                    ops[d]['sig'] = True
                continue
            deps = {}
            for k in op['r']:
                if k in last_w:
                    deps[last_w[k]] = 'raw'
            for k in op['w']:
                if k in last_w:
                    deps.setdefault(last_w[k], 'waw')
                for rr in readers.get(k, ()):
                    deps.setdefault(rr, 'war')
            deps.pop(i, None)
            keep = []
            for d, kind in deps.items():
                p = ops[d]
                if p['eng'] == op['eng'] and not p['dma']:
                    if op['eng'] == 'pe':
                        continue
                keep.append(d)
            op['deps'] = keep
            for d in keep:
                ops[d]['sig'] = True
            for k in op['r']:
                lst = readers.setdefault(k, [])
                if not op['dma']:
                    lst[:] = [q for q in lst if ops[q]['dma'] or ops[q]['eng'] != op['eng']]
                lst.append(i)
            for k in op['w']:
                last_w[k] = i
                readers[k] = []
            if op['dma']:
                outstanding.append(i)
            else:
                eng_last[op['eng']] = i
        cnt = {e: 0 for e in ENGS}
        dcnt = [0] * N_DSEM
        ndq = [0, 0]
        for op in ops:
            if op['bar']:
                continue
            if op['dma']:
                half = N_DSEM // 2
                qi = 0 if op['eng'] == 'sp' else 1
                d = qi * half + (ndq[qi] % half)
                ndq[qi] += 1
                op['dsem'] = d
                op['dprev'] = dcnt[d]
                dcnt[d] += 16
                op['dval'] = dcnt[d]
            elif op['sig']:
                cnt[op['eng']] += 1
                op['sval'] = cnt[op['eng']]
        import contextlib
        with contextlib.ExitStack() as st:
            esem = {e: st.enter_context(nc.semaphore("s_" + e)) for e in ENGS}
            dsem = [st.enter_context(nc.semaphore("d_%d" % i)) for i in range(N_DSEM)]
            block = st.enter_context(nc.Block())

            def waits_for(deps):
                need = {}
                for d in deps:
                    p = ops[d]
                    if p['dma']:
                        key, val = ('d', p['dsem']), p['dval']
                    else:
                        key, val = ('e', p['eng']), p['sval']
                    if need.get(key, 0) < val:
                        need[key] = val
                return need

            def emit(eng, e):
                seen = {}

                def do_waits(need):
                    for key, val in need.items():
                        if seen.get(key, 0) >= val:
                            continue
                        seen[key] = val
                        sem = dsem[key[1]] if key[0] == 'd' else esem[key[1]]
                        e.wait_ge(sem, val)

                for op in ops:
                    if op['bar']:
                        do_waits(waits_for(op['deps']))
                        continue
                    if op['eng'] != eng:
                        continue
                    need = waits_for(op['deps'])
                    if op['dma'] and op['dprev'] > 0:
                        key = ('d', op['dsem'])
                        if need.get(key, 0) < op['dprev']:
                            need[key] = op['dprev']
                    do_waits(need)
                    ins = op['fn'](e)
                    if op['dma']:
                        ins.then_inc(dsem[op['dsem']], 16)
                    elif op['sig']:
                        ins.then_inc(esem[eng], 1)
                if eng == 'sp':
                    for d in range(N_DSEM):
                        if dcnt[d] > 0 and seen.get(('d', d), 0) < dcnt[d]:
                            e.wait_ge(dsem[d], dcnt[d])
                    for en in ENGS:
                        if cnt[en] > 0 and seen.get(('e', en), 0) < cnt[en]:
                            e.wait_ge(esem[en], cnt[en])

            @block.tensor
            def _(e):
                emit('pe', e)

            @block.scalar
            def _(e):
                emit('act', e)

            @block.vector
            def _(e):
                emit('dve', e)

            @block.gpsimd
            def _(e):
                emit('pool', e)

            @block.sync
            def _(e):
                emit('sp', e)


class Arena:
    def __init__(self, nc, nbytes):
        self.t = nc.alloc_sbuf_tensor("arena", [128, nbytes], U8)
        self.cap = nbytes
        self.off = 0

    def alloc(self, shape, dtype):
        n = int(np.prod(shape)) * DSZ[dtype]
        n_al = (n + 63) // 64 * 64
        assert self.off + n_al <= self.cap, ("arena overflow", self.off, n_al, self.cap)
        ap = self.t[:, self.off:self.off + n].bitcast(dtype)
        self.off += n_al
        if len(shape) == 2:
            ap = ap.rearrange("p (a b) -> p a b", a=shape[0])
        elif len(shape) == 3:
            ap = ap.rearrange("p (a b c) -> p a b c", a=shape[0], b=shape[1])
        return ap

    def view(self, off, shape, dtype):
        n = int(np.prod(shape)) * DSZ[dtype]
        assert off + n <= self.cap
        ap = self.t[:, off:off + n].bitcast(dtype)
        if len(shape) == 2:
            ap = ap.rearrange("p (a b) -> p a b", a=shape[0])
        return ap

    def mark(self):
        return self.off

    def release(self, m):
        self.off = m


def build_program(stop_after=None, debug=False):
    nc = bass.Bass("TRN2", target_bir_lowering=False)
    sc = Sched(nc)

    def din(name, shape, dt=F32):
        return nc.dram_tensor(name, list(shape), dt, kind="ExternalInput").ap()

    def dscr(name, shape, dt):
        return nc.dram_tensor(name, list(shape), dt, kind="Internal").ap()

    x = din("x", [S, D])
    w0f = din("w0f", [D, 1664])
    w0v = din("w0v", [D, 128])
    a_sink = din("a_sink", [1, 8])
    gq_d = din("gq", [128, 3])
    gkv_d = din("gkv", [128, 2])
    wq_d = din("wq", [384, 768])
    wqs_d = din("wqs", [384, 768])
    wkv_d = din("wkv", [256, 1024])
    wo_d = [din("wo0", [D, D]), din("wo1", [D, D])]
    lnp = {n: din(n, [1, D]) for n in
           ["ln0a_g", "ln0a_b", "ln0b_g", "ln0b_b", "ln1a_g", "ln1a_b", "ln1b_g", "ln1b_b"]}
    router_d = [din("router0", [D, 16]), din("router1", [D, 16])]
    NE_ = 1 if stop_after in ('INIT', 'L0A', 'L0W', 'L0M', 'L0O') else 16
    wg_d = [din("w_gate0", [NE_, D, 2048]), din("w_gate1", [NE_, D, 2048])]
    wu_d = [din("w_up0", [NE_, D, 2048]), din("w_up1", [NE_, D, 2048])]
    wd_d = [din("w_down0", [NE_, 2048, D]), din("w_down1", [NE_, 2048, D])]
    wqkv_d = din("w_qkv1", [D, 3072])
    rpbT_d = din("rpbT", [16, 128, 16, 64])
    identb_d = din("ident_bf", [128, 128], BF16)
    identf_d = din("ident_f", [128, 128])
    cos_d = din("cos_t", [32, S])
    sin_d = din("sin_t", [32, S])
    biasw_d = din("biasw", [128, 8, 384])
    namask_d = din("namask", [128, 16, 64])
    cvals_d = din("cvals", [128, 4])
    out_d = nc.dram_tensor("out", [S, D], F32, kind="ExternalOutput").ap()

    Hb = dscr("Hb", [S, D], BF16)
    Yd = dscr("Yd", [S, D], F32)
    R1 = dscr("R1", [S, D], F32)
    affd = dscr("affd", [S, 16], F32)
    cumd = dscr("cumd", [16, S], F16)
    mixd = dscr("mixd", [8, 128, S], BF16)
    dbg = {}
    if debug:
        for n, shp, dt in [("dbg_h1", [S, D], F32), ("dbg_mix", [8, 128, S], F32), ("dbg_aff", [S, 16], F32),
                           ("dbg_y", [S, D], F32), ("dbg_cum", [16, S], F32)]:
            dbg[n] = nc.dram_tensor(n, shp, dt, kind="ExternalOutput").ap()

    ar = Arena(nc, 200 * 1024)
    psb = [nc.alloc_psum_tensor("psb%d" % i, [128, 512], F32) for i in range(8)]

    def PS(b):
        return psb[b][:, :]

    def PSB(b):
        return psb[b][:, :].bitcast(BF16)

    def pk(b):
        return "ps%d" % b

    identb = ar.alloc([128], BF16)
    identf = ar.alloc([128], F32)
    onesb = ar.alloc([128], BF16)
    esink = ar.alloc([8], F32)
    cvals = ar.alloc([4], F32)
    eps_rms = ar.alloc([1], F32)
    eps_ln = ar.alloc([1], F32)
    afft = ar.alloc([NT, 16], F32)
    sc.dma('sp', identb, identb_d, (), ['identb'])
    sc.dma('sp', identf, identf_d, (), ['identf'])
    sc.dma('sp', cvals, cvals_d, (), ['cvals'])
    sc.dma('sp', esink, a_sink.partition_broadcast(128), (), ['esink'])
    sc.memset('pool', onesb, 1.0, ['onesb'])
    sc.memset('pool', eps_rms, 1e-6, ['eps'])
    sc.memset('pool', eps_ln, 1e-5, ['eps'])
    sc.act(esink, esink, AF.Exp, ['esink'], ['esink'])
    base_mark = ar.mark()
    if stop_after == 'INIT':
        sc.finalize()
        return nc

    def load_bcast(dst, name, key):
        sc.dma('sp', dst, lnp[name].partition_broadcast(128), (), [key])

    def ln_chunk(src, srck, gB, bB, dst, dstk, st, mv, sd, sfx='', pool_affine=True):
        for hh in range(2):
            sc.add('dve', (lambda a, b: (lambda e: e.bn_stats(a, b)))(st[:, hh, :], src[:, hh * 512:(hh + 1) * 512]),
                   [srck], ['lnst' + sfx])
        sc.add('dve', lambda e: e.bn_aggr(mv, st.rearrange("p a b -> p (a b)")), ['lnst' + sfx], ['lnmv' + sfx])
        sc.act(sd, mv[:, 1:2], AF.Ln, ['lnmv' + sfx, 'eps'], ['lnsd' + sfx], bias=eps_ln)
        sc.act(sd, sd, AF.Exp, ['lnsd' + sfx], ['lnsd' + sfx], scale=-0.5)
        sc.stt('dve', dst, src, mv[:, 0:1], gB, ALU.subtract, ALU.mult, [srck, 'lnmv' + sfx, 'lng'], [dstk])
        sc.stt('dve', dst, dst, sd[:, 0:1], bB, ALU.mult, ALU.add, [dstk, 'lnsd' + sfx, 'lnb'], [dstk])

    def post_h_moe(h, hk, tc, layer, bufs, sfx=''):
        ahb, hbf, hT, ex, ssum, rtr, affT = bufs
        rows = slice(tc * 128, (tc + 1) * 128)
        sc.act(ahb, h, AF.Copy, [hk], ['ahb' + sfx], scale=ALPHA)
        sc.dma('sp', Yd[rows, :], ahb, ['ahb' + sfx], ['Yd'])
        sc.copy('act', hbf, h, [hk], ['hbf' + sfx])
        sc.dma('sp', Hb[rows, :], hbf, ['hbf' + sfx], ['Hb'])
        tb = 4 + (tc % 2)
        for k in range(8):
            sc.tr(PSB(tb)[:, k * 128:(k + 1) * 128], hbf[:, k * 128:(k + 1) * 128], identb, ['hbf' + sfx, 'identb'], [pk(tb)])
        sc.copy('act', hT, PSB(tb).rearrange("p (k t) -> p k t", k=8), [pk(tb)], ['hT' + sfx])
        for k in range(8):
            sc.mm(PS(6)[:, 0:16], hT[:, k, :], rtr[:, k, :], k == 0, k == 7, ['hT' + sfx, 'rtr'], [pk(6)])
        sc.act(ex, PS(6)[:, 0:16], AF.Exp, [pk(6)], ['ex' + sfx, 'ssum' + sfx], accum=ssum)
        sc.recip(ssum, ssum, ['ssum' + sfx], ['ssum' + sfx])
        sc.ts('dve', afft[:, tc, :], ex, ssum[:, 0:1], None, ALU.mult, None, ['ex' + sfx, 'ssum' + sfx], ['afft%d' % tc])
        sc.tr(PS(7)[0:16, (tc % 4) * 128:(tc % 4 + 1) * 128], afft[:, tc, :], identf, ['afft%d' % tc, 'identf'], [pk(7)])
        if tc % 4 == 3:
            g4 = tc // 4
            sc.copy('act', affT[0:16, g4 * 512:(g4 + 1) * 512], PS(7)[0:16, :], [pk(7)], ['affT'])

    def moe_route(affT, rbufs):
        junk, ones16, maskT, cum, lo, mid, cntt, step = rbufs
        sc.memset('pool', ones16, 1.0, ['ones16'])
        sc.memset('dve', lo, 0.0, ['lo'])
        for it in range(27):
            wk = 2.0 ** -(it + 1)
            sc.ts('dve', mid, lo, wk, None, ALU.add, None, ['lo'], ['mid'])
            sc.ts('dve', junk, affT, mid[:, 0:1], 0.0, ALU.is_ge, ALU.add, ['affT', 'mid'], ['junk', 'cnt'], accum=cntt)
            sc.ts('dve', step, cntt, 511.5, wk, ALU.is_ge, ALU.mult, ['cnt'], ['step'])
            sc.tt('dve', lo, lo, step, ALU.add, ['lo', 'step'], ['lo'])
        sc.ts('dve', maskT, affT, lo[:, 0:1], None, ALU.is_ge, None, ['affT', 'lo'], ['maskT'])
        sc.add('dve', lambda e: e.tensor_tensor_scan(cum, ones16, maskT, 0.0, ALU.mult, ALU.add),
               ['ones16', 'maskT'], ['cum'])
        sc.dma('sp', cumd, cum, ['cum'], ['cumd'])
        if debug:
            sc.dma('sp', dbg["dbg_cum"], maskT, ['maskT'], ['dbg_cum'])

    def moe_experts(layer, mb):
        (wring, cumB, idxf, idxi, xg, gg, xgT, hidT, ysb, sg, junkc) = mb
        NSL = len(wring)
        pieces = []
        for e in range(16):
            for fq in range(4):
                pieces.append((e, 'gu', fq))
            for dh in range(2):
                pieces.append((e, 'd', dh))
        state = dict(next_load=0)

        def load_piece(n):
            e, kind, q = pieces[n]
            slot = wring[n % NSL]
            key = 'w%d' % (n % NSL)
            if kind == 'gu':
                sc.dma('pool', slot[:, 0:8, :], wg_d[layer][e][:, q * 512:(q + 1) * 512].rearrange("(k p) f -> p k f", p=128),
                       (), [key])
                sc.dma('pool', slot[:, 8:16, :], wu_d[layer][e][:, q * 512:(q + 1) * 512].rearrange("(k p) f -> p k f", p=128),
                       (), [key])
            else:
                sc.dma('pool', slot, wd_d[layer][e][:, q * 512:(q + 1) * 512].rearrange("(f p) d -> p f d", p=128),
                       (), [key])

        def prefetch(upto):
            while state['next_load'] < min(upto, len(pieces)):
                load_piece(state['next_load'])
                state['next_load'] += 1

        def prep_a(e):
            b = e % 2
            b3 = e % 4
            g3 = e % 3
            sc.dma('sp', cumB[b], cumd[e:e + 1, :].partition_broadcast(128), ['cumd'], ['cumB%d' % b])
            for j in range(4):
                sc.ts('dve', junkc, cumB[b], cvals[:, j:j + 1], 0.0, ALU.is_le, ALU.add,
                      ['cumB%d' % b, 'cvals'], ['junkc', 'idxf%d' % b3], accum=idxf[b3][:, j:j + 1])
            sc.ts('dve', idxf[b3], idxf[b3], float(S - 1), None, ALU.min, None, ['idxf%d' % b3], ['idxf%d' % b3])
            sc.copy('dve', idxi[b3], idxf[b3], ['idxf%d' % b3], ['idxi%d' % b3])

        def prep_a2(e):
            b = e % 2
            b3 = e % 4
            g3 = e % 3
            for j in range(4):
                sc.idma(xg[b][:, j, :], None, Hb[:, :], bass.IndirectOffsetOnAxis(ap=idxi[b3][:, j:j + 1], axis=0),
                        ['idxi%d' % b3, 'Hb'], ['xg%d' % b])
                sc.idma(gg[g3][:, j, :], None, affd[:, :], bass.IndirectOffsetOnAxis(ap=idxi[b3][:, j:j + 1], axis=0),
                        ['idxi%d' % b3, 'affd'], ['gg%d' % g3])

        def prep_b(e):
            b = e % 2
            for j in range(4):
                tb = 0 if j % 2 == 0 else 7
                for k in range(8):
                    sc.tr(PSB(tb)[:, k * 128:(k + 1) * 128], xg[b][:, j, k * 128:(k + 1) * 128], identb,
                          ['xg%d' % b, 'identb'], [pk(tb)])
                sc.copy('act' if j % 2 == 0 else 'dve', xgT[b][:, :, j * 128:(j + 1) * 128],
                        PSB(tb).rearrange("p (k t) -> p k t", k=8), [pk(tb)], ['xgT%d' % b])

        prefetch(NSL)
        prep_a(0)
        prep_a2(0)
        prep_a(1)
        prep_a2(1)
        prep_b(0)
        n = 0
        for e in range(16):
            b = e % 2
            b3 = e % 4
            g3 = e % 3
            if e + 2 < 16:
                prep_a(e + 2)
            for fq in range(4):
                if fq == 2 and e + 2 < 16:
                    prep_a2(e + 2)
                if fq == 3 and e + 1 < 16:
                    prep_b(e + 1)
                slot = wring[n % NSL]
                wkey = 'w%d' % (n % NSL)
                for fcl in range(4):
                    fc = fq * 4 + fcl
                    bg, bu = 1 + (fc % 2), 3 + (fc % 2)
                    for k in range(8):
                        sc.mm(PS(bg), slot[:, k, fcl * 128:(fcl + 1) * 128], xgT[b][:, k, :], k == 0, k == 7,
                              [wkey, 'xgT%d' % b], [pk(bg)])
                    for k in range(8):
                        sc.mm(PS(bu), slot[:, 8 + k, fcl * 128:(fcl + 1) * 128], xgT[b][:, k, :], k == 0, k == 7,
                              [wkey, 'xgT%d' % b], [pk(bu)])
                    sc.act(sg[fc % 2], PS(bg), AF.Silu, [pk(bg)], ['sg%d' % (fc % 2)])
                    sc.tt('dve', hidT[:, fc, :], sg[fc % 2], PS(bu), ALU.mult, ['sg%d' % (fc % 2), pk(bu)], ['hid%d' % fc])
                n += 1
                prefetch(n + NSL)
            for dh in range(2):
                slot = wring[n % NSL]
                wkey = 'w%d' % (n % NSL)
                for j in range(4):
                    bd = 5 + (j % 2)
                    for fc in range(16):
                        sc.mm(PS(bd), hidT[:, fc, j * 128:(j + 1) * 128], slot[:, fc, :], fc == 0, fc == 15,
                              ['hid%d' % fc, wkey], [pk(bd)])
                    sc.act(ysb[:, j, dh * 512:(dh + 1) * 512], PS(bd), AF.Copy, [pk(bd), 'gg%d' % g3], ['ysb%d' % j],
                           scale=gg[g3][:, j, e:e + 1])
                n += 1
                prefetch(n + NSL)
            for j in range(4):
                sc.idma(Yd[:, :], bass.IndirectOffsetOnAxis(ap=idxi[b3][:, j:j + 1], axis=0), ysb[:, j, :], None,
                        ['ysb%d' % j, 'idxi%d' % b3] + ['Yd_%d_%d' % ((e - 1) % 2, jj) for jj in range(4)],
                        ['Yd_%d_%d' % (e % 2, j)], compute_op=ALU.add)

    def moe_phase(layer, affT):
        m0 = ar.mark()
        wring = [ar.alloc([16, 512], BF16) for _ in range(6)]
        m1 = ar.mark()
        junk = ar.alloc([S], BF16)
        ones16 = ar.alloc([S], F32)
        maskT = ar.alloc([S], F32)
        cum = ar.alloc([S], F16)
        smalls = [ar.alloc([1], F32) for _ in range(4)]
        rb = (junk[0:16], ones16[0:16], maskT[0:16], cum[0:16]) + tuple(t[0:16] for t in smalls)
        assert ar.off <= 184 * 1024
        moe_route(affT[0:16], rb)
        sc.barrier()
        ar.release(m1)
        cumB = [ar.alloc([S], F16) for _ in range(2)]
        idxf = [ar.alloc([4], F32) for _ in range(4)]
        idxi = [ar.alloc([4], I32) for _ in range(4)]
        xg = [ar.alloc([4, D], BF16) for _ in range(2)]
        gg = [ar.alloc([4, 16], F32) for _ in range(3)]
        xgT = [ar.alloc([8, 512], BF16) for _ in range(2)]
        hidT = ar.alloc([16, 512], BF16)
        ysb = ar.alloc([4, D], F32)
        sg = [ar.alloc([512], F32) for _ in range(2)]
        junkc = ar.alloc([S], BF16)
        moe_experts(layer, (wring, cumB, idxf, idxi, xg, gg, xgT, hidT, ysb, sg, junkc))
        sc.barrier()
        ar.release(m0)

    def attn_norm_store(po_bank, nq, esk, mixrow, rd, mixt):
        if esk is not None:
            sc.ts('dve', rd[64:128, 0:nq], PS(po_bank)[64:128, 0:nq], esk, None, ALU.add, None,
                  [pk(po_bank), 'esink'], ['rdh'])
        else:
            sc.copy('dve', rd[64:128, 0:nq], PS(po_bank)[64:128, 0:nq], [pk(po_bank)], ['rdh'])
        sc.ts('dve', rd[0:64, 0:nq], rd[64:128, 0:nq], 1.0, None, ALU.mult, None, ['rdh'], ['rd'])
        sc.recip(rd[0:64, 0:nq], rd[0:64, 0:nq], ['rd'], ['rd'])
        sc.tt('dve', mixt[0:64, 0:nq], PS(po_bank)[0:64, 0:nq], rd[0:64, 0:nq], ALU.mult, [pk(po_bank), 'rd'], ['mixt'])
        sc.dma('sp', mixrow, mixt[0:64, 0:nq], ['mixt'], ['mixd'])

    def outproj_ln_phase(layer, wo_ap, gname, bname, resid_load, resid_fn):
        m0 = ar.mark()
        wo = ar.alloc([8, D], BF16)
        gB = ar.alloc([D], F32)
        bB = ar.alloc([D], F32)
        rtr = ar.alloc([8, 16], BF16)
        affT = ar.view(184 * 1024, [S], F32)
        mt = [ar.alloc([8, 128], BF16) for _ in range(2)]
        xc = [ar.alloc([D], F32) for _ in range(2)]
        rbuf = [ar.alloc([D], F32) for _ in range(2)]
        hbuf = [ar.alloc([D], F32) for _ in range(2)]
        ahb = [ar.alloc([D], F32) for _ in range(2)]
        hbf = [ar.alloc([D], BF16) for _ in range(2)]
        hT = [ar.alloc([8, 128], BF16) for _ in range(2)]
        ex = [ar.alloc([16], F32) for _ in range(2)]
        ssum = [ar.alloc([1], F32) for _ in range(2)]
        st = [ar.alloc([2, 6], F32) for _ in range(2)]
        mv = [ar.alloc([2], F32) for _ in range(2)]
        sd = [ar.alloc([1], F32) for _ in range(2)]
        sc.dma('pool', wo, wo_ap.rearrange("(k p) d -> p k d", p=128), (), ['wo'])
        sc.dma('pool', rtr, router_d[layer].rearrange("(k p) e -> p k e", p=128), (), ['rtr'])
        load_bcast(gB, gname, 'lng')
        load_bcast(bB, bname, 'lnb')
        def loads(tc):
            b = tc % 2
            sc.dma('sp', mt[b], mixd[:, :, tc * 128:(tc + 1) * 128].rearrange("f p t -> p f t"), ['mixd'], ['mt%d' % b])
            resid_load(tc, xc[b], 'xc%d' % b)

        def banks_of(tc):
            return (0, 1) if tc % 2 == 0 else (2, 3)

        def mms(tc):
            b = tc % 2
            for half, bank in zip((0, 1), banks_of(tc)):
                for f in range(8):
                    sc.mm(PS(bank), mt[b][:, f, :], wo[:, f, half * 512:(half + 1) * 512], f == 0, f == 7,
                          ['mt%d' % b, 'wo'], [pk(bank)])

        loads(0)
        loads(1)
        mms(0)
        for tc in range(NT):
            b = tc % 2
            pa, pb_ = banks_of(tc)
            if tc + 1 < NT:
                mms(tc + 1)
            sf = str(b)
            resid_fn(tc, pa, pb_, rbuf[b], 'rbuf' + sf, xc[b], 'xc%d' % b)
            if tc + 2 < NT:
                loads(tc + 2)
            ln_chunk(rbuf[b], 'rbuf' + sf, gB, bB, hbuf[b], 'hbuf' + sf, st[b], mv[b], sd[b], sf)
            post_h_moe(hbuf[b], 'hbuf' + sf, tc, layer, (ahb[b], hbf[b], hT[b], ex[b], ssum[b], rtr, affT), sf)
            if debug and layer == 0:
                sc.dma('sp', dbg["dbg_h1"][tc * 128:(tc + 1) * 128, :], hbuf[b], ['hbuf' + sf], ['dbg_h1'])
        sc.dma('sp', affd.rearrange("(c p) e -> p c e", p=128), afft, ['afft%d' % t for t in range(NT)], ['affd'])
        if debug and layer == 0:
            sc.dma('sp', dbg["dbg_aff"].rearrange("(c p) e -> p c e", p=128), afft, ['afft%d' % t for t in range(NT)], ['dbg_aff'])
        sc.barrier()
        ar.release(m0)
        return affT, m0

    qaT = ar.alloc([4, S], BF16)
    kaT2 = ar.alloc([2, S], BF16)
    va = ar.alloc([NT, 2, 128], BF16)
    cqn = ar.alloc([3, S], BF16)
    ckvn = ar.alloc([2, S], BF16)
    KT = [ar.alloc([S], BF16) for _ in range(2)]
    mA = ar.mark()
    w0 = ar.alloc([8, 1792], BF16)
    xb = ar.alloc([4, D], BF16)
    xT = ar.alloc([8, 512], BF16)
    cqg = ar.alloc([3, 512], F32)
    ckg = ar.alloc([2, 512], F32)
    sq = ar.alloc([5, 512], BF16)
    rq = ar.alloc([512], F32)
    rk = ar.alloc([512], F32)
    cs = ar.alloc([512], F32)
    sn = ar.alloc([512], F32)
    t1 = ar.alloc([512], F32)
    t2 = ar.alloc([512], F32)
    gq = ar.alloc([3], F32)
    gkv = ar.alloc([2], F32)
    sc.dma('pool', w0[:, :, 0:1664], w0f.rearrange("(k p) f -> p k f", p=128), (), ['w0'])
    sc.dma('pool', w0[:, :, 1664:1792], w0v.rearrange("(k p) f -> p k f", p=128), (), ['w0'])
    sc.dma('sp', gq, gq_d, (), ['gq'])
    sc.dma('sp', gkv, gkv_d, (), ['gkv'])
    sc.memset('pool', va[:, :, :, 64:128], 1.0, ['va_ones'])
    for i in range(8):
        tl = slice(i * 512, (i + 1) * 512)
        sc.dma('pool', xb, x[tl, :].rearrange("(c p) d -> p c d", p=128), (), ['xb'])
        sc.dma('sp', cs[64:96, :], cos_d[:, tl], (), ['cs'])
        sc.dma('sp', sn[64:96, :], sin_d[:, tl], (), ['sn'])
        for c in range(4):
            tb = c % 2
            for k in range(8):
                sc.tr(PSB(tb)[:, k * 128:(k + 1) * 128], xb[:, c, k * 128:(k + 1) * 128], identb, ['xb', 'identb'], [pk(tb)])
            sc.copy('act' if c % 2 == 0 else 'dve', xT[:, :, c * 128:(c + 1) * 128],
                    PSB(tb).rearrange("p (k t) -> p k t", k=8), [pk(tb)], ['xT'])
        for f in range(13):
            bank = 2 + (f % 3)
            for k in range(8):
                sc.mm(PS(bank), w0[:, k, f * 128:(f + 1) * 128], xT[:, k, :], k == 0, k == 7, ['w0', 'xT'], [pk(bank)])
            if f < 4:
                sc.copy('act', qaT[:, f, tl], PS(bank), [pk(bank)], ['qaT'])
            elif f < 6:
                sc.copy('dve', kaT2[:, f - 4, tl], PS(bank), [pk(bank)], ['kaT2'])
            elif f < 9:
                r = f - 6
                sc.act(sq[:, r, :], PS(bank), AF.Square, [pk(bank)], ['sq%d' % r, 'psord'])
                sc.ts('dve', cqg[:, r, :], PS(bank), gq[:, r:r + 1], None, ALU.mult, None, [pk(bank), 'gq', 'psord'], ['cqg%d' % r])
            elif f < 11:
                r = f - 9
                sc.act(sq[:, 3 + r, :], PS(bank), AF.Square, [pk(bank)], ['sq%d' % (3 + r), 'psord'])
                sc.ts('dve', ckg[:, r, :], PS(bank), gkv[:, r:r + 1], None, ALU.mult, None, [pk(bank), 'gkv', 'psord'], ['ckg%d' % r])
            elif f == 11:
                sc.tt('dve', t1[64:96, :], PS(bank)[64:96, :], cs[64:96, :], ALU.mult, [pk(bank), 'cs'], ['t1'])
            else:
                sc.tt('dve', t2[64:96, :], PS(bank)[64:96, :], sn[64:96, :], ALU.mult, [pk(bank), 'sn'], ['t2'])
                sc.tt('pool', KT[0][64:96, tl], t1[64:96, :], t2[64:96, :], ALU.add, ['t1', 't2'], ['KT0pe'])
                sc.copy('pool', KT[1][64:96, tl], KT[0][64:96, tl], ['KT0pe'], ['KT1pe'])
        for r in range(3):
            sc.mm(PS(5), onesb, sq[:, r, :], r == 0, r == 2, ['onesb', 'sq%d' % r], [pk(5)])
        for r in range(2):
            sc.mm(PS(6), onesb, sq[:, 3 + r, :], r == 0, r == 1, ['onesb', 'sq%d' % (3 + r)], [pk(6)])
        sc.act(rq, PS(5), AF.Sqrt, [pk(5), 'eps'], ['rq'], bias=eps_rms, scale=1.0 / 384.0)
        sc.recip(rq, rq, ['rq'], ['rq'])
        sc.act(rk, PS(6), AF.Sqrt, [pk(6), 'eps'], ['rk'], bias=eps_rms, scale=1.0 / 256.0)
        sc.recip(rk, rk, ['rk'], ['rk'])
        for r in range(3):
            sc.tt('dve', cqn[:, r, tl], cqg[:, r, :], rq, ALU.mult, ['cqg%d' % r, 'rq'], ['cqn'])
        for r in range(2):
            sc.tt('dve', ckvn[:, r, tl], ckg[:, r, :], rk, ALU.mult, ['ckg%d' % r, 'rk'], ['ckvn'])
        for c in range(4):
            for k in range(8):
                sc.mm(PS(7)[:, c * 128:(c + 1) * 128], xT[:, k, c * 128:(c + 1) * 128], w0[:, k, 1664:1792],
                      k == 0, k == 7, ['xT', 'w0'], [pk(7)])
        sc.copy('act', va[:, i * 4:(i + 1) * 4, :, 0:64], PS(7).rearrange("p (c g d) -> p c g d", c=4, g=2),
                [pk(7)], ['va'])
    sc.barrier()
    ar.release(mA)
    if stop_after == 'L0A':
        sc.finalize()
        return nc

    mW = ar.mark()
    biasw = ar.alloc([8, 384], F32)
    pT = [ar.alloc([384], BF16) for _ in range(4)]
    tbuf = [ar.alloc([384], F32) for _ in range(2)]
    rd = ar.alloc([512], F32)
    mixt = ar.alloc([512], BF16)
    sc.dma('sp', biasw, biasw_d, (), ['biasw'])
    scale_a = 64.0 ** -0.5
    for h in range(8):
        g = h // 4
        f = h // 2
        rbs = (h % 2) * 64
        pr = slice(rbs, rbs + 64)

        def q0_of(j):
            return max(0, (j - 1) * 128)

        def s_step(j):
            q0 = q0_of(j)
            q1 = min(S, (j + 2) * 128)
            n = q1 - q0
            off = q0 - (j - 1) * 128
            bank = j % 3
            sc.mm(PS(bank)[:, 0:n], kaT2[pr, g, j * 128:(j + 1) * 128], qaT[pr, f, q0:q1], True, True,
                  ['kaT2', 'qaT'], [pk(bank)])
            tbk = 'tb%d' % (j % 2)
            sc.stt('dve', tbuf[j % 2][:, 0:n], PS(bank)[:, 0:n], scale_a, biasw[:, h, off:off + n], ALU.mult, ALU.add,
                   [pk(bank), 'biasw'], [tbk])
            sc.act(pT[j % 4][:, 0:n], tbuf[j % 2][:, 0:n], AF.Exp, [tbk], ['pT%d' % (j % 4)])

        def pv_step(i):
            pob = 3 + ((i // 4) % 2)
            js = [jj for jj in (i - 1, i, i + 1) if 0 <= jj < NT]
            for n_, jj in enumerate(js):
                c0 = i * 128 - q0_of(jj)
                sc.mm(PS(pob)[:, (i % 4) * 128:(i % 4 + 1) * 128], va[:, jj, g, :], pT[jj % 4][:, c0:c0 + 128],
                      n_ == 0, n_ == len(js) - 1, ['va', 'va_ones', 'pT%d' % (jj % 4)], [pk(pob)])
            if i % 4 == 3:
                t0 = (i // 4) * 512
                attn_norm_store(pob, 512, esink[64:128, h:h + 1], mixd[f, pr, t0:t0 + 512], rd, mixt)

        s_step(0)
        for j in range(1, NT):
            s_step(j)
            pv_step(j - 1)
        pv_step(NT - 1)
    sc.barrier()
    ar.release(mW)
    if stop_after == 'L0W':
        sc.finalize()
        return nc

    mM = ar.mark()
    wq = ar.alloc([3, 768], BF16)
    wqs = ar.alloc([3, 768], BF16)
    wkv = ar.alloc([2, 1024], BF16)
    QT = [ar.alloc([S], BF16) for _ in range(2)]
    vm = [ar.alloc([NT, 128], BF16) for _ in range(2)]
    csq = [ar.alloc([512], F32) for _ in range(2)]
    snq = [ar.alloc([512], F32) for _ in range(2)]
    u1 = ar.alloc([512], F32)
    u2 = ar.alloc([512], F32)
    pbuf = [ar.alloc([512], BF16) for _ in range(4)]
    rd = ar.alloc([512], F32)
    mixt = ar.alloc([512], BF16)
    sc.dma('pool', wq, wq_d.rearrange("(k p) f -> p k f", p=128), (), ['wq'])
    sc.dma('pool', wqs, wqs_d.rearrange("(k p) f -> p k f", p=128), (), ['wqs'])
    sc.dma('pool', wkv, wkv_d.rearrange("(k p) f -> p k f", p=128), (), ['wkv'])
    for b in range(2):
        sc.memset('pool', vm[b][:, :, 64:128], 1.0, ['vm_ones%d' % b])
    scale_b = 96.0 ** -0.5

    def mla_proj(h):
        b = h % 2
        for i in range(8):
            tl = slice(i * 512, (i + 1) * 512)
            cb = i % 2
            sc.dma('sp', csq[cb][64:96, :], cos_d[:, tl], (), ['csq%d' % cb])
            sc.dma('sp', snq[cb][64:96, :], sin_d[:, tl], (), ['snq%d' % cb])
            for r in range(3):
                sc.mm(PS(5)[0:96, :], wq[:, r, h * 96:(h + 1) * 96], cqn[:, r, tl], r == 0, r == 2, ['wq', 'cqn'], [pk(5)])
            for r in range(3):
                sc.mm(PS(6)[0:96, :], wqs[:, r, h * 96:(h + 1) * 96], cqn[:, r, tl], r == 0, r == 2, ['wqs', 'cqn'], [pk(6)])
            sc.copy('act', QT[b][0:64, tl], PS(5)[0:64, :], [pk(5)], ['QTn%d' % b, 'psord'])
            sc.tt('dve', u1[64:96, :], PS(5)[64:96, :], csq[cb][64:96, :], ALU.mult, [pk(5), 'csq%d' % cb, 'psord'], ['u1'])
            sc.tt('dve', u2[64:96, :], PS(6)[64:96, :], snq[cb][64:96, :], ALU.mult, [pk(6), 'snq%d' % cb], ['u2'])
            sc.tt('pool', QT[b][64:96, tl], u1[64:96, :], u2[64:96, :], ALU.add, ['u1', 'u2'], ['QTp%d' % b])
            for c in range(2):
                sc.mm(PS(7)[0:64, :], wkv[:, c, h * 128:h * 128 + 64], ckvn[:, c, tl], c == 0, c == 1, ['wkv', 'ckvn'], [pk(7)])
            sc.copy('act', KT[b][0:64, tl], PS(7)[0:64, :], [pk(7)], ['KTn%d' % b])
        for g4 in range(4):
            for t8 in range(8):
                tc = g4 * 8 + t8
                for c in range(2):
                    sc.mm(PS(7)[:, t8 * 64:(t8 + 1) * 64], ckvn[:, c, tc * 128:(tc + 1) * 128],
                          wkv[:, c, h * 128 + 64:h * 128 + 128], c == 0, c == 1, ['ckvn', 'wkv'], [pk(7)])
            sc.copy('dve', vm[b][:, g4 * 8:(g4 + 1) * 8, 0:64], PS(7).rearrange("p (t d) -> p t d", t=8), [pk(7)], ['vm%d' % b])

    def mla_attn(h):
        b = h % 2
        f = 4 + h // 2
        rbs = (h % 2) * 64
        kkeys = ['KTn%d' % b, 'KT%dpe' % b]
        qkeys = ['QTn%d' % b, 'QTp%d' % b]
        cnt = 0
        for i in range(8):
            tl = slice(i * 512, (i + 1) * 512)
            pob = 3 + (i % 2)

            def s_step(kc, slot):
                bank = slot % 3
                sc.mm(PS(bank), KT[b][0:96, kc * 128:(kc + 1) * 128], QT[b][0:96, tl], True, True, kkeys + qkeys, [pk(bank)])
                sc.act(pbuf[slot % 4], PS(bank), AF.Exp, [pk(bank)], ['pb%d' % (slot % 4)], scale=scale_b)

            def pv_step(kc, slot):
                sc.mm(PS(pob), vm[b][:, kc, :], pbuf[slot % 4], kc == 0, kc == NT - 1,
                      ['vm%d' % b, 'vm_ones%d' % b, 'pb%d' % (slot % 4)], [pk(pob)])

            s_step(0, cnt)
            s_step(1, cnt + 1)
            for kc in range(NT):
                if kc + 2 < NT:
                    s_step(kc + 2, cnt + kc + 2)
                pv_step(kc, cnt + kc)
            cnt += NT
            attn_norm_store(pob, 512, None, mixd[f, rbs:rbs + 64, tl], rd, mixt)

    mla_proj(0)
    for h in range(8):
        if h + 1 < 8:
            mla_proj(h + 1)
        mla_attn(h)
    sc.barrier()
    ar.release(base_mark)

    if debug:
        mD = ar.mark()
        mtb = ar.alloc([S], BF16)
        mtf = ar.alloc([S], F32)
        for f in range(8):
            sc.dma('sp', mtb, mixd[f], ['mixd'], ['mtb'])
            sc.copy('dve', mtf, mtb, ['mtb'], ['mtf'])
            sc.dma('sp', dbg["dbg_mix"][f], mtf, ['mtf'], ['dbg_mix'])
        sc.barrier()
        ar.release(mD)

    def resid0_load(tc, xcb, xck):
        sc.dma('sp', xcb, x[tc * 128:(tc + 1) * 128, :], (), [xck])

    def resid0(tc, pa, pb_, rbuf, rbk, xcb, xck):
        for half, bank in ((0, pa), (1, pb_)):
            sc.stt('dve', rbuf[:, half * 512:(half + 1) * 512], xcb[:, half * 512:(half + 1) * 512], ALPHA, PS(bank),
                   ALU.mult, ALU.add, [xck, pk(bank)], [rbk])

    affT, _ = outproj_ln_phase(0, wo_d[0], "ln0a_g", "ln0a_b", resid0_load, resid0)
    if stop_after == 'L0O':
        sc.finalize()
        return nc

    moe_phase(0, affT)
    ar.release(base_mark)
    if debug:
        mD = ar.mark()
        yb = ar.alloc([D], F32)
        for tc in range(NT):
            sc.dma('sp', yb, Yd[tc * 128:(tc + 1) * 128, :], ['Yd'], ['yb'])
            sc.dma('sp', dbg["dbg_y"][tc * 128:(tc + 1) * 128, :], yb, ['yb'], ['dbg_y'])
        sc.barrier()
        ar.release(mD)
    if stop_after == 'MOE0':
        sc.finalize()
        return nc

    h2T = ar.alloc([8, S], BF16)
    mL1 = ar.mark()
    gB = ar.alloc([D], F32)
    bB = ar.alloc([D], F32)
    yc = [ar.alloc([D], F32) for _ in range(2)]
    hbuf = [ar.alloc([D], F32) for _ in range(2)]
    ahb = [ar.alloc([D], F32) for _ in range(2)]
    hbf = [ar.alloc([D], BF16) for _ in range(2)]
    st = [ar.alloc([2, 6], F32) for _ in range(2)]
    mv = [ar.alloc([2], F32) for _ in range(2)]
    sd = [ar.alloc([1], F32) for _ in range(2)]
    load_bcast(gB, "ln0b_g", 'lng')
    load_bcast(bB, "ln0b_b", 'lnb')
    sc.dma('sp', yc[0], Yd[0:128, :], ['Yd'], ['yc0'])
    for tc in range(NT):
        b = tc % 2
        sf = str(b)
        rows = slice(tc * 128, (tc + 1) * 128)
        if tc + 1 < NT:
            sc.dma('sp', yc[1 - b], Yd[(tc + 1) * 128:(tc + 2) * 128, :], ['Yd'], ['yc%d' % (1 - b)])
        ln_chunk(yc[b], 'yc%d' % b, gB, bB, hbuf[b], 'hbuf' + sf, st[b], mv[b], sd[b], sf)
        sc.act(ahb[b], hbuf[b], AF.Copy, ['hbuf' + sf], ['ahb' + sf], scale=ALPHA)
        sc.dma('sp', R1[rows, :], ahb[b], ['ahb' + sf], ['R1'])
        sc.copy('act', hbf[b], hbuf[b], ['hbuf' + sf], ['hbf' + sf])
        tb = tc % 2
        for k in range(8):
            sc.tr(PSB(tb)[:, k * 128:(k + 1) * 128], hbf[b][:, k * 128:(k + 1) * 128], identb, ['hbf' + sf, 'identb'], [pk(tb)])
        sc.copy('dve', h2T[:, :, rows], PSB(tb).rearrange("p (k t) -> p k t", k=8), [pk(tb)], ['h2T'])
    sc.barrier()
    ar.release(mL1)

    wp = [ar.alloc([8, 384], BF16) for _ in range(2)]
    QTp = [ar.alloc([S], BF16) for _ in range(2)]
    KTp = [ar.alloc([S], BF16) for _ in range(2)]
    Vp = [ar.alloc([NT, 2, 128], BF16) for _ in range(2)]
    TTp = [ar.alloc([2, 16, 64], F32) for _ in range(2)]
    TTb = [ar.alloc([2, 16, 64], BF16) for _ in range(2)]
    namask = ar.alloc([16, 64], F32)
    tbn = [ar.alloc([5, 64], F32) for _ in range(4)]
    pTn = [ar.alloc([5, 64], BF16) for _ in range(5)]
    rd = ar.alloc([512], F32)
    mixt = ar.alloc([512], BF16)
    sc.dma('sp', namask, namask_d, (), ['namask'])
    for b in range(2):
        sc.memset('pool', Vp[b][:, :, :, 64:128], 1.0, ['vp_ones%d' % b])
    scale_c = 64.0 ** -0.5

    def na_proj(hp):
        b = hp % 2
        for part in range(3):
            sc.dma('pool', wp[b][:, :, part * 128:(part + 1) * 128],
                   wqkv_d[:, part * 1024 + hp * 128: part * 1024 + (hp + 1) * 128].rearrange("(k p) f -> p k f", p=128),
                   (), ['wp%d' % b])
        for g in range(2):
            sc.dma('sp', TTp[b][:, g], rpbT_d[hp * 2 + g], (), ['TT%d_%d' % (b, g)])
            sc.tt('pool', TTp[b][:, g], TTp[b][:, g], namask, ALU.add, ['TT%d_%d' % (b, g), 'namask'], ['TT%d_%d' % (b, g)])
            sc.ts('pool', TTb[b][:, g], TTp[b][:, g], 1.0 / scale_c, None, ALU.mult, None, ['TT%d_%d' % (b, g)], ['TTb%d_%d' % (b, g)])
        for i in range(8):
            tl = slice(i * 512, (i + 1) * 512)
            for k in range(8):
                sc.mm(PS(5), wp[b][:, k, 0:128], h2T[:, k, tl], k == 0, k == 7, ['wp%d' % b, 'h2T'], [pk(5)])
            sc.copy('act', QTp[b][:, tl], PS(5), [pk(5)], ['QTp%d' % b])
            for k in range(8):
                sc.mm(PS(6), wp[b][:, k, 128:256], h2T[:, k, tl], k == 0, k == 7, ['wp%d' % b, 'h2T'], [pk(6)])
            sc.copy('dve', KTp[b][:, tl], PS(6), [pk(6)], ['KTp%d' % b])
            for c in range(4):
                tc = i * 4 + c
                for k in range(8):
                    sc.mm(PS(5)[:, c * 128:(c + 1) * 128], h2T[:, k, tc * 128:(tc + 1) * 128], wp[b][:, k, 256:384],
                          k == 0, k == 7, ['h2T', 'wp%d' % b], [pk(5)])
            sc.copy('act', Vp[b][:, i * 4:(i + 1) * 4, :, 0:64], PS(5).rearrange("p (c g d) -> p c g d", c=4, g=2),
                    [pk(5)], ['Vp%d' % b])

    def na_attn(hp):
        b = hp % 2
        tasks = [(g, r) for g in range(2) for r in range(64)]

        def geom(r):
            rs = min(max(r - 4, 0), 56)
            odd = rs % 2
            kr0 = rs - odd
            nch = 5 if odd else 4
            return odd, kr0, nch, kr0 - r + 8

        def s_part(t):
            g, r = tasks[t]
            pr = slice(g * 64, g * 64 + 64)
            odd, kr0, nch, u0 = geom(r)
            bank = (0, 1, 2, 7)[t % 4]
            sc.mm(PS(bank)[:, 0:nch * 64].rearrange("p (c q) -> p c q", c=nch), identb,
                  TTb[b][:, g, u0:u0 + 2 * nch - 1:2, :], True, False, ['identb', 'TTb%d_%d' % (b, g)], [pk(bank)])
            for c in range(nch):
                kc = kr0 // 2 + c
                sc.mm(PS(bank)[:, c * 64:(c + 1) * 64], KTp[b][pr, kc * 128:(kc + 1) * 128], QTp[b][pr, r * 64:(r + 1) * 64],
                      False, c == nch - 1, ['KTp%d' % b, 'QTp%d' % b], [pk(bank)])
            sc.act(pTn[t % 5][:, 0:nch, :], PS(bank)[:, 0:nch * 64].rearrange("p (c q) -> p c q", c=nch), AF.Exp,
                   [pk(bank)], ['pTn%d' % (t % 5)], scale=scale_c)

        def pv_part(t):
            g, r = tasks[t]
            pr = slice(g * 64, g * 64 + 64)
            odd, kr0, nch, u0 = geom(r)
            pob = 3 + ((r // 8) % 2)
            pkk = 'pTn%d' % (t % 5)
            for c in range(nch):
                kc = kr0 // 2 + c
                if odd and c == 0:
                    ps_ = slice(64, 128)
                elif odd and c == nch - 1:
                    ps_ = slice(0, 64)
                else:
                    ps_ = slice(0, 128)
                sc.mm(PS(pob)[:, (r % 8) * 64:(r % 8 + 1) * 64], Vp[b][ps_, kc, g, :], pTn[t % 5][ps_, c, :],
                      c == 0, c == nch - 1, ['Vp%d' % b, 'vp_ones%d' % b, pkk], [pk(pob)])
            if r % 8 == 7:
                t0 = (r // 8) * 512
                attn_norm_store(pob, 512, None, mixd[hp, pr, t0:t0 + 512], rd, mixt)

        LA = 3
        for t in range(min(LA, len(tasks))):
            s_part(t)
        for t in range(len(tasks)):
            if t + LA < len(tasks):
                s_part(t + LA)
            pv_part(t)

    na_proj(0)
    for hp in range(8):
        if hp + 1 < 8:
            na_proj(hp + 1)
        na_attn(hp)
    sc.barrier()
    ar.release(base_mark)

    def resid1_load(tc, xcb, xck):
        sc.dma('sp', xcb, R1[tc * 128:(tc + 1) * 128, :], ['R1'], [xck])

    def resid1(tc, pa, pb_, rbuf, rbk, xcb, xck):
        for half, bank in ((0, pa), (1, pb_)):
            sc.tt('dve', rbuf[:, half * 512:(half + 1) * 512], xcb[:, half * 512:(half + 1) * 512], PS(bank),
                  ALU.add, [xck, pk(bank)], [rbk])

    affT, _ = outproj_ln_phase(1, wo_d[1], "ln1a_g", "ln1a_b", resid1_load, resid1)
    moe_phase(1, affT)
    ar.release(base_mark)

    gB = ar.alloc([D], F32)
    bB = ar.alloc([D], F32)
    yc = [ar.alloc([D], F32) for _ in range(2)]
    hb2 = [ar.alloc([D], F32) for _ in range(2)]
    st = [ar.alloc([2, 6], F32) for _ in range(2)]
    mv = [ar.alloc([2], F32) for _ in range(2)]
    sd = [ar.alloc([1], F32) for _ in range(2)]
    load_bcast(gB, "ln1b_g", 'lng')
    load_bcast(bB, "ln1b_b", 'lnb')
    sc.dma('sp', yc[0], Yd[0:128, :], ['Yd'], ['yc0'])
    for tc in range(NT):
        b = tc % 2
        rows = slice(tc * 128, (tc + 1) * 128)
        if tc + 1 < NT:
            sc.dma('sp', yc[1 - b], Yd[(tc + 1) * 128:(tc + 2) * 128, :], ['Yd'], ['yc%d' % (1 - b)])
        ln_chunk(yc[b], 'yc%d' % b, gB, bB, hb2[b], 'hb%d' % b, st[b], mv[b], sd[b], str(b))
        sc.dma('sp', out_d[rows, :], hb2[b], ['hb%d' % b], ['out'])
    sc.finalize()
    return nc


def _consts():
    c = {}
    c["ident_bf"] = np.eye(128, dtype=np.float32).astype(ml_dtypes.bfloat16)
    c["ident_f"] = np.eye(128, dtype=np.float32)
    half = 16
    inv = (10000.0 ** (-np.arange(half, dtype=np.float32) / half)).astype(np.float32)
    ang = np.arange(S, dtype=np.float32)[None, :] * inv[:, None]
    cos = np.cos(ang).astype(np.float32)
    sin = np.sin(ang).astype(np.float32)
    c["cos_t"] = np.concatenate([cos, cos], 0)
    c["sin_t"] = np.concatenate([-sin, sin], 0)
    k = np.arange(128)[:, None]
    qp = np.arange(384)[None, :]
    dist = np.abs(qp - 128 - k).astype(np.float32)
    slopes = 2.0 ** (-8.0 * (np.arange(8, dtype=np.float32) + 1.0) / 8)
    bw = np.where(dist[:, None, :] <= 128, -slopes[None, :, None] * dist[:, None, :], NEG).astype(np.float32)
    c["biasw"] = np.ascontiguousarray(bw)
    cols = np.arange(64)
    cstart = np.clip(cols - 8, 0, 48)
    valid = (cols[None, :] >= cstart[:, None]) & (cols[None, :] < cstart[:, None] + 16)
    m = np.where(valid.T, 0.0, NEG).astype(np.float32)
    m2 = np.concatenate([m, m], 0)
    c["namask"] = np.ascontiguousarray(np.broadcast_to(m2[:, None, :], (128, 16, 64)))
    c["cvals"] = (np.arange(4)[None, :] * 128 + np.arange(128)[:, None]).astype(np.float32)
    return c


def _prep_shared(inp):
    f32 = np.float32
    d = {}
    w_in0 = np.asarray(inp["w_in0"], f32)
    qa = w_in0[:, 0:512]
    ka0, ka1 = w_in0[:, 512:576], w_in0[:, 576:640]
    vaw = w_in0[:, 640:768]
    cq = w_in0[:, 768:1152]
    ckv = w_in0[:, 1152:1408]
    kr = w_in0[:, 1408:1440]
    krs = np.concatenate([kr[:, 16:32], kr[:, 0:16]], 1)
    d["w0f"] = np.ascontiguousarray(np.concatenate(
        [qa, ka0, ka0, ka1, ka1, cq, ckv, kr, kr, kr, kr, krs, krs, krs, krs], 1))
    d["w0v"] = np.ascontiguousarray(vaw)
    d["a_sink"] = np.asarray(inp["a_sink"], f32).reshape(1, 8)
    d["gq"] = np.ascontiguousarray(np.asarray(inp["mla_q_norm"], f32).reshape(3, 128).T)
    d["gkv"] = np.ascontiguousarray(np.asarray(inp["mla_kv_norm"], f32).reshape(2, 128).T)
    wq = np.asarray(inp["w_q_up"], f32)
    d["wq"] = wq
    wq3 = wq.reshape(384, 8, 96)
    d["wqs"] = np.ascontiguousarray(
        np.concatenate([wq3[:, :, 0:64], wq3[:, :, 80:96], wq3[:, :, 64:80]], 2).reshape(384, 768))
    d["wkv"] = np.asarray(inp["w_kv_up"], f32)
    d["wo0"] = np.asarray(inp["w_out0"], f32)
    d["wo1"] = np.asarray(inp["w_out1"], f32)
    for n in ["ln0a_g", "ln0a_b", "ln0b_g", "ln0b_b", "ln1a_g", "ln1a_b", "ln1b_g", "ln1b_b"]:
        d[n] = np.asarray(inp[n], f32).reshape(1, D)
    for n in ["router0", "router1", "w_gate0", "w_gate1", "w_up0", "w_up1", "w_down0", "w_down1", "w_qkv1"]:
        d[n] = np.asarray(inp[n], f32)
    rpb = np.asarray(inp["na_rpb"], f32)
    cols = np.arange(64)
    dc = np.clip(cols[None, :] - cols[:, None] + 15, 0, 30)
    u = np.arange(16)
    out = np.empty((16, 2, 64, 16, 64), f32)
    for kr2 in range(2):
        dr = np.clip(u + kr2 - 8, -7, 7) + 7
        g_ = rpb[:, dr[:, None, None], dc[None, :, :]]
        out[:, kr2] = np.transpose(g_, (0, 3, 1, 2))
    d["rpbT"] = np.ascontiguousarray(out.reshape(16, 128, 16, 64))
    d.update(_consts())
    return d


_CACHE = {}


def kernel(**inputs):
    x = np.asarray(inputs["x"], np.float32)
    shared = _prep_shared(inputs)
    if "nc" not in _CACHE:
        _CACHE["nc"] = build_program()
    nc = _CACHE["nc"]
    in_maps = []
    for c in range(N_CORES):
        m = dict(shared)
        m["x"] = np.ascontiguousarray(x[c])
        in_maps.append(m)
    res = run_bass_kernel_spmd(nc, in_maps, core_ids=list(range(N_CORES)))
    return np.stack([np.asarray(r["out"], np.float32) for r in res.results], 0)
```

```python
import math
import numpy as np
import ml_dtypes
import concourse.bass as bass
import concourse.mybir as mybir
from concourse.bass_utils import run_bass_kernel_spmd

F32 = mybir.dt.float32
BF16 = mybir.dt.bfloat16
F16 = mybir.dt.float16
I32 = mybir.dt.int32
U8 = mybir.dt.uint8
ALU = mybir.AluOpType
AF = mybir.ActivationFunctionType
DSZ = {F32: 4, BF16: 2, F16: 2, I32: 4, U8: 1}

S = 4096
D = 1024
NT = 32
NEG = -1.0e30
ALPHA = 4.0 ** 0.25
N_CORES = 8
ENGS = ['pe', 'act', 'dve', 'pool', 'sp']
N_DSEM = 48


class Sched:
    def __init__(self, nc):
        self.nc = nc
        self.ops = []

    def add(self, eng, fn, r=(), w=(), dma=False):
        self.ops.append(dict(eng=eng, fn=fn, r=list(r), w=list(w), dma=dma, sig=dma, bar=False))

    def barrier(self):
        self.ops.append(dict(eng=None, bar=True))

    def mm(self, out, lhsT, rhs, start, stop, r, w):
        self.add('pe', lambda e: e.matmul(out, lhsT, rhs, start=start, stop=stop), r, w)

    def tr(self, out, in_, ident, r, w):
        self.add('pe', lambda e: e.transpose(out, in_, ident), r, w)

    def act(self, out, in_, func, r, w, bias=None, scale=None, accum=None):
        kw = {}
        if bias is not None:
            kw['bias'] = bias
        if scale is not None:
            kw['scale'] = scale
        if accum is not None:
            kw['accum_out'] = accum
        self.add('act', lambda e: e.activation(out, in_, func, **kw), r, w)

    def ts(self, eng, out, in0, s1, s2, op0, op1, r, w, accum=None):
        if accum is not None:
            self.add(eng, lambda e: e.tensor_scalar(out, in0, s1, s2, op0, op1, accum_out=accum), r, w)
        elif op1 is None:
            self.add(eng, lambda e: e.tensor_scalar(out, in0, s1, None, op0), r, w)
        else:
            self.add(eng, lambda e: e.tensor_scalar(out, in0, s1, s2, op0, op1), r, w)

    def tt(self, eng, out, in0, in1, op, r, w):
        self.add(eng, lambda e: e.tensor_tensor(out, in0, in1, op), r, w)

    def stt(self, eng, out, in0, scalar, in1, op0, op1, r, w):
        self.add(eng, lambda e: e.scalar_tensor_tensor(out, in0, scalar, in1, op0, op1), r, w)

    def copy(self, eng, out, in_, r, w):
        if eng == 'act':
            self.add('act', lambda e: e.copy(out, in_), r, w)
        else:
            self.add(eng, lambda e: e.tensor_copy(out, in_), r, w)

    def recip(self, out, in_, r, w):
        self.add('dve', lambda e: e.reciprocal(out, in_), r, w)

    def memset(self, eng, ap, val, w):
        self.add(eng, lambda e: e.memset(ap, val), (), w)

    def dma(self, eng, out, in_, r, w, **kw):
        self.add(eng, lambda e: e.dma_start(out=out, in_=in_, **kw), r, w, dma=True)

    def idma(self, out, out_off, in_, in_off, r, w, **kw):
        self.add('pool', lambda e: e.indirect_dma_start(out=out, out_offset=out_off, in_=in_,
                                                        in_offset=in_off, **kw), r, w, dma=True)

    def finalize(self):
        nc = self.nc
        ops = self.ops
        last_w, readers = {}, {}
        eng_last = {e: None for e in ENGS}
        outstanding = []
        for i, op in enumerate(ops):
            if op['bar']:
                op['deps'] = [v for v in eng_last.values() if v is not None] + outstanding
                outstanding = []
                last_w.clear()
                readers.clear()
                for d in op['deps']:
                    ops[d]['sig'] = True
                continue
            deps = {}
            for k in op['r']:
                if k in last_w:
                    deps[last_w[k]] = 'raw'
            for k in op['w']:
                if k in last_w:
                    deps.setdefault(last_w[k], 'waw')
                for rr in readers.get(k, ()):
                    deps.setdefault(rr, 'war')
            deps.pop(i, None)
            keep = []
            for d, kind in deps.items():
                p = ops[d]
                if p['eng'] == op['eng'] and not p['dma']:
                    if op['eng'] == 'pe':
                        continue
                keep.append(d)
            op['deps'] = keep
            for d in keep:
                ops[d]['sig'] = True
            for k in op['r']:
                lst = readers.setdefault(k, [])
                if not op['dma']:
                    lst[:] = [q for q in lst if ops[q]['dma'] or ops[q]['eng'] != op['eng']]
                lst.append(i)
            for k in op['w']:
                last_w[k] = i
                readers[k] = []
            if op['dma']:
                outstanding.append(i)
            else:
                eng_last[op['eng']] = i
        cnt = {e: 0 for e in ENGS}
        dcnt = [0] * N_DSEM
        ndq = [0, 0]
        for op in ops:
            if op['bar']:
                continue
            if op['dma']:
                half = N_DSEM // 2
                qi = 0 if op['eng'] == 'sp' else 1
                d = qi * half + (ndq[qi] % half)
                ndq[qi] += 1
                op['dsem'] = d
                op['dprev'] = dcnt[d]
                dcnt[d] += 16
                op['dval'] = dcnt[d]
            elif op['sig']:
                cnt[op['eng']] += 1
                op['sval'] = cnt[op['eng']]
        import contextlib
        with contextlib.ExitStack() as st:
            esem = {e: st.enter_context(nc.semaphore("s_" + e)) for e in ENGS}
            dsem = [st.enter_context(nc.semaphore("d_%d" % i)) for i in range(N_DSEM)]
            block = st.enter_context(nc.Block())

            def waits_for(deps):
                need = {}
                for d in deps:
                    p = ops[d]
                    if p['dma']:
                        key, val = ('d', p['dsem']), p['dval']
                    else:
                        key, val = ('e', p['eng']), p['sval']
                    if need.get(key, 0) < val:
                        need[key] = val
                return need

            def emit(eng, e):
                seen = {}

                def do_waits(need):
                    for key, val in need.items():
                        if seen.get(key, 0) >= val:
                            continue
                        seen[key] = val
                        sem = dsem[key[1]] if key[0] == 'd' else esem[key[1]]
                        e.wait_ge(sem, val)

                for op in ops:
                    if op['bar']:
                        do_waits(waits_for(op['deps']))
                        continue
                    if op['eng'] != eng:
                        continue
                    need = waits_for(op['deps'])
                    if op['dma'] and op['dprev'] > 0:
                        key = ('d', op['dsem'])
                        if need.get(key, 0) < op['dprev']:
                            need[key] = op['dprev']
                    do_waits(need)
                    ins = op['fn'](e)
                    if op['dma']:
                        ins.then_inc(dsem[op['dsem']], 16)
                    elif op['sig']:
                        ins.then_inc(esem[eng], 1)
                if eng == 'sp':
                    for d in range(N_DSEM):
                        if dcnt[d] > 0 and seen.get(('d', d), 0) < dcnt[d]:
                            e.wait_ge(dsem[d], dcnt[d])
                    for en in ENGS:
                        if cnt[en] > 0 and seen.get(('e', en), 0) < cnt[en]:
                            e.wait_ge(esem[en], cnt[en])

            @block.tensor
            def _(e):
                emit('pe', e)

            @block.scalar
            def _(e):
                emit('act', e)

            @block.vector
            def _(e):
                emit('dve', e)

            @block.gpsimd
            def _(e):
                emit('pool', e)

            @block.sync
            def _(e):
                emit('sp', e)


class Arena:
    def __init__(self, nc, nbytes):
        self.t = nc.alloc_sbuf_tensor("arena", [128, nbytes], U8)
        self.cap = nbytes
        self.off = 0

    def alloc(self, shape, dtype):
        n = int(np.prod(shape)) * DSZ[dtype]
        n_al = (n + 63) // 64 * 64
        assert self.off + n_al <= self.cap, ("arena overflow", self.off, n_al, self.cap)
        ap = self.t[:, self.off:self.off + n].bitcast(dtype)
        self.off += n_al
        if len(shape) == 2:
            ap = ap.rearrange("p (a b) -> p a b", a=shape[0])
        elif len(shape) == 3:
            ap = ap.rearrange("p (a b c) -> p a b c", a=shape[0], b=shape[1])
        return ap

    def view(self, off, shape, dtype):
        n = int(np.prod(shape)) * DSZ[dtype]
        assert off + n <= self.cap
        ap = self.t[:, off:off + n].bitcast(dtype)
        if len(shape) == 2:
            ap = ap.rearrange("p (a b) -> p a b", a=shape[0])
        return ap

    def mark(self):
        return self.off

    def release(self, m):
        self.off = m


def build_program(stop_after=None, debug=False):
    nc = bass.Bass("TRN2", target_bir_lowering=False)
    sc = Sched(nc)

    def din(name, shape, dt=F32):
        return nc.dram_tensor(name, list(shape), dt, kind="ExternalInput").ap()

    def dscr(name, shape, dt):
        return nc.dram_tensor(name, list(shape), dt, kind="Internal").ap()

    x = din("x", [S, D])
    w0f = din("w0f", [D, 1664])
    w0v = din("w0v", [D, 128])
    a_sink = din("a_sink", [1, 8])
    gq_d = din("gq", [128, 3])
    gkv_d = din("gkv", [128, 2])
    wq_d = din("wq", [384, 768])
    wqs_d = din("wqs", [384, 768])
    wkv_d = din("wkv", [256, 1024])
    wo_d = [din("wo0", [D, D]), din("wo1", [D, D])]
    lnp = {n: din(n, [1, D]) for n in
           ["ln0a_g", "ln0a_b", "ln0b_g", "ln0b_b", "ln1a_g", "ln1a_b", "ln1b_g", "ln1b_b"]}
    router_d = [din("router0", [D, 16]), din("router1", [D, 16])]
    NE_ = 1 if stop_after in ('INIT', 'L0A', 'L0W', 'L0M', 'L0O') else 16
    wg_d = [din("w_gate0", [NE_, D, 2048]), din("w_gate1", [NE_, D, 2048])]
    wu_d = [din("w_up0", [NE_, D, 2048]), din("w_up1", [NE_, D, 2048])]
    wd_d = [din("w_down0", [NE_, 2048, D]), din("w_down1", [NE_, 2048, D])]
    wqkv_d = din("w_qkv1", [D, 3072])
    rpbT_d = din("rpbT", [16, 128, 16, 64])
    identb_d = din("ident_bf", [128, 128], BF16)
    identf_d = din("ident_f", [128, 128])
    cos_d = din("cos_t", [32, S])
    sin_d = din("sin_t", [32, S])
    biasw_hi_d = din("biasw_hi", [128, 8, 384], BF16)
    biasw_lo_d = din("biasw_lo", [128, 8, 384], BF16)
    namask_d = din("namask", [128, 16, 64])
    cvals_d = din("cvals", [128, 4])
    out_d = nc.dram_tensor("out", [S, D], F32, kind="ExternalOutput").ap()

    Hb = dscr("Hb", [S, D], BF16)
    Yd = dscr("Yd", [S, D], F32)
    R1 = dscr("R1", [S, D], F32)
    affd = dscr("affd", [S, 16], F32)
    cumd = dscr("cumd", [16, S], F16)
    mixd = dscr("mixd", [8, 128, S], BF16)
    dbg = {}
    if debug:
        for n, shp, dt in [("dbg_h1", [S, D], F32), ("dbg_mix", [8, 128, S], F32), ("dbg_aff", [S, 16], F32),
                           ("dbg_y", [S, D], F32), ("dbg_cum", [16, S], F32)]:
            dbg[n] = nc.dram_tensor(n, shp, dt, kind="ExternalOutput").ap()

    ar = Arena(nc, 200 * 1024)
    psb = [nc.alloc_psum_tensor("psb%d" % i, [128, 512], F32) for i in range(8)]

    def PS(b):
        return psb[b][:, :]

    def PSB(b):
        return psb[b][:, :].bitcast(BF16)

    def pk(b):
        return "ps%d" % b

    identb = ar.alloc([128], BF16)
    identf = ar.alloc([128], F32)
    onesb = ar.alloc([128], BF16)
    esink = ar.alloc([8], F32)
    cvals = ar.alloc([4], F32)
    eps_rms = ar.alloc([1], F32)
    eps_ln = ar.alloc([1], F32)
    afft = ar.alloc([NT, 16], F32)
    sc.dma('sp', identb, identb_d, (), ['identb'])
    sc.dma('sp', identf, identf_d, (), ['identf'])
    sc.dma('sp', cvals, cvals_d, (), ['cvals'])
    sc.dma('sp', esink, a_sink.partition_broadcast(128), (), ['esink'])
    sc.memset('pool', onesb, 1.0, ['onesb'])
    sc.memset('pool', eps_rms, 1e-6, ['eps'])
    sc.memset('pool', eps_ln, 1e-5, ['eps'])
    sc.act(esink, esink, AF.Exp, ['esink'], ['esink'])
    base_mark = ar.mark()
    if stop_after == 'INIT':
        sc.finalize()
        return nc

    def load_bcast(dst, name, key):
        sc.dma('sp', dst, lnp[name].partition_broadcast(128), (), [key])

    def ln_chunk(src, srck, gB, bB, dst, dstk, st, mv, sd, sfx='', pool_affine=True):
        for hh in range(2):
            sc.add('dve', (lambda a, b: (lambda e: e.bn_stats(a, b)))(st[:, hh, :], src[:, hh * 512:(hh + 1) * 512]),
                   [srck], ['lnst' + sfx])
        sc.add('dve', lambda e: e.bn_aggr(mv, st.rearrange("p a b -> p (a b)")), ['lnst' + sfx], ['lnmv' + sfx])
        sc.act(sd, mv[:, 1:2], AF.Ln, ['lnmv' + sfx, 'eps'], ['lnsd' + sfx], bias=eps_ln)
        sc.act(sd, sd, AF.Exp, ['lnsd' + sfx], ['lnsd' + sfx], scale=-0.5)
        sc.stt('dve', dst, src, mv[:, 0:1], gB, ALU.subtract, ALU.mult, [srck, 'lnmv' + sfx, 'lng'], [dstk])
        sc.stt('dve', dst, dst, sd[:, 0:1], bB, ALU.mult, ALU.add, [dstk, 'lnsd' + sfx, 'lnb'], [dstk])

    def post_h_moe(h, hk, tc, layer, bufs, sfx=''):
        ahb, hbf, hT, ex, ssum, rtr, affT = bufs
        rows = slice(tc * 128, (tc + 1) * 128)
        sc.act(ahb, h, AF.Copy, [hk], ['ahb' + sfx], scale=ALPHA)
        sc.dma('sp', Yd[rows, :], ahb, ['ahb' + sfx], ['Yd'])
        sc.copy('act', hbf, h, [hk], ['hbf' + sfx])
        sc.dma('sp', Hb[rows, :], hbf, ['hbf' + sfx], ['Hb'])
        tb = 4 + (tc % 2)
        for k in range(8):
            sc.tr(PSB(tb)[:, k * 128:(k + 1) * 128], hbf[:, k * 128:(k + 1) * 128], identb, ['hbf' + sfx, 'identb'], [pk(tb)])
        sc.copy('act', hT, PSB(tb).rearrange("p (k t) -> p k t", k=8), [pk(tb)], ['hT' + sfx])
        for k in range(8):
            sc.mm(PS(6)[:, 0:16], hT[:, k, :], rtr[:, k, :], k == 0, k == 7, ['hT' + sfx, 'rtr'], [pk(6)])
        sc.act(ex, PS(6)[:, 0:16], AF.Exp, [pk(6)], ['ex' + sfx, 'ssum' + sfx], accum=ssum)
        sc.recip(ssum, ssum, ['ssum' + sfx], ['ssum' + sfx])
        sc.ts('dve', afft[:, tc, :], ex, ssum[:, 0:1], None, ALU.mult, None, ['ex' + sfx, 'ssum' + sfx], ['afft%d' % tc])
        sc.tr(PS(7)[0:16, (tc % 4) * 128:(tc % 4 + 1) * 128], afft[:, tc, :], identf, ['afft%d' % tc, 'identf'], [pk(7)])
        if tc % 4 == 3:
            g4 = tc // 4
            sc.copy('act', affT[0:16, g4 * 512:(g4 + 1) * 512], PS(7)[0:16, :], [pk(7)], ['affT'])

    def moe_route(affT, rbufs):
        junk, ones16, maskT, cum, lo, mid, cntt, step = rbufs
        sc.memset('pool', ones16, 1.0, ['ones16'])
        sc.memset('dve', lo, 0.0, ['lo'])
        for it in range(27):
            wk = 2.0 ** -(it + 1)
            sc.ts('dve', mid, lo, wk, None, ALU.add, None, ['lo'], ['mid'])
            sc.ts('dve', junk, affT, mid[:, 0:1], 0.0, ALU.is_ge, ALU.add, ['affT', 'mid'], ['junk', 'cnt'], accum=cntt)
            sc.ts('dve', step, cntt, 511.5, wk, ALU.is_ge, ALU.mult, ['cnt'], ['step'])
            sc.tt('dve', lo, lo, step, ALU.add, ['lo', 'step'], ['lo'])
        sc.ts('dve', maskT, affT, lo[:, 0:1], None, ALU.is_ge, None, ['affT', 'lo'], ['maskT'])
        sc.add('dve', lambda e: e.tensor_tensor_scan(cum, ones16, maskT, 0.0, ALU.mult, ALU.add),
               ['ones16', 'maskT'], ['cum'])
        sc.dma('sp', cumd, cum, ['cum'], ['cumd'])
        if debug:
            sc.dma('sp', dbg["dbg_cum"], maskT, ['maskT'], ['dbg_cum'])

    def moe_experts(layer, mb):
        (wring, cumB, idxf, idxi, xg, gg, xgT, hidT, ysb, sg, junkc) = mb
        NSL = len(wring)
        pieces = []
        for e in range(16):
            for fq in range(4):
                pieces.append((e, 'gu', fq))
            for dh in range(2):
                pieces.append((e, 'd', dh))
        state = dict(next_load=0)

        def load_piece(n):
            e, kind, q = pieces[n]
            slot = wring[n % NSL]
            key = 'w%d' % (n % NSL)
            if kind == 'gu':
                sc.dma('pool', slot[:, 0:8, :], wg_d[layer][e][:, q * 512:(q + 1) * 512].rearrange("(k p) f -> p k f", p=128),
                       (), [key])
                sc.dma('pool', slot[:, 8:16, :], wu_d[layer][e][:, q * 512:(q + 1) * 512].rearrange("(k p) f -> p k f", p=128),
                       (), [key])
            else:
                sc.dma('pool', slot, wd_d[layer][e][:, q * 512:(q + 1) * 512].rearrange("(f p) d -> p f d", p=128),
                       (), [key])

        def prefetch(upto):
            while state['next_load'] < min(upto, len(pieces)):
                load_piece(state['next_load'])
                state['next_load'] += 1

        def prep_a(e):
            b = e % 2
            b3 = e % 4
            g3 = e % 3
            sc.dma('sp', cumB[b], cumd[e:e + 1, :].partition_broadcast(128), ['cumd'], ['cumB%d' % b])
            for j in range(4):
                sc.ts('dve', junkc, cumB[b], cvals[:, j:j + 1], 0.0, ALU.is_le, ALU.add,
                      ['cumB%d' % b, 'cvals'], ['junkc', 'idxf%d' % b3], accum=idxf[b3][:, j:j + 1])
            sc.ts('dve', idxf[b3], idxf[b3], float(S - 1), None, ALU.min, None, ['idxf%d' % b3], ['idxf%d' % b3])
            sc.copy('dve', idxi[b3], idxf[b3], ['idxf%d' % b3], ['idxi%d' % b3])

        def prep_a2(e):
            b = e % 2
            b3 = e % 4
            g3 = e % 3
            for j in range(4):
                sc.idma(xg[b][:, j, :], None, Hb[:, :], bass.IndirectOffsetOnAxis(ap=idxi[b3][:, j:j + 1], axis=0),
                        ['idxi%d' % b3, 'Hb'], ['xg%d' % b])
                sc.idma(gg[g3][:, j, :], None, affd[:, :], bass.IndirectOffsetOnAxis(ap=idxi[b3][:, j:j + 1], axis=0),
                        ['idxi%d' % b3, 'affd'], ['gg%d' % g3])

        def prep_b(e):
            b = e % 2
            for j in range(4):
                tb = 0 if j % 2 == 0 else 7
                for k in range(8):
                    sc.tr(PSB(tb)[:, k * 128:(k + 1) * 128], xg[b][:, j, k * 128:(k + 1) * 128], identb,
                          ['xg%d' % b, 'identb'], [pk(tb)])
                sc.copy('act' if j % 2 == 0 else 'dve', xgT[b][:, :, j * 128:(j + 1) * 128],
                        PSB(tb).rearrange("p (k t) -> p k t", k=8), [pk(tb)], ['xgT%d' % b])

        prefetch(NSL)
        prep_a(0)
        prep_a2(0)
        prep_a(1)
        prep_a2(1)
        prep_b(0)
        n = 0
        for e in range(16):
            b = e % 2
            b3 = e % 4
            g3 = e % 3
            if e + 2 < 16:
                prep_a(e + 2)
            for fq in range(4):
                if fq == 2 and e + 2 < 16:
                    prep_a2(e + 2)
                if fq == 3 and e + 1 < 16:
                    prep_b(e + 1)
                slot = wring[n % NSL]
                wkey = 'w%d' % (n % NSL)
                for fcl in range(4):
                    fc = fq * 4 + fcl
                    bg, bu = 1 + (fc % 2), 3 + (fc % 2)
                    for k in range(8):
                        sc.mm(PS(bg), slot[:, k, fcl * 128:(fcl + 1) * 128], xgT[b][:, k, :], k == 0, k == 7,
                              [wkey, 'xgT%d' % b], [pk(bg)])
                    for k in range(8):
                        sc.mm(PS(bu), slot[:, 8 + k, fcl * 128:(fcl + 1) * 128], xgT[b][:, k, :], k == 0, k == 7,
                              [wkey, 'xgT%d' % b], [pk(bu)])
                    sc.act(sg[fc % 2], PS(bg), AF.Silu, [pk(bg)], ['sg%d' % (fc % 2)])
                    sc.tt('dve', hidT[:, fc, :], sg[fc % 2], PS(bu), ALU.mult, ['sg%d' % (fc % 2), pk(bu)], ['hid%d' % fc])
                n += 1
                prefetch(n + NSL)
            for dh in range(2):
                slot = wring[n % NSL]
                wkey = 'w%d' % (n % NSL)
                for j in range(4):
                    bd = 5 + (j % 2)
                    for fc in range(16):
                        sc.mm(PS(bd), hidT[:, fc, j * 128:(j + 1) * 128], slot[:, fc, :], fc == 0, fc == 15,
                              ['hid%d' % fc, wkey], [pk(bd)])
                    sc.act(ysb[:, j, dh * 512:(dh + 1) * 512], PS(bd), AF.Copy, [pk(bd), 'gg%d' % g3], ['ysb%d' % j],
                           scale=gg[g3][:, j, e:e + 1])
                n += 1
                prefetch(n + NSL)
            for j in range(4):
                sc.idma(Yd[:, :], bass.IndirectOffsetOnAxis(ap=idxi[b3][:, j:j + 1], axis=0), ysb[:, j, :], None,
                        ['ysb%d' % j, 'idxi%d' % b3] + ['Yd_%d_%d' % ((e - 1) % 2, jj) for jj in range(4)],
                        ['Yd_%d_%d' % (e % 2, j)], compute_op=ALU.add)

    def moe_phase(layer, affT):
        m0 = ar.mark()
        wring = [ar.alloc([16, 512], BF16) for _ in range(6)]
        m1 = ar.mark()
        junk = ar.alloc([S], BF16)
        ones16 = ar.alloc([S], F32)
        maskT = ar.alloc([S], F32)
        cum = ar.alloc([S], F16)
        smalls = [ar.alloc([1], F32) for _ in range(4)]
        rb = (junk[0:16], ones16[0:16], maskT[0:16], cum[0:16]) + tuple(t[0:16] for t in smalls)
        assert ar.off <= 184 * 1024
        moe_route(affT[0:16], rb)
        sc.barrier()
        ar.release(m1)
        cumB = [ar.alloc([S], F16) for _ in range(2)]
        idxf = [ar.alloc([4], F32) for _ in range(4)]
        idxi = [ar.alloc([4], I32) for _ in range(4)]
        xg = [ar.alloc([4, D], BF16) for _ in range(2)]
        gg = [ar.alloc([4, 16], F32) for _ in range(3)]
        xgT = [ar.alloc([8, 512], BF16) for _ in range(2)]
        hidT = ar.alloc([16, 512], BF16)
        ysb = ar.alloc([4, D], F32)
        sg = [ar.alloc([512], F32) for _ in range(2)]
        junkc = ar.alloc([S], BF16)
        moe_experts(layer, (wring, cumB, idxf, idxi, xg, gg, xgT, hidT, ysb, sg, junkc))
        sc.barrier()
        ar.release(m0)

    def attn_norm_store(po_bank, nq, esk, mixrow, rd, mixt):
        if esk is not None:
            sc.ts('dve', rd[64:128, 0:nq], PS(po_bank)[64:128, 0:nq], esk, None, ALU.add, None,
                  [pk(po_bank), 'esink'], ['rdh'])
        else:
            sc.copy('dve', rd[64:128, 0:nq], PS(po_bank)[64:128, 0:nq], [pk(po_bank)], ['rdh'])
        sc.ts('dve', rd[0:64, 0:nq], rd[64:128, 0:nq], 1.0, None, ALU.mult, None, ['rdh'], ['rd'])
        sc.recip(rd[0:64, 0:nq], rd[0:64, 0:nq], ['rd'], ['rd'])
        sc.tt('dve', mixt[0:64, 0:nq], PS(po_bank)[0:64, 0:nq], rd[0:64, 0:nq], ALU.mult, [pk(po_bank), 'rd'], ['mixt'])
        sc.dma('sp', mixrow, mixt[0:64, 0:nq], ['mixt'], ['mixd'])

    def outproj_ln_phase(layer, wo_ap, gname, bname, resid_load, resid_fn):
        m0 = ar.mark()
        wo = ar.alloc([8, D], BF16)
        gB = ar.alloc([D], F32)
        bB = ar.alloc([D], F32)
        rtr = ar.alloc([8, 16], BF16)
        affT = ar.view(184 * 1024, [S], F32)
        mt = [ar.alloc([8, 128], BF16) for _ in range(2)]
        xc = [ar.alloc([D], F32) for _ in range(2)]
        rbuf = [ar.alloc([D], F32) for _ in range(2)]
        hbuf = [ar.alloc([D], F32) for _ in range(2)]
        ahb = [ar.alloc([D], F32) for _ in range(2)]
        hbf = [ar.alloc([D], BF16) for _ in range(2)]
        hT = [ar.alloc([8, 128], BF16) for _ in range(2)]
        ex = [ar.alloc([16], F32) for _ in range(2)]
        ssum = [ar.alloc([1], F32) for _ in range(2)]
        st = [ar.alloc([2, 6], F32) for _ in range(2)]
        mv = [ar.alloc([2], F32) for _ in range(2)]
        sd = [ar.alloc([1], F32) for _ in range(2)]
        sc.dma('pool', wo, wo_ap.rearrange("(k p) d -> p k d", p=128), (), ['wo'])
        sc.dma('pool', rtr, router_d[layer].rearrange("(k p) e -> p k e", p=128), (), ['rtr'])
        load_bcast(gB, gname, 'lng')
        load_bcast(bB, bname, 'lnb')
        def loads(tc):
            b = tc % 2
            sc.dma('sp', mt[b], mixd[:, :, tc * 128:(tc + 1) * 128].rearrange("f p t -> p f t"), ['mixd'], ['mt%d' % b])
            resid_load(tc, xc[b], 'xc%d' % b)

        def banks_of(tc):
            return (0, 1) if tc % 2 == 0 else (2, 3)

        def mms(tc):
            b = tc % 2
            for half, bank in zip((0, 1), banks_of(tc)):
                for f in range(8):
                    sc.mm(PS(bank), mt[b][:, f, :], wo[:, f, half * 512:(half + 1) * 512], f == 0, f == 7,
                          ['mt%d' % b, 'wo'], [pk(bank)])

        loads(0)
        loads(1)
        mms(0)
        for tc in range(NT):
            b = tc % 2
            pa, pb_ = banks_of(tc)
            if tc + 1 < NT:
                mms(tc + 1)
            sf = str(b)
            resid_fn(tc, pa, pb_, rbuf[b], 'rbuf' + sf, xc[b], 'xc%d' % b)
            if tc + 2 < NT:
                loads(tc + 2)
            ln_chunk(rbuf[b], 'rbuf' + sf, gB, bB, hbuf[b], 'hbuf' + sf, st[b], mv[b], sd[b], sf)
            post_h_moe(hbuf[b], 'hbuf' + sf, tc, layer, (ahb[b], hbf[b], hT[b], ex[b], ssum[b], rtr, affT), sf)
            if debug and layer == 0:
                sc.dma('sp', dbg["dbg_h1"][tc * 128:(tc + 1) * 128, :], hbuf[b], ['hbuf' + sf], ['dbg_h1'])
        sc.dma('sp', affd.rearrange("(c p) e -> p c e", p=128), afft, ['afft%d' % t for t in range(NT)], ['affd'])
        if debug and layer == 0:
            sc.dma('sp', dbg["dbg_aff"].rearrange("(c p) e -> p c e", p=128), afft, ['afft%d' % t for t in range(NT)], ['dbg_aff'])
        sc.barrier()
        ar.release(m0)
        return affT, m0

    qaT = ar.alloc([4, S], BF16)
    kaT2 = ar.alloc([2, S], BF16)
    va = ar.alloc([NT, 2, 128], BF16)
    cqn = ar.alloc([3, S], BF16)
    ckvn = ar.alloc([2, S], BF16)
    KT = [ar.alloc([S], BF16) for _ in range(2)]
    mA = ar.mark()
    w0 = ar.alloc([8, 1792], BF16)
    xb = ar.alloc([4, D], BF16)
    xT = ar.alloc([8, 512], BF16)
    cqg = ar.alloc([3, 512], F32)
    ckg = ar.alloc([2, 512], F32)
    sq = ar.alloc([5, 512], BF16)
    rq = ar.alloc([512], F32)
    rk = ar.alloc([512], F32)
    cs = ar.alloc([512], F32)
    sn = ar.alloc([512], F32)
    t1 = ar.alloc([512], F32)
    t2 = ar.alloc([512], F32)
    gq = ar.alloc([3], F32)
    gkv = ar.alloc([2], F32)
    sc.dma('pool', w0[:, :, 0:1664], w0f.rearrange("(k p) f -> p k f", p=128), (), ['w0'])
    sc.dma('pool', w0[:, :, 1664:1792], w0v.rearrange("(k p) f -> p k f", p=128), (), ['w0'])
    sc.dma('sp', gq, gq_d, (), ['gq'])
    sc.dma('sp', gkv, gkv_d, (), ['gkv'])
    sc.memset('pool', va[:, :, :, 64:128], 1.0, ['va_ones'])
    for i in range(8):
        tl = slice(i * 512, (i + 1) * 512)
        sc.dma('pool', xb, x[tl, :].rearrange("(c p) d -> p c d", p=128), (), ['xb'])
        sc.dma('sp', cs[64:96, :], cos_d[:, tl], (), ['cs'])
        sc.dma('sp', sn[64:96, :], sin_d[:, tl], (), ['sn'])
        for c in range(4):
            tb = c % 2
            for k in range(8):
                sc.tr(PSB(tb)[:, k * 128:(k + 1) * 128], xb[:, c, k * 128:(k + 1) * 128], identb, ['xb', 'identb'], [pk(tb)])
            sc.copy('act' if c % 2 == 0 else 'dve', xT[:, :, c * 128:(c + 1) * 128],
                    PSB(tb).rearrange("p (k t) -> p k t", k=8), [pk(tb)], ['xT'])
        for f in range(13):
            bank = 2 + (f % 3)
            for k in range(8):
                sc.mm(PS(bank), w0[:, k, f * 128:(f + 1) * 128], xT[:, k, :], k == 0, k == 7, ['w0', 'xT'], [pk(bank)])
            if f < 4:
                sc.copy('act', qaT[:, f, tl], PS(bank), [pk(bank)], ['qaT'])
            elif f < 6:
                sc.copy('dve', kaT2[:, f - 4, tl], PS(bank), [pk(bank)], ['kaT2'])
            elif f < 9:
                r = f - 6
                sc.act(sq[:, r, :], PS(bank), AF.Square, [pk(bank)], ['sq%d' % r, 'psord'])
                sc.ts('dve', cqg[:, r, :], PS(bank), gq[:, r:r + 1], None, ALU.mult, None, [pk(bank), 'gq', 'psord'], ['cqg%d' % r])
            elif f < 11:
                r = f - 9
                sc.act(sq[:, 3 + r, :], PS(bank), AF.Square, [pk(bank)], ['sq%d' % (3 + r), 'psord'])
                sc.ts('dve', ckg[:, r, :], PS(bank), gkv[:, r:r + 1], None, ALU.mult, None, [pk(bank), 'gkv', 'psord'], ['ckg%d' % r])
            elif f == 11:
                sc.tt('dve', t1[64:96, :], PS(bank)[64:96, :], cs[64:96, :], ALU.mult, [pk(bank), 'cs'], ['t1'])
            else:
                sc.tt('dve', t2[64:96, :], PS(bank)[64:96, :], sn[64:96, :], ALU.mult, [pk(bank), 'sn'], ['t2'])
                sc.tt('pool', KT[0][64:96, tl], t1[64:96, :], t2[64:96, :], ALU.add, ['t1', 't2'], ['KT0pe'])
                sc.copy('pool', KT[1][64:96, tl], KT[0][64:96, tl], ['KT0pe'], ['KT1pe'])
        for r in range(3):
            sc.mm(PS(5), onesb, sq[:, r, :], r == 0, r == 2, ['onesb', 'sq%d' % r], [pk(5)])
        for r in range(2):
            sc.mm(PS(6), onesb, sq[:, 3 + r, :], r == 0, r == 1, ['onesb', 'sq%d' % (3 + r)], [pk(6)])
        sc.act(rq, PS(5), AF.Sqrt, [pk(5), 'eps'], ['rq'], bias=eps_rms, scale=1.0 / 384.0)
        sc.recip(rq, rq, ['rq'], ['rq'])
        sc.act(rk, PS(6), AF.Sqrt, [pk(6), 'eps'], ['rk'], bias=eps_rms, scale=1.0 / 256.0)
        sc.recip(rk, rk, ['rk'], ['rk'])
        for r in range(3):
            sc.tt('dve', cqn[:, r, tl], cqg[:, r, :], rq, ALU.mult, ['cqg%d' % r, 'rq'], ['cqn'])
        for r in range(2):
            sc.tt('dve', ckvn[:, r, tl], ckg[:, r, :], rk, ALU.mult, ['ckg%d' % r, 'rk'], ['ckvn'])
        for c in range(4):
            for k in range(8):
                sc.mm(PS(7)[:, c * 128:(c + 1) * 128], xT[:, k, c * 128:(c + 1) * 128], w0[:, k, 1664:1792],
                      k == 0, k == 7, ['xT', 'w0'], [pk(7)])
        sc.copy('act', va[:, i * 4:(i + 1) * 4, :, 0:64], PS(7).rearrange("p (c g d) -> p c g d", c=4, g=2),
                [pk(7)], ['va'])
    sc.barrier()
    ar.release(mA)
    if stop_after == 'L0A':
        sc.finalize()
        return nc

    mW = ar.mark()
    bhi = ar.alloc([8, 384], BF16)
    blo = ar.alloc([8, 384], BF16)
    pT = [ar.alloc([384], BF16) for _ in range(4)]
    rd = ar.alloc([512], F32)
    mixt = ar.alloc([512], BF16)
    sc.dma('sp', bhi, biasw_hi_d, (), ['biasw'])
    sc.dma('sp', blo, biasw_lo_d, (), ['biasw'])
    scale_a = 64.0 ** -0.5
    for h in range(8):
        g = h // 4
        f = h // 2
        rbs = (h % 2) * 64
        pr = slice(rbs, rbs + 64)

        def q0_of(j):
            return max(0, (j - 1) * 128)

        def s_step(j):
            q0 = q0_of(j)
            q1 = min(S, (j + 2) * 128)
            n = q1 - q0
            off = q0 - (j - 1) * 128
            bank = j % 3
            sc.mm(PS(bank)[:, 0:n], identb, bhi[:, h, off:off + n], True, False, ['identb', 'biasw'], [pk(bank)])
            sc.mm(PS(bank)[:, 0:n], identb, blo[:, h, off:off + n], False, False, ['identb', 'biasw'], [pk(bank)])
            sc.mm(PS(bank)[:, 0:n], kaT2[pr, g, j * 128:(j + 1) * 128], qaT[pr, f, q0:q1], False, True,
                  ['kaT2', 'qaT'], [pk(bank)])
            sc.act(pT[j % 4][:, 0:n], PS(bank)[:, 0:n], AF.Exp, [pk(bank)], ['pT%d' % (j % 4)], scale=scale_a)

        def pv_step(i):
            pob = 3 + ((i // 4) % 2)
            js = [jj for jj in (i - 1, i, i + 1) if 0 <= jj < NT]
            for n_, jj in enumerate(js):
                c0 = i * 128 - q0_of(jj)
                sc.mm(PS(pob)[:, (i % 4) * 128:(i % 4 + 1) * 128], va[:, jj, g, :], pT[jj % 4][:, c0:c0 + 128],
                      n_ == 0, n_ == len(js) - 1, ['va', 'va_ones', 'pT%d' % (jj % 4)], [pk(pob)])
            if i % 4 == 3:
                t0 = (i // 4) * 512
                attn_norm_store(pob, 512, esink[64:128, h:h + 1], mixd[f, pr, t0:t0 + 512], rd, mixt)

        s_step(0)
        s_step(1)
        for j in range(2, NT):
            s_step(j)
            pv_step(j - 2)
        pv_step(NT - 2)
        pv_step(NT - 1)
    sc.barrier()
    ar.release(mW)
    if stop_after == 'L0W':
        sc.finalize()
        return nc

    mM = ar.mark()
    wq = ar.alloc([3, 768], BF16)
    wqs = ar.alloc([3, 768], BF16)
    wkv = ar.alloc([2, 1024], BF16)
    QT = [ar.alloc([S], BF16) for _ in range(2)]
    vm = [ar.alloc([NT, 128], BF16) for _ in range(2)]
    csq = [ar.alloc([512], F32) for _ in range(2)]
    snq = [ar.alloc([512], F32) for _ in range(2)]
    u1 = ar.alloc([512], F32)
    u2 = ar.alloc([512], F32)
    pbuf = [ar.alloc([512], BF16) for _ in range(4)]
    rd = ar.alloc([512], F32)
    mixt = ar.alloc([512], BF16)
    sc.dma('pool', wq, wq_d.rearrange("(k p) f -> p k f", p=128), (), ['wq'])
    sc.dma('pool', wqs, wqs_d.rearrange("(k p) f -> p k f", p=128), (), ['wqs'])
    sc.dma('pool', wkv, wkv_d.rearrange("(k p) f -> p k f", p=128), (), ['wkv'])
    for b in range(2):
        sc.memset('pool', vm[b][:, :, 64:128], 1.0, ['vm_ones%d' % b])
    scale_b = 96.0 ** -0.5

    def mla_proj(h):
        b = h % 2
        for i in range(8):
            tl = slice(i * 512, (i + 1) * 512)
            cb = i % 2
            sc.dma('sp', csq[cb][64:96, :], cos_d[:, tl], (), ['csq%d' % cb])
            sc.dma('sp', snq[cb][64:96, :], sin_d[:, tl], (), ['snq%d' % cb])
            for r in range(3):
                sc.mm(PS(5)[0:96, :], wq[:, r, h * 96:(h + 1) * 96], cqn[:, r, tl], r == 0, r == 2, ['wq', 'cqn'], [pk(5)])
            for r in range(3):
                sc.mm(PS(6)[0:96, :], wqs[:, r, h * 96:(h + 1) * 96], cqn[:, r, tl], r == 0, r == 2, ['wqs', 'cqn'], [pk(6)])
            sc.copy('act', QT[b][0:64, tl], PS(5)[0:64, :], [pk(5)], ['QTn%d' % b, 'psord'])
            sc.tt('dve', u1[64:96, :], PS(5)[64:96, :], csq[cb][64:96, :], ALU.mult, [pk(5), 'csq%d' % cb, 'psord'], ['u1'])
            sc.tt('dve', u2[64:96, :], PS(6)[64:96, :], snq[cb][64:96, :], ALU.mult, [pk(6), 'snq%d' % cb], ['u2'])
            sc.tt('pool', QT[b][64:96, tl], u1[64:96, :], u2[64:96, :], ALU.add, ['u1', 'u2'], ['QTp%d' % b])
            for c in range(2):
                sc.mm(PS(7)[0:64, :], wkv[:, c, h * 128:h * 128 + 64], ckvn[:, c, tl], c == 0, c == 1, ['wkv', 'ckvn'], [pk(7)])
            sc.copy('act', KT[b][0:64, tl], PS(7)[0:64, :], [pk(7)], ['KTn%d' % b])
        for g4 in range(4):
            for t8 in range(8):
                tc = g4 * 8 + t8
                for c in range(2):
                    sc.mm(PS(7)[:, t8 * 64:(t8 + 1) * 64], ckvn[:, c, tc * 128:(tc + 1) * 128],
                          wkv[:, c, h * 128 + 64:h * 128 + 128], c == 0, c == 1, ['ckvn', 'wkv'], [pk(7)])
            sc.copy('dve', vm[b][:, g4 * 8:(g4 + 1) * 8, 0:64], PS(7).rearrange("p (t d) -> p t d", t=8), [pk(7)], ['vm%d' % b])

    def mla_attn(h):
        b = h % 2
        f = 4 + h // 2
        rbs = (h % 2) * 64
        kkeys = ['KTn%d' % b, 'KT%dpe' % b]
        qkeys = ['QTn%d' % b, 'QTp%d' % b]
        cnt = 0
        for i in range(8):
            tl = slice(i * 512, (i + 1) * 512)
            pob = 3 + (i % 2)

            def s_step(kc, slot):
                bank = slot % 3
                sc.mm(PS(bank), KT[b][0:96, kc * 128:(kc + 1) * 128], QT[b][0:96, tl], True, True, kkeys + qkeys, [pk(bank)])
                sc.act(pbuf[slot % 4], PS(bank), AF.Exp, [pk(bank)], ['pb%d' % (slot % 4)], scale=scale_b)

            def pv_step(kc, slot):
                sc.mm(PS(pob), vm[b][:, kc, :], pbuf[slot % 4], kc == 0, kc == NT - 1,
                      ['vm%d' % b, 'vm_ones%d' % b, 'pb%d' % (slot % 4)], [pk(pob)])

            s_step(0, cnt)
            s_step(1, cnt + 1)
            for kc in range(NT):
                if kc + 2 < NT:
                    s_step(kc + 2, cnt + kc + 2)
                pv_step(kc, cnt + kc)
            cnt += NT
            attn_norm_store(pob, 512, None, mixd[f, rbs:rbs + 64, tl], rd, mixt)

    mla_proj(0)
    for h in range(8):
        if h + 1 < 8:
            mla_proj(h + 1)
        mla_attn(h)
    sc.barrier()
    ar.release(base_mark)

    if debug:
        mD = ar.mark()
        mtb = ar.alloc([S], BF16)
        mtf = ar.alloc([S], F32)
        for f in range(8):
            sc.dma('sp', mtb, mixd[f], ['mixd'], ['mtb'])
            sc.copy('dve', mtf, mtb, ['mtb'], ['mtf'])
            sc.dma('sp', dbg["dbg_mix"][f], mtf, ['mtf'], ['dbg_mix'])
        sc.barrier()
        ar.release(mD)

    def resid0_load(tc, xcb, xck):
        sc.dma('sp', xcb, x[tc * 128:(tc + 1) * 128, :], (), [xck])

    def resid0(tc, pa, pb_, rbuf, rbk, xcb, xck):
        for half, bank in ((0, pa), (1, pb_)):
            sc.stt('dve', rbuf[:, half * 512:(half + 1) * 512], xcb[:, half * 512:(half + 1) * 512], ALPHA, PS(bank),
                   ALU.mult, ALU.add, [xck, pk(bank)], [rbk])

    affT, _ = outproj_ln_phase(0, wo_d[0], "ln0a_g", "ln0a_b", resid0_load, resid0)
    if stop_after == 'L0O':
        sc.finalize()
        return nc

    moe_phase(0, affT)
    ar.release(base_mark)
    if debug:
        mD = ar.mark()
        yb = ar.alloc([D], F32)
        for tc in range(NT):
            sc.dma('sp', yb, Yd[tc * 128:(tc + 1) * 128, :], ['Yd'], ['yb'])
            sc.dma('sp', dbg["dbg_y"][tc * 128:(tc + 1) * 128, :], yb, ['yb'], ['dbg_y'])
        sc.barrier()
        ar.release(mD)
    if stop_after == 'MOE0':
        sc.finalize()
        return nc

    h2T = ar.alloc([8, S], BF16)
    mL1 = ar.mark()
    gB = ar.alloc([D], F32)
    bB = ar.alloc([D], F32)
    yc = [ar.alloc([D], F32) for _ in range(2)]
    hbuf = [ar.alloc([D], F32) for _ in range(2)]
    ahb = [ar.alloc([D], F32) for _ in range(2)]
    hbf = [ar.alloc([D], BF16) for _ in range(2)]
    st = [ar.alloc([2, 6], F32) for _ in range(2)]
    mv = [ar.alloc([2], F32) for _ in range(2)]
    sd = [ar.alloc([1], F32) for _ in range(2)]
    load_bcast(gB, "ln0b_g", 'lng')
    load_bcast(bB, "ln0b_b", 'lnb')
    sc.dma('sp', yc[0], Yd[0:128, :], ['Yd'], ['yc0'])
    for tc in range(NT):
        b = tc % 2
        sf = str(b)
        rows = slice(tc * 128, (tc + 1) * 128)
        if tc + 1 < NT:
            sc.dma('sp', yc[1 - b], Yd[(tc + 1) * 128:(tc + 2) * 128, :], ['Yd'], ['yc%d' % (1 - b)])
        ln_chunk(yc[b], 'yc%d' % b, gB, bB, hbuf[b], 'hbuf' + sf, st[b], mv[b], sd[b], sf)
        sc.act(ahb[b], hbuf[b], AF.Copy, ['hbuf' + sf], ['ahb' + sf], scale=ALPHA)
        sc.dma('sp', R1[rows, :], ahb[b], ['ahb' + sf], ['R1'])
        sc.copy('act', hbf[b], hbuf[b], ['hbuf' + sf], ['hbf' + sf])
        tb = tc % 2
        for k in range(8):
            sc.tr(PSB(tb)[:, k * 128:(k + 1) * 128], hbf[b][:, k * 128:(k + 1) * 128], identb, ['hbf' + sf, 'identb'], [pk(tb)])
        sc.copy('dve', h2T[:, :, rows], PSB(tb).rearrange("p (k t) -> p k t", k=8), [pk(tb)], ['h2T'])
    sc.barrier()
    ar.release(mL1)

    wp = [ar.alloc([8, 384], BF16) for _ in range(2)]
    QTp = [ar.alloc([S], BF16) for _ in range(2)]
    KTp = [ar.alloc([S], BF16) for _ in range(2)]
    Vp = [ar.alloc([NT, 2, 128], BF16) for _ in range(2)]
    TTp = [ar.alloc([2, 16, 64], F32) for _ in range(2)]
    TTb = [ar.alloc([2, 16, 64], BF16) for _ in range(2)]
    namask = ar.alloc([16, 64], F32)
    tbn = [ar.alloc([5, 64], F32) for _ in range(4)]
    pTn = [ar.alloc([5, 64], BF16) for _ in range(5)]
    rd = ar.alloc([512], F32)
    mixt = ar.alloc([512], BF16)
    sc.dma('sp', namask, namask_d, (), ['namask'])
    for b in range(2):
        sc.memset('pool', Vp[b][:, :, :, 64:128], 1.0, ['vp_ones%d' % b])
    scale_c = 64.0 ** -0.5

    def na_proj(hp):
        b = hp % 2
        for part in range(3):
            sc.dma('pool', wp[b][:, :, part * 128:(part + 1) * 128],
                   wqkv_d[:, part * 1024 + hp * 128: part * 1024 + (hp + 1) * 128].rearrange("(k p) f -> p k f", p=128),
                   (), ['wp%d' % b])
        for g in range(2):
            sc.dma('sp', TTp[b][:, g], rpbT_d[hp * 2 + g], (), ['TT%d_%d' % (b, g)])
            sc.tt('pool', TTp[b][:, g], TTp[b][:, g], namask, ALU.add, ['TT%d_%d' % (b, g), 'namask'], ['TT%d_%d' % (b, g)])
            sc.ts('pool', TTb[b][:, g], TTp[b][:, g], 1.0 / scale_c, None, ALU.mult, None, ['TT%d_%d' % (b, g)], ['TTb%d_%d' % (b, g)])
        for i in range(8):
            tl = slice(i * 512, (i + 1) * 512)
            for k in range(8):
                sc.mm(PS(5), wp[b][:, k, 0:128], h2T[:, k, tl], k == 0, k == 7, ['wp%d' % b, 'h2T'], [pk(5)])
            sc.copy('act', QTp[b][:, tl], PS(5), [pk(5)], ['QTp%d' % b])
            for k in range(8):
                sc.mm(PS(6), wp[b][:, k, 128:256], h2T[:, k, tl], k == 0, k == 7, ['wp%d' % b, 'h2T'], [pk(6)])
            sc.copy('dve', KTp[b][:, tl], PS(6), [pk(6)], ['KTp%d' % b])
            for c in range(4):
                tc = i * 4 + c
                for k in range(8):
                    sc.mm(PS(5)[:, c * 128:(c + 1) * 128], h2T[:, k, tc * 128:(tc + 1) * 128], wp[b][:, k, 256:384],
                          k == 0, k == 7, ['h2T', 'wp%d' % b], [pk(5)])
            sc.copy('act', Vp[b][:, i * 4:(i + 1) * 4, :, 0:64], PS(5).rearrange("p (c g d) -> p c g d", c=4, g=2),
                    [pk(5)], ['Vp%d' % b])

    def na_attn(hp):
        b = hp % 2
        tasks = [(g, r) for g in range(2) for r in range(64)]

        def geom(r):
            rs = min(max(r - 4, 0), 56)
            odd = rs % 2
            kr0 = rs - odd
            nch = 5 if odd else 4
            return odd, kr0, nch, kr0 - r + 8

        def s_part(t):
            g, r = tasks[t]
            pr = slice(g * 64, g * 64 + 64)
            odd, kr0, nch, u0 = geom(r)
            bank = (0, 1, 2, 7)[t % 4]
            sc.mm(PS(bank)[:, 0:nch * 64].rearrange("p (c q) -> p c q", c=nch), identb,
                  TTb[b][:, g, u0:u0 + 2 * nch - 1:2, :], True, False, ['identb', 'TTb%d_%d' % (b, g)], [pk(bank)])
            for c in range(nch):
                kc = kr0 // 2 + c
                sc.mm(PS(bank)[:, c * 64:(c + 1) * 64], KTp[b][pr, kc * 128:(kc + 1) * 128], QTp[b][pr, r * 64:(r + 1) * 64],
                      False, c == nch - 1, ['KTp%d' % b, 'QTp%d' % b], [pk(bank)])
            sc.act(pTn[t % 5][:, 0:nch, :], PS(bank)[:, 0:nch * 64].rearrange("p (c q) -> p c q", c=nch), AF.Exp,
                   [pk(bank)], ['pTn%d' % (t % 5)], scale=scale_c)

        def pv_part(t):
            g, r = tasks[t]
            pr = slice(g * 64, g * 64 + 64)
            odd, kr0, nch, u0 = geom(r)
            pob = 3 + ((r // 8) % 2)
            pkk = 'pTn%d' % (t % 5)
            for c in range(nch):
                kc = kr0 // 2 + c
                if odd and c == 0:
                    ps_ = slice(64, 128)
                elif odd and c == nch - 1:
                    ps_ = slice(0, 64)
                else:
                    ps_ = slice(0, 128)
                sc.mm(PS(pob)[:, (r % 8) * 64:(r % 8 + 1) * 64], Vp[b][ps_, kc, g, :], pTn[t % 5][ps_, c, :],
                      c == 0, c == nch - 1, ['Vp%d' % b, 'vp_ones%d' % b, pkk], [pk(pob)])
            if r % 8 == 7:
                t0 = (r // 8) * 512
                attn_norm_store(pob, 512, None, mixd[hp, pr, t0:t0 + 512], rd, mixt)

        LA = 3
        for t in range(min(LA, len(tasks))):
            s_part(t)
        for t in range(len(tasks)):
            if t + LA < len(tasks):
                s_part(t + LA)
            pv_part(t)

    na_proj(0)
    for hp in range(8):
        if hp + 1 < 8:
            na_proj(hp + 1)
        na_attn(hp)
    sc.barrier()
    ar.release(base_mark)

    def resid1_load(tc, xcb, xck):
        sc.dma('sp', xcb, R1[tc * 128:(tc + 1) * 128, :], ['R1'], [xck])

    def resid1(tc, pa, pb_, rbuf, rbk, xcb, xck):
        for half, bank in ((0, pa), (1, pb_)):
            sc.tt('dve', rbuf[:, half * 512:(half + 1) * 512], xcb[:, half * 512:(half + 1) * 512], PS(bank),
                  ALU.add, [xck, pk(bank)], [rbk])

    affT, _ = outproj_ln_phase(1, wo_d[1], "ln1a_g", "ln1a_b", resid1_load, resid1)
    moe_phase(1, affT)
    ar.release(base_mark)

    gB = ar.alloc([D], F32)
    bB = ar.alloc([D], F32)
    yc = [ar.alloc([D], F32) for _ in range(2)]
    hb2 = [ar.alloc([D], F32) for _ in range(2)]
    st = [ar.alloc([2, 6], F32) for _ in range(2)]
    mv = [ar.alloc([2], F32) for _ in range(2)]
    sd = [ar.alloc([1], F32) for _ in range(2)]
    load_bcast(gB, "ln1b_g", 'lng')
    load_bcast(bB, "ln1b_b", 'lnb')
    sc.dma('sp', yc[0], Yd[0:128, :], ['Yd'], ['yc0'])
    for tc in range(NT):
        b = tc % 2
        rows = slice(tc * 128, (tc + 1) * 128)
        if tc + 1 < NT:
            sc.dma('sp', yc[1 - b], Yd[(tc + 1) * 128:(tc + 2) * 128, :], ['Yd'], ['yc%d' % (1 - b)])
        ln_chunk(yc[b], 'yc%d' % b, gB, bB, hb2[b], 'hb%d' % b, st[b], mv[b], sd[b], str(b))
        sc.dma('sp', out_d[rows, :], hb2[b], ['hb%d' % b], ['out'])
    sc.finalize()
    return nc


def _consts():
    c = {}
    c["ident_bf"] = np.eye(128, dtype=np.float32).astype(ml_dtypes.bfloat16)
    c["ident_f"] = np.eye(128, dtype=np.float32)
    half = 16
    inv = (10000.0 ** (-np.arange(half, dtype=np.float32) / half)).astype(np.float32)
    ang = np.arange(S, dtype=np.float32)[None, :] * inv[:, None]
    cos = np.cos(ang).astype(np.float32)
    sin = np.sin(ang).astype(np.float32)
    c["cos_t"] = np.concatenate([cos, cos], 0)
    c["sin_t"] = np.concatenate([-sin, sin], 0)
    k = np.arange(128)[:, None]
    qp = np.arange(384)[None, :]
    dist = np.abs(qp - 128 - k).astype(np.float32)
    slopes = 2.0 ** (-8.0 * (np.arange(8, dtype=np.float32) + 1.0) / 8)
    bw = np.where(dist[:, None, :] <= 128, -slopes[None, :, None] * dist[:, None, :], NEG).astype(np.float32)
    bws = (bw.astype(np.float64) / (64.0 ** -0.5)).astype(np.float32)
    hi = bws.astype(ml_dtypes.bfloat16)
    lo = (bws - hi.astype(np.float32)).astype(ml_dtypes.bfloat16)
    c["biasw_hi"] = np.ascontiguousarray(hi)
    c["biasw_lo"] = np.ascontiguousarray(lo)
    cols = np.arange(64)
    cstart = np.clip(cols - 8, 0, 48)
    valid = (cols[None, :] >= cstart[:, None]) & (cols[None, :] < cstart[:, None] + 16)
    m = np.where(valid.T, 0.0, NEG).astype(np.float32)
    m2 = np.concatenate([m, m], 0)
    c["namask"] = np.ascontiguousarray(np.broadcast_to(m2[:, None, :], (128, 16, 64)))
    c["cvals"] = (np.arange(4)[None, :] * 128 + np.arange(128)[:, None]).astype(np.float32)
    return c


def _prep_shared(inp):
    f32 = np.float32
    d = {}
    w_in0 = np.asarray(inp["w_in0"], f32)
    qa = w_in0[:, 0:512]
    ka0, ka1 = w_in0[:, 512:576], w_in0[:, 576:640]
    vaw = w_in0[:, 640:768]
    cq = w_in0[:, 768:1152]
    ckv = w_in0[:, 1152:1408]
    kr = w_in0[:, 1408:1440]
    krs = np.concatenate([kr[:, 16:32], kr[:, 0:16]], 1)
    d["w0f"] = np.ascontiguousarray(np.concatenate(
        [qa, ka0, ka0, ka1, ka1, cq, ckv, kr, kr, kr, kr, krs, krs, krs, krs], 1))
    d["w0v"] = np.ascontiguousarray(vaw)
    d["a_sink"] = np.asarray(inp["a_sink"], f32).reshape(1, 8)
    d["gq"] = np.ascontiguousarray(np.asarray(inp["mla_q_norm"], f32).reshape(3, 128).T)
    d["gkv"] = np.ascontiguousarray(np.asarray(inp["mla_kv_norm"], f32).reshape(2, 128).T)
    wq = np.asarray(inp["w_q_up"], f32)
    d["wq"] = wq
    wq3 = wq.reshape(384, 8, 96)
    d["wqs"] = np.ascontiguousarray(
        np.concatenate([wq3[:, :, 0:64], wq3[:, :, 80:96], wq3[:, :, 64:80]], 2).reshape(384, 768))
    d["wkv"] = np.asarray(inp["w_kv_up"], f32)
    d["wo0"] = np.asarray(inp["w_out0"], f32)
    d["wo1"] = np.asarray(inp["w_out1"], f32)
    for n in ["ln0a_g", "ln0a_b", "ln0b_g", "ln0b_b", "ln1a_g", "ln1a_b", "ln1b_g", "ln1b_b"]:
        d[n] = np.asarray(inp[n], f32).reshape(1, D)
    for n in ["router0", "router1", "w_gate0", "w_gate1", "w_up0", "w_up1", "w_down0", "w_down1", "w_qkv1"]:
        d[n] = np.asarray(inp[n], f32)
    rpb = np.asarray(inp["na_rpb"], f32)
    cols = np.arange(64)
    dc = np.clip(cols[None, :] - cols[:, None] + 15, 0, 30)
    u = np.arange(16)
    out = np.empty((16, 2, 64, 16, 64), f32)
    for kr2 in range(2):
        dr = np.clip(u + kr2 - 8, -7, 7) + 7
        g_ = rpb[:, dr[:, None, None], dc[None, :, :]]
        out[:, kr2] = np.transpose(g_, (0, 3, 1, 2))
    d["rpbT"] = np.ascontiguousarray(out.reshape(16, 128, 16, 64))
    d.update(_consts())
    return d


_CACHE = {}


def kernel(**inputs):
    x = np.asarray(inputs["x"], np.float32)
    shared = _prep_shared(inputs)
    if "nc" not in _CACHE:
        _CACHE["nc"] = build_program()
    nc = _CACHE["nc"]
    in_maps = []
    for c in range(N_CORES):
        m = dict(shared)
        m["x"] = np.ascontiguousarray(x[c])
        in_maps.append(m)
    res = run_bass_kernel_spmd(nc, in_maps, core_ids=list(range(N_CORES)))
    return np.stack([np.asarray(r["out"], np.float32) for r in res.results], 0)
```

```python
import math
import numpy as np
import ml_dtypes
import concourse.bass as bass
import concourse.mybir as mybir
from concourse.bass_utils import run_bass_kernel_spmd

F32 = mybir.dt.float32
BF16 = mybir.dt.bfloat16
F16 = mybir.dt.float16
I32 = mybir.dt.int32
U8 = mybir.dt.uint8
ALU = mybir.AluOpType
AF = mybir.ActivationFunctionType
DSZ = {F32: 4, BF16: 2, F16: 2, I32: 4, U8: 1}

S = 4096
D = 1024
NT = 32
NEG = -1.0e30
ALPHA = 4.0 ** 0.25
N_CORES = 8
ENGS = ['pe', 'act', 'dve', 'pool', 'sp']
N_DSEM = 48


class Sched:
    def __init__(self, nc):
        self.nc = nc
        self.ops = []

    def add(self, eng, fn, r=(), w=(), dma=False):
        self.ops.append(dict(eng=eng, fn=fn, r=list(r), w=list(w), dma=dma, sig=dma, bar=False))

    def barrier(self):
        self.ops.append(dict(eng=None, bar=True))

    def mm(self, out, lhsT, rhs, start, stop, r, w):
        self.add('pe', lambda e: e.matmul(out, lhsT, rhs, start=start, stop=stop), r, w)

    def tr(self, out, in_, ident, r, w):
        self.add('pe', lambda e: e.transpose(out, in_, ident), r, w)

    def act(self, out, in_, func, r, w, bias=None, scale=None, accum=None):
        kw = {}
        if bias is not None:
            kw['bias'] = bias
        if scale is not None:
            kw['scale'] = scale
        if accum is not None:
            kw['accum_out'] = accum
        self.add('act', lambda e: e.activation(out, in_, func, **kw), r, w)

    def ts(self, eng, out, in0, s1, s2, op0, op1, r, w, accum=None):
        if accum is not None:
            self.add(eng, lambda e: e.tensor_scalar(out, in0, s1, s2, op0, op1, accum_out=accum), r, w)
        elif op1 is None:
            self.add(eng, lambda e: e.tensor_scalar(out, in0, s1, None, op0), r, w)
        else:
            self.add(eng, lambda e: e.tensor_scalar(out, in0, s1, s2, op0, op1), r, w)

    def tt(self, eng, out, in0, in1, op, r, w):
        self.add(eng, lambda e: e.tensor_tensor(out, in0, in1, op), r, w)

    def stt(self, eng, out, in0, scalar, in1, op0, op1, r, w):
        self.add(eng, lambda e: e.scalar_tensor_tensor(out, in0, scalar, in1, op0, op1), r, w)

    def copy(self, eng, out, in_, r, w):
        if eng == 'act':
            self.add('act', lambda e: e.copy(out, in_), r, w)
        else:
            self.add(eng, lambda e: e.tensor_copy(out, in_), r, w)

    def recip(self, out, in_, r, w):
        self.add('dve', lambda e: e.reciprocal(out, in_), r, w)

    def memset(self, eng, ap, val, w):
        self.add(eng, lambda e: e.memset(ap, val), (), w)

    def dma(self, eng, out, in_, r, w, **kw):
        self.add(eng, lambda e: e.dma_start(out=out, in_=in_, **kw), r, w, dma=True)

    def idma(self, out, out_off, in_, in_off, r, w, **kw):
        self.add('pool', lambda e: e.indirect_dma_start(out=out, out_offset=out_off, in_=in_,
                                                        in_offset=in_off, **kw), r, w, dma=True)

    def finalize(self):
        nc = self.nc
        ops = self.ops
        last_w, readers = {}, {}
        eng_last = {e: None for e in ENGS}
        outstanding = []
        for i, op in enumerate(ops):
            if op['bar']:
                op['deps'] = [v for v in eng_last.values() if v is not None] + outstanding
                outstanding = []
                last_w.clear()
                readers.clear()
                for d in op['deps']:
                    ops[d]['sig'] = True
                continue
            deps = {}
            for k in op['r']:
                if k in last_w:
                    deps[last_w[k]] = 'raw'
            for k in op['w']:
                if k in last_w:
                    deps.setdefault(last_w[k], 'waw')
                for rr in readers.get(k, ()):
                    deps.setdefault(rr, 'war')
            deps.pop(i, None)
            keep = []
            for d, kind in deps.items():
                p = ops[d]
                if p['eng'] == op['eng'] and not p['dma']:
                    if op['eng'] == 'pe':
                        continue
                keep.append(d)
            op['deps'] = keep
            for d in keep:
                ops[d]['sig'] = True
            for k in op['r']:
                lst = readers.setdefault(k, [])
                if not op['dma']:
                    lst[:] = [q for q in lst if ops[q]['dma'] or ops[q]['eng'] != op['eng']]
                lst.append(i)
            for k in op['w']:
                last_w[k] = i
                readers[k] = []
            if op['dma']:
                outstanding.append(i)
            else:
                eng_last[op['eng']] = i
        cnt = {e: 0 for e in ENGS}
        dcnt = [0] * N_DSEM
        ndq = [0, 0]
        for op in ops:
            if op['bar']:
                continue
            if op['dma']:
                half = N_DSEM // 2
                qi = 0 if op['eng'] == 'sp' else 1
                d = qi * half + (ndq[qi] % half)
                ndq[qi] += 1
                op['dsem'] = d
                op['dprev'] = dcnt[d]
                dcnt[d] += 16
                op['dval'] = dcnt[d]
            elif op['sig']:
                cnt[op['eng']] += 1
                op['sval'] = cnt[op['eng']]
        import contextlib
        with contextlib.ExitStack() as st:
            esem = {e: st.enter_context(nc.semaphore("s_" + e)) for e in ENGS}
            dsem = [st.enter_context(nc.semaphore("d_%d" % i)) for i in range(N_DSEM)]
            block = st.enter_context(nc.Block())

            def waits_for(deps):
                need = {}
                for d in deps:
                    p = ops[d]
                    if p['dma']:
                        key, val = ('d', p['dsem']), p['dval']
                    else:
                        key, val = ('e', p['eng']), p['sval']
                    if need.get(key, 0) < val:
                        need[key] = val
                return need

            def emit(eng, e):
                seen = {}

                def do_waits(need):
                    for key, val in need.items():
                        if seen.get(key, 0) >= val:
                            continue
                        seen[key] = val
                        sem = dsem[key[1]] if key[0] == 'd' else esem[key[1]]
                        e.wait_ge(sem, val)

                for op in ops:
                    if op['bar']:
                        do_waits(waits_for(op['deps']))
                        continue
                    if op['eng'] != eng:
                        continue
                    need = waits_for(op['deps'])
                    if op['dma'] and op['dprev'] > 0:
                        key = ('d', op['dsem'])
                        if need.get(key, 0) < op['dprev']:
                            need[key] = op['dprev']
                    do_waits(need)
                    ins = op['fn'](e)
                    if op['dma']:
                        ins.then_inc(dsem[op['dsem']], 16)
                    elif op['sig']:
                        ins.then_inc(esem[eng], 1)
                if eng == 'sp':
                    for d in range(N_DSEM):
                        if dcnt[d] > 0 and seen.get(('d', d), 0) < dcnt[d]:
                            e.wait_ge(dsem[d], dcnt[d])
                    for en in ENGS:
                        if cnt[en] > 0 and seen.get(('e', en), 0) < cnt[en]:
                            e.wait_ge(esem[en], cnt[en])

            @block.tensor
            def _(e):
                emit('pe', e)

            @block.scalar
            def _(e):
                emit('act', e)

            @block.vector
            def _(e):
                emit('dve', e)

            @block.gpsimd
            def _(e):
                emit('pool', e)

            @block.sync
            def _(e):
                emit('sp', e)


class Arena:
    def __init__(self, nc, nbytes):
        self.t = nc.alloc_sbuf_tensor("arena", [128, nbytes], U8)
        self.cap = nbytes
        self.off = 0

    def alloc(self, shape, dtype):
        n = int(np.prod(shape)) * DSZ[dtype]
        n_al = (n + 63) // 64 * 64
        assert self.off + n_al <= self.cap, ("arena overflow", self.off, n_al, self.cap)
        ap = self.t[:, self.off:self.off + n].bitcast(dtype)
        self.off += n_al
        if len(shape) == 2:
            ap = ap.rearrange("p (a b) -> p a b", a=shape[0])
        elif len(shape) == 3:
            ap = ap.rearrange("p (a b c) -> p a b c", a=shape[0], b=shape[1])
        return ap

    def view(self, off, shape, dtype):
        n = int(np.prod(shape)) * DSZ[dtype]
        assert off + n <= self.cap
        ap = self.t[:, off:off + n].bitcast(dtype)
        if len(shape) == 2:
            ap = ap.rearrange("p (a b) -> p a b", a=shape[0])
        return ap

    def mark(self):
        return self.off

    def release(self, m):
        self.off = m


def build_program(stop_after=None, debug=False):
    nc = bass.Bass("TRN2", target_bir_lowering=False)
    sc = Sched(nc)

    def din(name, shape, dt=F32):
        return nc.dram_tensor(name, list(shape), dt, kind="ExternalInput").ap()

    def dscr(name, shape, dt):
        return nc.dram_tensor(name, list(shape), dt, kind="Internal").ap()

    x = din("x", [S, D])
    w0f = din("w0f", [D, 1664])
    w0v = din("w0v", [D, 128])
    a_sink = din("a_sink", [1, 8])
    gq_d = din("gq", [128, 3])
    gkv_d = din("gkv", [128, 2])
    wq_d = din("wq", [384, 768])
    wqs_d = din("wqs", [384, 768])
    wkv_d = din("wkv", [256, 1024])
    wo_d = [din("wo0", [D, D]), din("wo1", [D, D])]
    lnp = {n: din(n, [1, D]) for n in
           ["ln0a_g", "ln0a_b", "ln0b_g", "ln0b_b", "ln1a_g", "ln1a_b", "ln1b_g", "ln1b_b"]}
    router_d = [din("router0", [D, 16]), din("router1", [D, 16])]
    NE_ = 1 if stop_after in ('INIT', 'L0A', 'L0W', 'L0M', 'L0O') else 16
    wg_d = [din("w_gate0", [NE_, D, 2048]), din("w_gate1", [NE_, D, 2048])]
    wu_d = [din("w_up0", [NE_, D, 2048]), din("w_up1", [NE_, D, 2048])]
    wd_d = [din("w_down0", [NE_, 2048, D]), din("w_down1", [NE_, 2048, D])]
    wqkv_d = din("w_qkv1", [D, 3072])
    rpbT_d = din("rpbT", [16, 128, 16, 64])
    identb_d = din("ident_bf", [128, 128], BF16)
    identf_d = din("ident_f", [128, 128])
    cos_d = din("cos_t", [32, S])
    sin_d = din("sin_t", [32, S])
    biasw_hi_d = din("biasw_hi", [128, 8, 384], BF16)
    biasw_lo_d = din("biasw_lo", [128, 8, 384], BF16)
    namask_d = din("namask", [128, 16, 64])
    cvals_d = din("cvals", [128, 4])
    out_d = nc.dram_tensor("out", [S, D], F32, kind="ExternalOutput").ap()

    Hb = dscr("Hb", [S, D], BF16)
    Yd = dscr("Yd", [S, D], F32)
    R1 = dscr("R1", [S, D], F32)
    affd = dscr("affd", [S, 16], F32)
    cumd = dscr("cumd", [16, S], F16)
    mixd = dscr("mixd", [8, 128, S], BF16)
    dbg = {}
    if debug:
        for n, shp, dt in [("dbg_h1", [S, D], F32), ("dbg_mix", [8, 128, S], F32), ("dbg_aff", [S, 16], F32),
                           ("dbg_y", [S, D], F32), ("dbg_cum", [16, S], F32)]:
            dbg[n] = nc.dram_tensor(n, shp, dt, kind="ExternalOutput").ap()

    ar = Arena(nc, 200 * 1024)
    psb = [nc.alloc_psum_tensor("psb%d" % i, [128, 512], F32) for i in range(8)]

    def PS(b):
        return psb[b][:, :]

    def PSB(b):
        return psb[b][:, :].bitcast(BF16)

    def pk(b):
        return "ps%d" % b

    identb = ar.alloc([128], BF16)
    identf = ar.alloc([128], F32)
    onesb = ar.alloc([128], BF16)
    esink = ar.alloc([8], F32)
    cvals = ar.alloc([4], F32)
    eps_rms = ar.alloc([1], F32)
    eps_ln = ar.alloc([1], F32)
    afft = ar.alloc([NT, 16], F32)
    sc.dma('sp', identb, identb_d, (), ['identb'])
    sc.dma('sp', identf, identf_d, (), ['identf'])
    sc.dma('sp', cvals, cvals_d, (), ['cvals'])
    sc.dma('sp', esink, a_sink.partition_broadcast(128), (), ['esink'])
    sc.memset('pool', onesb, 1.0, ['onesb'])
    sc.memset('pool', eps_rms, 1e-6, ['eps'])
    sc.memset('pool', eps_ln, 1e-5, ['eps'])
    sc.act(esink, esink, AF.Exp, ['esink'], ['esink'])
    base_mark = ar.mark()
    if stop_after == 'INIT':
        sc.finalize()
        return nc

    def load_bcast(dst, name, key):
        sc.dma('sp', dst, lnp[name].partition_broadcast(128), (), [key])

    def ln_stats(src, srck, st, mv, sd, sfx=''):
        for hh in range(2):
            sc.add('dve', (lambda a, b: (lambda e: e.bn_stats(a, b)))(st[:, hh, :], src[:, hh * 512:(hh + 1) * 512]),
                   [srck], ['lnst' + sfx])
        sc.add('dve', lambda e: e.bn_aggr(mv, st.rearrange("p a b -> p (a b)")), ['lnst' + sfx], ['lnmv' + sfx])
        sc.act(sd, mv[:, 1:2], AF.Ln, ['lnmv' + sfx, 'eps'], ['lnsd' + sfx], bias=eps_ln)
        sc.act(sd, sd, AF.Exp, ['lnsd' + sfx], ['lnsd' + sfx], scale=-0.5)

    def ln_apply(src, srck, gB, bB, dst, dstk, mv, sd, sfx=''):
        sc.stt('dve', dst, src, mv[:, 0:1], gB, ALU.subtract, ALU.mult, [srck, 'lnmv' + sfx, 'lng'], [dstk])
        sc.stt('dve', dst, dst, sd[:, 0:1], bB, ALU.mult, ALU.add, [dstk, 'lnsd' + sfx, 'lnb'], [dstk])

    def ln_chunk(src, srck, gB, bB, dst, dstk, st, mv, sd, sfx='', pool_affine=True):
        ln_stats(src, srck, st, mv, sd, sfx)
        ln_apply(src, srck, gB, bB, dst, dstk, mv, sd, sfx)

    def post_h_moe(h, hk, tc, layer, bufs, sfx=''):
        ahb, hbf, hT, ex, ssum, rtr, affT = bufs
        rows = slice(tc * 128, (tc + 1) * 128)
        sc.act(ahb, h, AF.Copy, [hk], ['ahb' + sfx], scale=ALPHA)
        sc.dma('sp', Yd[rows, :], ahb, ['ahb' + sfx], ['Yd'])
        sc.copy('act', hbf, h, [hk], ['hbf' + sfx])
        sc.dma('sp', Hb[rows, :], hbf, ['hbf' + sfx], ['Hb'])
        tb = 4 + (tc % 2)
        for k in range(8):
            sc.tr(PSB(tb)[:, k * 128:(k + 1) * 128], hbf[:, k * 128:(k + 1) * 128], identb, ['hbf' + sfx, 'identb'], [pk(tb)])
        sc.copy('act', hT, PSB(tb).rearrange("p (k t) -> p k t", k=8), [pk(tb)], ['hT' + sfx])
        for k in range(8):
            sc.mm(PS(6)[:, 0:16], hT[:, k, :], rtr[:, k, :], k == 0, k == 7, ['hT' + sfx, 'rtr'], [pk(6)])
        sc.act(ex, PS(6)[:, 0:16], AF.Exp, [pk(6)], ['ex' + sfx, 'ssum' + sfx], accum=ssum)
        sc.recip(ssum, ssum, ['ssum' + sfx], ['ssum' + sfx])
        sc.ts('dve', afft[:, tc, :], ex, ssum[:, 0:1], None, ALU.mult, None, ['ex' + sfx, 'ssum' + sfx], ['afft%d' % tc])
        sc.tr(PS(7)[0:16, (tc % 4) * 128:(tc % 4 + 1) * 128], afft[:, tc, :], identf, ['afft%d' % tc, 'identf'], [pk(7)])
        if tc % 4 == 3:
            g4 = tc // 4
            sc.copy('act', affT[0:16, g4 * 512:(g4 + 1) * 512], PS(7)[0:16, :], [pk(7)], ['affT'])

    def moe_route(affT, rbufs):
        junk, ones16, maskT, cum, lo, mid, cntt, step = rbufs
        sc.memset('pool', ones16, 1.0, ['ones16'])
        sc.memset('dve', lo, 0.0, ['lo'])
        for it in range(27):
            wk = 2.0 ** -(it + 1)
            sc.ts('dve', mid, lo, wk, None, ALU.add, None, ['lo'], ['mid'])
            sc.ts('dve', junk, affT, mid[:, 0:1], 0.0, ALU.is_ge, ALU.add, ['affT', 'mid'], ['junk', 'cnt'], accum=cntt)
            sc.ts('dve', step, cntt, 511.5, wk, ALU.is_ge, ALU.mult, ['cnt'], ['step'])
            sc.tt('dve', lo, lo, step, ALU.add, ['lo', 'step'], ['lo'])
        sc.ts('dve', maskT, affT, lo[:, 0:1], None, ALU.is_ge, None, ['affT', 'lo'], ['maskT'])
        sc.add('dve', lambda e: e.tensor_tensor_scan(cum, ones16, maskT, 0.0, ALU.mult, ALU.add),
               ['ones16', 'maskT'], ['cum'])
        sc.dma('sp', cumd, cum, ['cum'], ['cumd'])
        if debug:
            sc.dma('sp', dbg["dbg_cum"], maskT, ['maskT'], ['dbg_cum'])

    def moe_experts(layer, mb):
        (wring, cumB, idxf, idxi, xg, gg, xgT, hidT, ysb, sg, junkc) = mb
        NSL = len(wring)
        pieces = []
        for e in range(16):
            for fq in range(4):
                pieces.append((e, 'gu', fq))
            for dh in range(2):
                pieces.append((e, 'd', dh))
        state = dict(next_load=0)

        def load_piece(n):
            e, kind, q = pieces[n]
            slot = wring[n % NSL]
            key = 'w%d' % (n % NSL)
            if kind == 'gu':
                sc.dma('pool', slot[:, 0:8, :], wg_d[layer][e][:, q * 512:(q + 1) * 512].rearrange("(k p) f -> p k f", p=128),
                       (), [key])
                sc.dma('pool', slot[:, 8:16, :], wu_d[layer][e][:, q * 512:(q + 1) * 512].rearrange("(k p) f -> p k f", p=128),
                       (), [key])
            else:
                sc.dma('pool', slot, wd_d[layer][e][:, q * 512:(q + 1) * 512].rearrange("(f p) d -> p f d", p=128),
                       (), [key])

        def prefetch(upto):
            while state['next_load'] < min(upto, len(pieces)):
                load_piece(state['next_load'])
                state['next_load'] += 1

        def prep_a(e):
            b = e % 2
            b3 = e % 4
            g3 = e % 3
            sc.dma('sp', cumB[b], cumd[e:e + 1, :].partition_broadcast(128), ['cumd'], ['cumB%d' % b])
            for j in range(4):
                sc.ts('dve', junkc, cumB[b], cvals[:, j:j + 1], 0.0, ALU.is_le, ALU.add,
                      ['cumB%d' % b, 'cvals'], ['junkc', 'idxf%d' % b3], accum=idxf[b3][:, j:j + 1])
            sc.ts('dve', idxf[b3], idxf[b3], float(S - 1), None, ALU.min, None, ['idxf%d' % b3], ['idxf%d' % b3])
            sc.copy('dve', idxi[b3], idxf[b3], ['idxf%d' % b3], ['idxi%d' % b3])

        def prep_a2(e):
            b = e % 2
            b3 = e % 4
            g3 = e % 3
            for j in range(4):
                sc.idma(xg[b][:, j, :], None, Hb[:, :], bass.IndirectOffsetOnAxis(ap=idxi[b3][:, j:j + 1], axis=0),
                        ['idxi%d' % b3, 'Hb'], ['xg%d' % b])
                sc.idma(gg[g3][:, j, :], None, affd[:, :], bass.IndirectOffsetOnAxis(ap=idxi[b3][:, j:j + 1], axis=0),
                        ['idxi%d' % b3, 'affd'], ['gg%d' % g3])

        def prep_b(e):
            b = e % 2
            for j in range(4):
                tb = 0 if j % 2 == 0 else 7
                for k in range(8):
                    sc.tr(PSB(tb)[:, k * 128:(k + 1) * 128], xg[b][:, j, k * 128:(k + 1) * 128], identb,
                          ['xg%d' % b, 'identb'], [pk(tb)])
                sc.copy('act' if j % 2 == 0 else 'dve', xgT[b][:, :, j * 128:(j + 1) * 128],
                        PSB(tb).rearrange("p (k t) -> p k t", k=8), [pk(tb)], ['xgT%d' % b])

        prefetch(NSL)
        prep_a(0)
        prep_a2(0)
        prep_a(1)
        prep_a2(1)
        prep_b(0)
        n = 0
        for e in range(16):
            b = e % 2
            b3 = e % 4
            g3 = e % 3
            if e + 2 < 16:
                prep_a(e + 2)
            for fq in range(4):
                if fq == 2 and e + 2 < 16:
                    prep_a2(e + 2)
                if fq == 3 and e + 1 < 16:
                    prep_b(e + 1)
                slot = wring[n % NSL]
                wkey = 'w%d' % (n % NSL)
                for fcl in range(4):
                    fc = fq * 4 + fcl
                    bg, bu = 1 + (fc % 2), 3 + (fc % 2)
                    for k in range(8):
                        sc.mm(PS(bg), slot[:, k, fcl * 128:(fcl + 1) * 128], xgT[b][:, k, :], k == 0, k == 7,
                              [wkey, 'xgT%d' % b], [pk(bg)])
                    for k in range(8):
                        sc.mm(PS(bu), slot[:, 8 + k, fcl * 128:(fcl + 1) * 128], xgT[b][:, k, :], k == 0, k == 7,
                              [wkey, 'xgT%d' % b], [pk(bu)])
                    sc.act(sg[fc % 2], PS(bg), AF.Silu, [pk(bg)], ['sg%d' % (fc % 2)])
                    sc.tt('dve', hidT[:, fc, :], sg[fc % 2], PS(bu), ALU.mult, ['sg%d' % (fc % 2), pk(bu)], ['hid%d' % fc])
                n += 1
                prefetch(n + NSL)
            for dh in range(2):
                slot = wring[n % NSL]
                wkey = 'w%d' % (n % NSL)
                for j in range(4):
                    bd = 5 + (j % 2)
                    for fc in range(16):
                        sc.mm(PS(bd), hidT[:, fc, j * 128:(j + 1) * 128], slot[:, fc, :], fc == 0, fc == 15,
                              ['hid%d' % fc, wkey], [pk(bd)])
                    sc.act(ysb[:, j, dh * 512:(dh + 1) * 512], PS(bd), AF.Copy, [pk(bd), 'gg%d' % g3], ['ysb%d' % j],
                           scale=gg[g3][:, j, e:e + 1])
                n += 1
                prefetch(n + NSL)
            for j in range(4):
                sc.idma(Yd[:, :], bass.IndirectOffsetOnAxis(ap=idxi[b3][:, j:j + 1], axis=0), ysb[:, j, :], None,
                        ['ysb%d' % j, 'idxi%d' % b3] + ['Yd_%d_%d' % ((e - 1) % 2, jj) for jj in range(4)],
                        ['Yd_%d_%d' % (e % 2, j)], compute_op=ALU.add)

    def moe_phase(layer, affT):
        m0 = ar.mark()
        wring = [ar.alloc([16, 512], BF16) for _ in range(6)]
        m1 = ar.mark()
        junk = ar.alloc([S], BF16)
        ones16 = ar.alloc([S], F32)
        maskT = ar.alloc([S], F32)
        cum = ar.alloc([S], F16)
        smalls = [ar.alloc([1], F32) for _ in range(4)]
        rb = (junk[0:16], ones16[0:16], maskT[0:16], cum[0:16]) + tuple(t[0:16] for t in smalls)
        assert ar.off <= 184 * 1024
        moe_route(affT[0:16], rb)
        sc.barrier()
        ar.release(m1)
        cumB = [ar.alloc([S], F16) for _ in range(2)]
        idxf = [ar.alloc([4], F32) for _ in range(4)]
        idxi = [ar.alloc([4], I32) for _ in range(4)]
        xg = [ar.alloc([4, D], BF16) for _ in range(2)]
        gg = [ar.alloc([4, 16], F32) for _ in range(3)]
        xgT = [ar.alloc([8, 512], BF16) for _ in range(2)]
        hidT = ar.alloc([16, 512], BF16)
        ysb = ar.alloc([4, D], F32)
        sg = [ar.alloc([512], F32) for _ in range(2)]
        junkc = ar.alloc([S], BF16)
        moe_experts(layer, (wring, cumB, idxf, idxi, xg, gg, xgT, hidT, ysb, sg, junkc))
        sc.barrier()
        ar.release(m0)

    def attn_norm_store(po_bank, nq, esk, mixrow, rd, mixt):
        if esk is not None:
            sc.ts('dve', rd[64:128, 0:nq], PS(po_bank)[64:128, 0:nq], esk, None, ALU.add, None,
                  [pk(po_bank), 'esink'], ['rdh'])
        else:
            sc.copy('dve', rd[64:128, 0:nq], PS(po_bank)[64:128, 0:nq], [pk(po_bank)], ['rdh'])
        sc.ts('dve', rd[0:64, 0:nq], rd[64:128, 0:nq], 1.0, None, ALU.mult, None, ['rdh'], ['rd'])
        sc.recip(rd[0:64, 0:nq], rd[0:64, 0:nq], ['rd'], ['rd'])
        sc.tt('dve', mixt[0:64, 0:nq], PS(po_bank)[0:64, 0:nq], rd[0:64, 0:nq], ALU.mult, [pk(po_bank), 'rd'], ['mixt'])
        sc.dma('sp', mixrow, mixt[0:64, 0:nq], ['mixt'], ['mixd'])

    def outproj_ln_phase(layer, wo_ap, gname, bname, resid_load, resid_fn):
        m0 = ar.mark()
        wo = ar.alloc([8, D], BF16)
        gB = ar.alloc([D], F32)
        bB = ar.alloc([D], F32)
        rtr = ar.alloc([8, 16], BF16)
        affT = ar.view(184 * 1024, [S], F32)
        mt = [ar.alloc([8, 128], BF16) for _ in range(2)]
        xc = [ar.alloc([D], F32) for _ in range(2)]
        rbuf = [ar.alloc([D], F32) for _ in range(2)]
        hbuf = [ar.alloc([D], F32) for _ in range(2)]
        ahb = [ar.alloc([D], F32) for _ in range(2)]
        hbf = [ar.alloc([D], BF16) for _ in range(2)]
        hT = [ar.alloc([8, 128], BF16) for _ in range(2)]
        ex = [ar.alloc([16], F32) for _ in range(2)]
        ssum = [ar.alloc([1], F32) for _ in range(2)]
        st = [ar.alloc([2, 6], F32) for _ in range(2)]
        mv = [ar.alloc([2], F32) for _ in range(2)]
        sd = [ar.alloc([1], F32) for _ in range(2)]
        sc.dma('pool', wo, wo_ap.rearrange("(k p) d -> p k d", p=128), (), ['wo'])
        sc.dma('pool', rtr, router_d[layer].rearrange("(k p) e -> p k e", p=128), (), ['rtr'])
        load_bcast(gB, gname, 'lng')
        load_bcast(bB, bname, 'lnb')
        def loads(tc):
            b = tc % 2
            sc.dma('sp', mt[b], mixd[:, :, tc * 128:(tc + 1) * 128].rearrange("f p t -> p f t"), ['mixd'], ['mt%d' % b])
            resid_load(tc, xc[b], 'xc%d' % b)

        def banks_of(tc):
            return (0, 1) if tc % 2 == 0 else (2, 3)

        def mms(tc):
            b = tc % 2
            for half, bank in zip((0, 1), banks_of(tc)):
                for f in range(8):
                    sc.mm(PS(bank), mt[b][:, f, :], wo[:, f, half * 512:(half + 1) * 512], f == 0, f == 7,
                          ['mt%d' % b, 'wo'], [pk(bank)])

        def resid(tc):
            b = tc % 2
            pa, pb_ = banks_of(tc)
            resid_fn(tc, pa, pb_, rbuf[b], 'rbuf%d' % b, xc[b], 'xc%d' % b)

        def stats(tc):
            b = tc % 2
            ln_stats(rbuf[b], 'rbuf%d' % b, st[b], mv[b], sd[b], str(b))

        loads(0)
        loads(1)
        mms(0)
        mms(1)
        resid(0)
        stats(0)
        for tc in range(NT):
            b = tc % 2
            sf = str(b)
            if tc + 2 < NT:
                loads(tc + 2)
                mms(tc + 2)
            if tc + 1 < NT:
                resid(tc + 1)
                stats(tc + 1)
            ln_apply(rbuf[b], 'rbuf' + sf, gB, bB, hbuf[b], 'hbuf' + sf, mv[b], sd[b], sf)
            post_h_moe(hbuf[b], 'hbuf' + sf, tc, layer, (ahb[b], hbf[b], hT[b], ex[b], ssum[b], rtr, affT), sf)
            if debug and layer == 0:
                sc.dma('sp', dbg["dbg_h1"][tc * 128:(tc + 1) * 128, :], hbuf[b], ['hbuf' + sf], ['dbg_h1'])
        sc.dma('sp', affd.rearrange("(c p) e -> p c e", p=128), afft, ['afft%d' % t for t in range(NT)], ['affd'])
        if debug and layer == 0:
            sc.dma('sp', dbg["dbg_aff"].rearrange("(c p) e -> p c e", p=128), afft, ['afft%d' % t for t in range(NT)], ['dbg_aff'])
        sc.barrier()
        ar.release(m0)
        return affT, m0

    qaT = ar.alloc([4, S], BF16)
    kaT2 = ar.alloc([2, S], BF16)
    va = ar.alloc([NT, 2, 128], BF16)
    cqn = ar.alloc([3, S], BF16)
    ckvn = ar.alloc([2, S], BF16)
    KT = [ar.alloc([S], BF16) for _ in range(2)]
    mA = ar.mark()
    w0 = ar.alloc([8, 1792], BF16)
    xb = ar.alloc([4, D], BF16)
    xT = ar.alloc([8, 512], BF16)
    cqg = ar.alloc([3, 512], F32)
    ckg = ar.alloc([2, 512], F32)
    sq = ar.alloc([5, 512], BF16)
    rq = ar.alloc([512], F32)
    rk = ar.alloc([512], F32)
    cs = ar.alloc([512], F32)
    sn = ar.alloc([512], F32)
    t1 = ar.alloc([512], F32)
    t2 = ar.alloc([512], F32)
    gq = ar.alloc([3], F32)
    gkv = ar.alloc([2], F32)
    sc.dma('pool', w0[:, :, 0:1664], w0f.rearrange("(k p) f -> p k f", p=128), (), ['w0'])
    sc.dma('pool', w0[:, :, 1664:1792], w0v.rearrange("(k p) f -> p k f", p=128), (), ['w0'])
    sc.dma('sp', gq, gq_d, (), ['gq'])
    sc.dma('sp', gkv, gkv_d, (), ['gkv'])
    sc.memset('pool', va[:, :, :, 64:128], 1.0, ['va_ones'])
    for i in range(8):
        tl = slice(i * 512, (i + 1) * 512)
        sc.dma('pool', xb, x[tl, :].rearrange("(c p) d -> p c d", p=128), (), ['xb'])
        sc.dma('sp', cs[64:96, :], cos_d[:, tl], (), ['cs'])
        sc.dma('sp', sn[64:96, :], sin_d[:, tl], (), ['sn'])
        for c in range(4):
            tb = c % 2
            for k in range(8):
                sc.tr(PSB(tb)[:, k * 128:(k + 1) * 128], xb[:, c, k * 128:(k + 1) * 128], identb, ['xb', 'identb'], [pk(tb)])
            sc.copy('act' if c % 2 == 0 else 'dve', xT[:, :, c * 128:(c + 1) * 128],
                    PSB(tb).rearrange("p (k t) -> p k t", k=8), [pk(tb)], ['xT'])
        for f in range(13):
            bank = 2 + (f % 3)
            for k in range(8):
                sc.mm(PS(bank), w0[:, k, f * 128:(f + 1) * 128], xT[:, k, :], k == 0, k == 7, ['w0', 'xT'], [pk(bank)])
            if f < 4:
                sc.copy('act', qaT[:, f, tl], PS(bank), [pk(bank)], ['qaT'])
            elif f < 6:
                sc.copy('dve', kaT2[:, f - 4, tl], PS(bank), [pk(bank)], ['kaT2'])
            elif f < 9:
                r = f - 6
                sc.act(sq[:, r, :], PS(bank), AF.Square, [pk(bank)], ['sq%d' % r, 'psord'])
                sc.ts('dve', cqg[:, r, :], PS(bank), gq[:, r:r + 1], None, ALU.mult, None, [pk(bank), 'gq', 'psord'], ['cqg%d' % r])
            elif f < 11:
                r = f - 9
                sc.act(sq[:, 3 + r, :], PS(bank), AF.Square, [pk(bank)], ['sq%d' % (3 + r), 'psord'])
                sc.ts('dve', ckg[:, r, :], PS(bank), gkv[:, r:r + 1], None, ALU.mult, None, [pk(bank), 'gkv', 'psord'], ['ckg%d' % r])
            elif f == 11:
                sc.tt('dve', t1[64:96, :], PS(bank)[64:96, :], cs[64:96, :], ALU.mult, [pk(bank), 'cs'], ['t1'])
            else:
                sc.tt('dve', t2[64:96, :], PS(bank)[64:96, :], sn[64:96, :], ALU.mult, [pk(bank), 'sn'], ['t2'])
                sc.tt('pool', KT[0][64:96, tl], t1[64:96, :], t2[64:96, :], ALU.add, ['t1', 't2'], ['KT0pe'])
                sc.copy('pool', KT[1][64:96, tl], KT[0][64:96, tl], ['KT0pe'], ['KT1pe'])
        for r in range(3):
            sc.mm(PS(5), onesb, sq[:, r, :], r == 0, r == 2, ['onesb', 'sq%d' % r], [pk(5)])
        for r in range(2):
            sc.mm(PS(6), onesb, sq[:, 3 + r, :], r == 0, r == 1, ['onesb', 'sq%d' % (3 + r)], [pk(6)])
        sc.act(rq, PS(5), AF.Sqrt, [pk(5), 'eps'], ['rq'], bias=eps_rms, scale=1.0 / 384.0)
        sc.recip(rq, rq, ['rq'], ['rq'])
        sc.act(rk, PS(6), AF.Sqrt, [pk(6), 'eps'], ['rk'], bias=eps_rms, scale=1.0 / 256.0)
        sc.recip(rk, rk, ['rk'], ['rk'])
        for r in range(3):
            sc.tt('dve', cqn[:, r, tl], cqg[:, r, :], rq, ALU.mult, ['cqg%d' % r, 'rq'], ['cqn'])
        for r in range(2):
            sc.tt('dve', ckvn[:, r, tl], ckg[:, r, :], rk, ALU.mult, ['ckg%d' % r, 'rk'], ['ckvn'])
        for c in range(4):
            for k in range(8):
                sc.mm(PS(7)[:, c * 128:(c + 1) * 128], xT[:, k, c * 128:(c + 1) * 128], w0[:, k, 1664:1792],
                      k == 0, k == 7, ['xT', 'w0'], [pk(7)])
        sc.copy('act', va[:, i * 4:(i + 1) * 4, :, 0:64], PS(7).rearrange("p (c g d) -> p c g d", c=4, g=2),
                [pk(7)], ['va'])
    sc.barrier()
    ar.release(mA)
    if stop_after == 'L0A':
        sc.finalize()
        return nc

    mW = ar.mark()
    bhi = ar.alloc([8, 384], BF16)
    blo = ar.alloc([8, 384], BF16)
    pT = [ar.alloc([384], BF16) for _ in range(4)]
    rd = ar.alloc([512], F32)
    mixt = ar.alloc([512], BF16)
    sc.dma('sp', bhi, biasw_hi_d, (), ['biasw'])
    sc.dma('sp', blo, biasw_lo_d, (), ['biasw'])
    scale_a = 64.0 ** -0.5
    for h in range(8):
        g = h // 4
        f = h // 2
        rbs = (h % 2) * 64
        pr = slice(rbs, rbs + 64)

        def q0_of(j):
            return max(0, (j - 1) * 128)

        def s_step(j):
            q0 = q0_of(j)
            q1 = min(S, (j + 2) * 128)
            n = q1 - q0
            off = q0 - (j - 1) * 128
            bank = j % 3
            sc.mm(PS(bank)[:, 0:n], identb, bhi[:, h, off:off + n], True, False, ['identb', 'biasw'], [pk(bank)])
            sc.mm(PS(bank)[:, 0:n], identb, blo[:, h, off:off + n], False, False, ['identb', 'biasw'], [pk(bank)])
            sc.mm(PS(bank)[:, 0:n], kaT2[pr, g, j * 128:(j + 1) * 128], qaT[pr, f, q0:q1], False, True,
                  ['kaT2', 'qaT'], [pk(bank)])
            sc.act(pT[j % 4][:, 0:n], PS(bank)[:, 0:n], AF.Exp, [pk(bank)], ['pT%d' % (j % 4)], scale=scale_a)

        def pv_step(i):
            pob = 3 + ((i // 4) % 2)
            js = [jj for jj in (i - 1, i, i + 1) if 0 <= jj < NT]
            for n_, jj in enumerate(js):
                c0 = i * 128 - q0_of(jj)
                sc.mm(PS(pob)[:, (i % 4) * 128:(i % 4 + 1) * 128], va[:, jj, g, :], pT[jj % 4][:, c0:c0 + 128],
                      n_ == 0, n_ == len(js) - 1, ['va', 'va_ones', 'pT%d' % (jj % 4)], [pk(pob)])
            if i % 4 == 3:
                t0 = (i // 4) * 512
                attn_norm_store(pob, 512, esink[64:128, h:h + 1], mixd[f, pr, t0:t0 + 512], rd, mixt)

        s_step(0)
        s_step(1)
        for j in range(2, NT):
            s_step(j)
            pv_step(j - 2)
        pv_step(NT - 2)
        pv_step(NT - 1)
    sc.barrier()
    ar.release(mW)
    if stop_after == 'L0W':
        sc.finalize()
        return nc

    mM = ar.mark()
    wq = ar.alloc([3, 768], BF16)
    wqs = ar.alloc([3, 768], BF16)
    wkv = ar.alloc([2, 1024], BF16)
    QT = [ar.alloc([S], BF16) for _ in range(2)]
    vm = [ar.alloc([NT, 128], BF16) for _ in range(2)]
    csq = [ar.alloc([512], F32) for _ in range(2)]
    snq = [ar.alloc([512], F32) for _ in range(2)]
    u1 = ar.alloc([512], F32)
    u2 = ar.alloc([512], F32)
    pbuf = [ar.alloc([512], BF16) for _ in range(4)]
    rd = ar.alloc([512], F32)
    mixt = ar.alloc([512], BF16)
    sc.dma('pool', wq, wq_d.rearrange("(k p) f -> p k f", p=128), (), ['wq'])
    sc.dma('pool', wqs, wqs_d.rearrange("(k p) f -> p k f", p=128), (), ['wqs'])
    sc.dma('pool', wkv, wkv_d.rearrange("(k p) f -> p k f", p=128), (), ['wkv'])
    for b in range(2):
        sc.memset('pool', vm[b][:, :, 64:128], 1.0, ['vm_ones%d' % b])
    scale_b = 96.0 ** -0.5

    def mla_proj(h):
        b = h % 2
        for i in range(8):
            tl = slice(i * 512, (i + 1) * 512)
            cb = i % 2
            sc.dma('sp', csq[cb][64:96, :], cos_d[:, tl], (), ['csq%d' % cb])
            sc.dma('sp', snq[cb][64:96, :], sin_d[:, tl], (), ['snq%d' % cb])
            for r in range(3):
                sc.mm(PS(5)[0:96, :], wq[:, r, h * 96:(h + 1) * 96], cqn[:, r, tl], r == 0, r == 2, ['wq', 'cqn'], [pk(5)])
            for r in range(3):
                sc.mm(PS(6)[0:96, :], wqs[:, r, h * 96:(h + 1) * 96], cqn[:, r, tl], r == 0, r == 2, ['wqs', 'cqn'], [pk(6)])
            sc.copy('act', QT[b][0:64, tl], PS(5)[0:64, :], [pk(5)], ['QTn%d' % b, 'psord'])
            sc.tt('dve', u1[64:96, :], PS(5)[64:96, :], csq[cb][64:96, :], ALU.mult, [pk(5), 'csq%d' % cb, 'psord'], ['u1'])
            sc.tt('dve', u2[64:96, :], PS(6)[64:96, :], snq[cb][64:96, :], ALU.mult, [pk(6), 'snq%d' % cb], ['u2'])
            sc.tt('pool', QT[b][64:96, tl], u1[64:96, :], u2[64:96, :], ALU.add, ['u1', 'u2'], ['QTp%d' % b])
            for c in range(2):
                sc.mm(PS(7)[0:64, :], wkv[:, c, h * 128:h * 128 + 64], ckvn[:, c, tl], c == 0, c == 1, ['wkv', 'ckvn'], [pk(7)])
            sc.copy('act', KT[b][0:64, tl], PS(7)[0:64, :], [pk(7)], ['KTn%d' % b])
        for g4 in range(4):
            for t8 in range(8):
                tc = g4 * 8 + t8
                for c in range(2):
                    sc.mm(PS(7)[:, t8 * 64:(t8 + 1) * 64], ckvn[:, c, tc * 128:(tc + 1) * 128],
                          wkv[:, c, h * 128 + 64:h * 128 + 128], c == 0, c == 1, ['ckvn', 'wkv'], [pk(7)])
            sc.copy('dve', vm[b][:, g4 * 8:(g4 + 1) * 8, 0:64], PS(7).rearrange("p (t d) -> p t d", t=8), [pk(7)], ['vm%d' % b])

    def mla_attn(h):
        b = h % 2
        f = 4 + h // 2
        rbs = (h % 2) * 64
        kkeys = ['KTn%d' % b, 'KT%dpe' % b]
        qkeys = ['QTn%d' % b, 'QTp%d' % b]
        cnt = 0
        for i in range(8):
            tl = slice(i * 512, (i + 1) * 512)
            pob = 3 + (i % 2)

            def s_step(kc, slot):
                bank = slot % 3
                sc.mm(PS(bank), KT[b][0:96, kc * 128:(kc + 1) * 128], QT[b][0:96, tl], True, True, kkeys + qkeys, [pk(bank)])
                sc.act(pbuf[slot % 4], PS(bank), AF.Exp, [pk(bank)], ['pb%d' % (slot % 4)], scale=scale_b)

            def pv_step(kc, slot):
                sc.mm(PS(pob), vm[b][:, kc, :], pbuf[slot % 4], kc == 0, kc == NT - 1,
                      ['vm%d' % b, 'vm_ones%d' % b, 'pb%d' % (slot % 4)], [pk(pob)])

            s_step(0, cnt)
            s_step(1, cnt + 1)
            for kc in range(NT):
                if kc + 2 < NT:
                    s_step(kc + 2, cnt + kc + 2)
                pv_step(kc, cnt + kc)
            cnt += NT
            attn_norm_store(pob, 512, None, mixd[f, rbs:rbs + 64, tl], rd, mixt)

    mla_proj(0)
    for h in range(8):
        if h + 1 < 8:
            mla_proj(h + 1)
        mla_attn(h)
    sc.barrier()
    ar.release(base_mark)

    if debug:
        mD = ar.mark()
        mtb = ar.alloc([S], BF16)
        mtf = ar.alloc([S], F32)
        for f in range(8):
            sc.dma('sp', mtb, mixd[f], ['mixd'], ['mtb'])
            sc.copy('dve', mtf, mtb, ['mtb'], ['mtf'])
            sc.dma('sp', dbg["dbg_mix"][f], mtf, ['mtf'], ['dbg_mix'])
        sc.barrier()
        ar.release(mD)

    def resid0_load(tc, xcb, xck):
        sc.dma('sp', xcb, x[tc * 128:(tc + 1) * 128, :], (), [xck])

    def resid0(tc, pa, pb_, rbuf, rbk, xcb, xck):
        for half, bank in ((0, pa), (1, pb_)):
            sc.stt('dve', rbuf[:, half * 512:(half + 1) * 512], xcb[:, half * 512:(half + 1) * 512], ALPHA, PS(bank),
                   ALU.mult, ALU.add, [xck, pk(bank)], [rbk])

    affT, _ = outproj_ln_phase(0, wo_d[0], "ln0a_g", "ln0a_b", resid0_load, resid0)
    if stop_after == 'L0O':
        sc.finalize()
        return nc

    moe_phase(0, affT)
    ar.release(base_mark)
    if debug:
        mD = ar.mark()
        yb = ar.alloc([D], F32)
        for tc in range(NT):
            sc.dma('sp', yb, Yd[tc * 128:(tc + 1) * 128, :], ['Yd'], ['yb'])
            sc.dma('sp', dbg["dbg_y"][tc * 128:(tc + 1) * 128, :], yb, ['yb'], ['dbg_y'])
        sc.barrier()
        ar.release(mD)
    if stop_after == 'MOE0':
        sc.finalize()
        return nc

    h2T = ar.alloc([8, S], BF16)
    mL1 = ar.mark()
    gB = ar.alloc([D], F32)
    bB = ar.alloc([D], F32)
    yc = [ar.alloc([D], F32) for _ in range(2)]
    hbuf = [ar.alloc([D], F32) for _ in range(2)]
    ahb = [ar.alloc([D], F32) for _ in range(2)]
    hbf = [ar.alloc([D], BF16) for _ in range(2)]
    st = [ar.alloc([2, 6], F32) for _ in range(2)]
    mv = [ar.alloc([2], F32) for _ in range(2)]
    sd = [ar.alloc([1], F32) for _ in range(2)]
    load_bcast(gB, "ln0b_g", 'lng')
    load_bcast(bB, "ln0b_b", 'lnb')
    sc.dma('sp', yc[0], Yd[0:128, :], ['Yd'], ['yc0'])
    sc.dma('sp', yc[1], Yd[128:256, :], ['Yd'], ['yc1'])
    ln_stats(yc[0], 'yc0', st[0], mv[0], sd[0], '0')
    for tc in range(NT):
        b = tc % 2
        sf = str(b)
        rows = slice(tc * 128, (tc + 1) * 128)
        if tc + 1 < NT:
            ln_stats(yc[1 - b], 'yc%d' % (1 - b), st[1 - b], mv[1 - b], sd[1 - b], str(1 - b))
        ln_apply(yc[b], 'yc%d' % b, gB, bB, hbuf[b], 'hbuf' + sf, mv[b], sd[b], sf)
        if tc + 2 < NT:
            sc.dma('sp', yc[b], Yd[(tc + 2) * 128:(tc + 3) * 128, :], ['Yd'], ['yc%d' % b])
        sc.act(ahb[b], hbuf[b], AF.Copy, ['hbuf' + sf], ['ahb' + sf], scale=ALPHA)
        sc.dma('sp', R1[rows, :], ahb[b], ['ahb' + sf], ['R1'])
        sc.copy('act', hbf[b], hbuf[b], ['hbuf' + sf], ['hbf' + sf])
        tb = tc % 2
        for k in range(8):
            sc.tr(PSB(tb)[:, k * 128:(k + 1) * 128], hbf[b][:, k * 128:(k + 1) * 128], identb, ['hbf' + sf, 'identb'], [pk(tb)])
        sc.copy('act', h2T[:, :, rows], PSB(tb).rearrange("p (k t) -> p k t", k=8), [pk(tb)], ['h2T'])
    sc.barrier()
    ar.release(mL1)

    wp = [ar.alloc([8, 384], BF16) for _ in range(2)]
    QTp = [ar.alloc([S], BF16) for _ in range(2)]
    KTp = [ar.alloc([S], BF16) for _ in range(2)]
    Vp = [ar.alloc([NT, 2, 128], BF16) for _ in range(2)]
    TTp = [ar.alloc([2, 16, 64], F32) for _ in range(2)]
    TTb = [ar.alloc([2, 16, 64], BF16) for _ in range(2)]
    namask = ar.alloc([16, 64], F32)
    tbn = [ar.alloc([5, 64], F32) for _ in range(4)]
    pTn = [ar.alloc([5, 64], BF16) for _ in range(5)]
    rd = ar.alloc([512], F32)
    mixt = ar.alloc([512], BF16)
    sc.dma('sp', namask, namask_d, (), ['namask'])
    for b in range(2):
        sc.memset('pool', Vp[b][:, :, :, 64:128], 1.0, ['vp_ones%d' % b])
    scale_c = 64.0 ** -0.5

    def na_proj(hp):
        b = hp % 2
        for part in range(3):
            sc.dma('pool', wp[b][:, :, part * 128:(part + 1) * 128],
                   wqkv_d[:, part * 1024 + hp * 128: part * 1024 + (hp + 1) * 128].rearrange("(k p) f -> p k f", p=128),
                   (), ['wp%d' % b])
        for g in range(2):
            sc.dma('sp', TTp[b][:, g], rpbT_d[hp * 2 + g], (), ['TT%d_%d' % (b, g)])
            sc.tt('pool', TTp[b][:, g], TTp[b][:, g], namask, ALU.add, ['TT%d_%d' % (b, g), 'namask'], ['TT%d_%d' % (b, g)])
            sc.ts('pool', TTb[b][:, g], TTp[b][:, g], 1.0 / scale_c, None, ALU.mult, None, ['TT%d_%d' % (b, g)], ['TTb%d_%d' % (b, g)])
        for i in range(8):
            tl = slice(i * 512, (i + 1) * 512)
            for k in range(8):
                sc.mm(PS(5), wp[b][:, k, 0:128], h2T[:, k, tl], k == 0, k == 7, ['wp%d' % b, 'h2T'], [pk(5)])
            sc.copy('act', QTp[b][:, tl], PS(5), [pk(5)], ['QTp%d' % b])
            for k in range(8):
                sc.mm(PS(6), wp[b][:, k, 128:256], h2T[:, k, tl], k == 0, k == 7, ['wp%d' % b, 'h2T'], [pk(6)])
            sc.copy('dve', KTp[b][:, tl], PS(6), [pk(6)], ['KTp%d' % b])
            for c in range(4):
                tc = i * 4 + c
                for k in range(8):
                    sc.mm(PS(5)[:, c * 128:(c + 1) * 128], h2T[:, k, tc * 128:(tc + 1) * 128], wp[b][:, k, 256:384],
                          k == 0, k == 7, ['h2T', 'wp%d' % b], [pk(5)])
            sc.copy('act', Vp[b][:, i * 4:(i + 1) * 4, :, 0:64], PS(5).rearrange("p (c g d) -> p c g d", c=4, g=2),
                    [pk(5)], ['Vp%d' % b])

    def na_attn(hp):
        b = hp % 2
        tasks = [(g, r) for g in range(2) for r in range(64)]

        def geom(r):
            rs = min(max(r - 4, 0), 56)
            odd = rs % 2
            kr0 = rs - odd
            nch = 5 if odd else 4
            return odd, kr0, nch, kr0 - r + 8

        def s_part(t):
            g, r = tasks[t]
            pr = slice(g * 64, g * 64 + 64)
            odd, kr0, nch, u0 = geom(r)
            bank = (0, 1, 2, 7)[t % 4]
            sc.mm(PS(bank)[:, 0:nch * 64].rearrange("p (c q) -> p c q", c=nch), identb,
                  TTb[b][:, g, u0:u0 + 2 * nch - 1:2, :], True, False, ['identb', 'TTb%d_%d' % (b, g)], [pk(bank)])
            for c in range(nch):
                kc = kr0 // 2 + c
                sc.mm(PS(bank)[:, c * 64:(c + 1) * 64], KTp[b][pr, kc * 128:(kc + 1) * 128], QTp[b][pr, r * 64:(r + 1) * 64],
                      False, c == nch - 1, ['KTp%d' % b, 'QTp%d' % b], [pk(bank)])
            sc.act(pTn[t % 5][:, 0:nch, :], PS(bank)[:, 0:nch * 64].rearrange("p (c q) -> p c q", c=nch), AF.Exp,
                   [pk(bank)], ['pTn%d' % (t % 5)], scale=scale_c)

        def pv_part(t):
            g, r = tasks[t]
            pr = slice(g * 64, g * 64 + 64)
            odd, kr0, nch, u0 = geom(r)
            pob = 3 + ((r // 8) % 2)
            pkk = 'pTn%d' % (t % 5)
            for c in range(nch):
                kc = kr0 // 2 + c
                if odd and c == 0:
                    ps_ = slice(64, 128)
                elif odd and c == nch - 1:
                    ps_ = slice(0, 64)
                else:
                    ps_ = slice(0, 128)
                sc.mm(PS(pob)[:, (r % 8) * 64:(r % 8 + 1) * 64], Vp[b][ps_, kc, g, :], pTn[t % 5][ps_, c, :],
                      c == 0, c == nch - 1, ['Vp%d' % b, 'vp_ones%d' % b, pkk], [pk(pob)])
            if r % 8 == 7:
                t0 = (r // 8) * 512
                attn_norm_store(pob, 512, None, mixd[hp, pr, t0:t0 + 512], rd, mixt)

        LA = 3
        for t in range(min(LA, len(tasks))):
            s_part(t)
        for t in range(len(tasks)):
            if t + LA < len(tasks):
                s_part(t + LA)
            pv_part(t)

    na_proj(0)
    for hp in range(8):
        if hp + 1 < 8:
            na_proj(hp + 1)
        na_attn(hp)
    sc.barrier()
    ar.release(base_mark)

    def resid1_load(tc, xcb, xck):
        sc.dma('sp', xcb, R1[tc * 128:(tc + 1) * 128, :], ['R1'], [xck])

    def resid1(tc, pa, pb_, rbuf, rbk, xcb, xck):
        for half, bank in ((0, pa), (1, pb_)):
            sc.tt('dve', rbuf[:, half * 512:(half + 1) * 512], xcb[:, half * 512:(half + 1) * 512], PS(bank),
                  ALU.add, [xck, pk(bank)], [rbk])

    affT, _ = outproj_ln_phase(1, wo_d[1], "ln1a_g", "ln1a_b", resid1_load, resid1)
    moe_phase(1, affT)
    ar.release(base_mark)

    gB = ar.alloc([D], F32)
    bB = ar.alloc([D], F32)
    yc = [ar.alloc([D], F32) for _ in range(2)]
    hb2 = [ar.alloc([D], F32) for _ in range(2)]
    st = [ar.alloc([2, 6], F32) for _ in range(2)]
    mv = [ar.alloc([2], F32) for _ in range(2)]
    sd = [ar.alloc([1], F32) for _ in range(2)]
    load_bcast(gB, "ln1b_g", 'lng')
    load_bcast(bB, "ln1b_b", 'lnb')
    sc.dma('sp', yc[0], Yd[0:128, :], ['Yd'], ['yc0'])
    sc.dma('sp', yc[1], Yd[128:256, :], ['Yd'], ['yc1'])
    ln_stats(yc[0], 'yc0', st[0], mv[0], sd[0], '0')
    for tc in range(NT):
        b = tc % 2
        rows = slice(tc * 128, (tc + 1) * 128)
        if tc + 1 < NT:
            ln_stats(yc[1 - b], 'yc%d' % (1 - b), st[1 - b], mv[1 - b], sd[1 - b], str(1 - b))
        ln_apply(yc[b], 'yc%d' % b, gB, bB, hb2[b], 'hb%d' % b, mv[b], sd[b], str(b))
        if tc + 2 < NT:
            sc.dma('sp', yc[b], Yd[(tc + 2) * 128:(tc + 3) * 128, :], ['Yd'], ['yc%d' % b])
        sc.dma('sp', out_d[rows, :], hb2[b], ['hb%d' % b], ['out'])
    sc.finalize()
    return nc


def _consts():
    c = {}
    c["ident_bf"] = np.eye(128, dtype=np.float32).astype(ml_dtypes.bfloat16)
    c["ident_f"] = np.eye(128, dtype=np.float32)
    half = 16
    inv = (10000.0 ** (-np.arange(half, dtype=np.float32) / half)).astype(np.float32)
    ang = np.arange(S, dtype=np.float32)[None, :] * inv[:, None]
    cos = np.cos(ang).astype(np.float32)
    sin = np.sin(ang).astype(np.float32)
    c["cos_t"] = np.concatenate([cos, cos], 0)
    c["sin_t"] = np.concatenate([-sin, sin], 0)
    k = np.arange(128)[:, None]
    qp = np.arange(384)[None, :]
    dist = np.abs(qp - 128 - k).astype(np.float32)
    slopes = 2.0 ** (-8.0 * (np.arange(8, dtype=np.float32) + 1.0) / 8)
    bw = np.where(dist[:, None, :] <= 128, -slopes[None, :, None] * dist[:, None, :], NEG).astype(np.float32)
    bws = (bw.astype(np.float64) / (64.0 ** -0.5)).astype(np.float32)
    hi = bws.astype(ml_dtypes.bfloat16)
    lo = (bws - hi.astype(np.float32)).astype(ml_dtypes.bfloat16)
    c["biasw_hi"] = np.ascontiguousarray(hi)
    c["biasw_lo"] = np.ascontiguousarray(lo)
    cols = np.arange(64)
    cstart = np.clip(cols - 8, 0, 48)
    valid = (cols[None, :] >= cstart[:, None]) & (cols[None, :] < cstart[:, None] + 16)
    m = np.where(valid.T, 0.0, NEG).astype(np.float32)
    m2 = np.concatenate([m, m], 0)
    c["namask"] = np.ascontiguousarray(np.broadcast_to(m2[:, None, :], (128, 16, 64)))
    c["cvals"] = (np.arange(4)[None, :] * 128 + np.arange(128)[:, None]).astype(np.float32)
    return c


def _prep_shared(inp):
    f32 = np.float32
    d = {}
    w_in0 = np.asarray(inp["w_in0"], f32)
    qa = w_in0[:, 0:512]
    ka0, ka1 = w_in0[:, 512:576], w_in0[:, 576:640]
    vaw = w_in0[:, 640:768]
    cq = w_in0[:, 768:1152]
    ckv = w_in0[:, 1152:1408]
    kr = w_in0[:, 1408:1440]
    krs = np.concatenate([kr[:, 16:32], kr[:, 0:16]], 1)
    d["w0f"] = np.ascontiguousarray(np.concatenate(
        [qa, ka0, ka0, ka1, ka1, cq, ckv, kr, kr, kr, kr, krs, krs, krs, krs], 1))
    d["w0v"] = np.ascontiguousarray(vaw)
    d["a_sink"] = np.asarray(inp["a_sink"], f32).reshape(1, 8)
    d["gq"] = np.ascontiguousarray(np.asarray(inp["mla_q_norm"], f32).reshape(3, 128).T)
    d["gkv"] = np.ascontiguousarray(np.asarray(inp["mla_kv_norm"], f32).reshape(2, 128).T)
    wq = np.asarray(inp["w_q_up"], f32)
    d["wq"] = wq
    wq3 = wq.reshape(384, 8, 96)
    d["wqs"] = np.ascontiguousarray(
        np.concatenate([wq3[:, :, 0:64], wq3[:, :, 80:96], wq3[:, :, 64:80]], 2).reshape(384, 768))
    d["wkv"] = np.asarray(inp["w_kv_up"], f32)
    d["wo0"] = np.asarray(inp["w_out0"], f32)
    d["wo1"] = np.asarray(inp["w_out1"], f32)
    for n in ["ln0a_g", "ln0a_b", "ln0b_g", "ln0b_b", "ln1a_g", "ln1a_b", "ln1b_g", "ln1b_b"]:
        d[n] = np.asarray(inp[n], f32).reshape(1, D)
    for n in ["router0", "router1", "w_gate0", "w_gate1", "w_up0", "w_up1", "w_down0", "w_down1", "w_qkv1"]:
        d[n] = np.asarray(inp[n], f32)
    rpb = np.asarray(inp["na_rpb"], f32)
    cols = np.arange(64)
    dc = np.clip(cols[None, :] - cols[:, None] + 15, 0, 30)
    u = np.arange(16)
    out = np.empty((16, 2, 64, 16, 64), f32)
    for kr2 in range(2):
        dr = np.clip(u + kr2 - 8, -7, 7) + 7
        g_ = rpb[:, dr[:, None, None], dc[None, :, :]]
        out[:, kr2] = np.transpose(g_, (0, 3, 1, 2))
    d["rpbT"] = np.ascontiguousarray(out.reshape(16, 128, 16, 64))
    d.update(_consts())
    return d


_CACHE = {}


def kernel(**inputs):
    x = np.asarray(inputs["x"], np.float32)
    shared = _prep_shared(inputs)
    if "nc" not in _CACHE:
        _CACHE["nc"] = build_program()
    nc = _CACHE["nc"]
    in_maps = []
    for c in range(N_CORES):
        m = dict(shared)
        m["x"] = np.ascontiguousarray(x[c])
        in_maps.append(m)
    res = run_bass_kernel_spmd(nc, in_maps, core_ids=list(range(N_CORES)))
    return np.stack([np.asarray(r["out"], np.float32) for r in res.results], 0)
```

```python
import math
import numpy as np
import ml_dtypes
import concourse.bass as bass
import concourse.mybir as mybir
from concourse.bass_utils import run_bass_kernel_spmd

F32 = mybir.dt.float32
BF16 = mybir.dt.bfloat16
F16 = mybir.dt.float16
I32 = mybir.dt.int32
U8 = mybir.dt.uint8
ALU = mybir.AluOpType
AF = mybir.ActivationFunctionType
DSZ = {F32: 4, BF16: 2, F16: 2, I32: 4, U8: 1}

S = 4096
D = 1024
NT = 32
NEG = -1.0e30
ALPHA = 4.0 ** 0.25
N_CORES = 8
ENGS = ['pe', 'act', 'dve', 'pool', 'sp']
N_DSEM = 48


class Sched:
    def __init__(self, nc):
        self.nc = nc
        self.ops = []

    def add(self, eng, fn, r=(), w=(), dma=False):
        self.ops.append(dict(eng=eng, fn=fn, r=list(r), w=list(w), dma=dma, sig=dma, bar=False))

    def barrier(self):
        self.ops.append(dict(eng=None, bar=True))

    def mm(self, out, lhsT, rhs, start, stop, r, w):
        self.add('pe', lambda e: e.matmul(out, lhsT, rhs, start=start, stop=stop), r, w)

    def tr(self, out, in_, ident, r, w):
        self.add('pe', lambda e: e.transpose(out, in_, ident), r, w)

    def act(self, out, in_, func, r, w, bias=None, scale=None, accum=None):
        kw = {}
        if bias is not None:
            kw['bias'] = bias
        if scale is not None:
            kw['scale'] = scale
        if accum is not None:
            kw['accum_out'] = accum
        self.add('act', lambda e: e.activation(out, in_, func, **kw), r, w)

    def ts(self, eng, out, in0, s1, s2, op0, op1, r, w, accum=None):
        if accum is not None:
            self.add(eng, lambda e: e.tensor_scalar(out, in0, s1, s2, op0, op1, accum_out=accum), r, w)
        elif op1 is None:
            self.add(eng, lambda e: e.tensor_scalar(out, in0, s1, None, op0), r, w)
        else:
            self.add(eng, lambda e: e.tensor_scalar(out, in0, s1, s2, op0, op1), r, w)

    def tt(self, eng, out, in0, in1, op, r, w):
        self.add(eng, lambda e: e.tensor_tensor(out, in0, in1, op), r, w)

    def stt(self, eng, out, in0, scalar, in1, op0, op1, r, w):
        self.add(eng, lambda e: e.scalar_tensor_tensor(out, in0, scalar, in1, op0, op1), r, w)

    def copy(self, eng, out, in_, r, w):
        if eng == 'act':
            self.add('act', lambda e: e.copy(out, in_), r, w)
        else:
            self.add(eng, lambda e: e.tensor_copy(out, in_), r, w)

    def recip(self, out, in_, r, w):
        self.add('dve', lambda e: e.reciprocal(out, in_), r, w)

    def memset(self, eng, ap, val, w):
        self.add(eng, lambda e: e.memset(ap, val), (), w)

    def dma(self, eng, out, in_, r, w, **kw):
        self.add(eng, lambda e: e.dma_start(out=out, in_=in_, **kw), r, w, dma=True)

    def idma(self, out, out_off, in_, in_off, r, w, **kw):
        self.add('pool', lambda e: e.indirect_dma_start(out=out, out_offset=out_off, in_=in_,
                                                        in_offset=in_off, **kw), r, w, dma=True)

    def finalize(self):
        nc = self.nc
        ops = self.ops
        last_w, readers = {}, {}
        eng_last = {e: None for e in ENGS}
        outstanding = []
        for i, op in enumerate(ops):
            if op['bar']:
                op['deps'] = [v for v in eng_last.values() if v is not None] + outstanding
                outstanding = []
                last_w.clear()
                readers.clear()
                for d in op['deps']:
                    ops[d]['sig'] = True
                continue
            deps = {}
            for k in op['r']:
                if k in last_w:
                    deps[last_w[k]] = 'raw'
            for k in op['w']:
                if k in last_w:
                    deps.setdefault(last_w[k], 'waw')
                for rr in readers.get(k, ()):
                    deps.setdefault(rr, 'war')
            deps.pop(i, None)
            keep = []
            for d, kind in deps.items():
                p = ops[d]
                if p['eng'] == op['eng'] and not p['dma']:
                    if op['eng'] == 'pe':
                        continue
                keep.append(d)
            op['deps'] = keep
            for d in keep:
                ops[d]['sig'] = True
            for k in op['r']:
                lst = readers.setdefault(k, [])
                if not op['dma']:
                    lst[:] = [q for q in lst if ops[q]['dma'] or ops[q]['eng'] != op['eng']]
                lst.append(i)
            for k in op['w']:
                last_w[k] = i
                readers[k] = []
            if op['dma']:
                outstanding.append(i)
            else:
                eng_last[op['eng']] = i
        cnt = {e: 0 for e in ENGS}
        dcnt = [0] * N_DSEM
        ndq = [0, 0]
        for op in ops:
            if op['bar']:
                continue
            if op['dma']:
                half = N_DSEM // 2
                qi = 0 if op['eng'] == 'sp' else 1
                d = qi * half + (ndq[qi] % half)
                ndq[qi] += 1
                op['dsem'] = d
                op['dprev'] = dcnt[d]
                dcnt[d] += 16
                op['dval'] = dcnt[d]
            elif op['sig']:
                cnt[op['eng']] += 1
                op['sval'] = cnt[op['eng']]
        import contextlib
        with contextlib.ExitStack() as st:
            esem = {e: st.enter_context(nc.semaphore("s_" + e)) for e in ENGS}
            dsem = [st.enter_context(nc.semaphore("d_%d" % i)) for i in range(N_DSEM)]
            block = st.enter_context(nc.Block())

            def waits_for(deps):
                need = {}
                for d in deps:
                    p = ops[d]
                    if p['dma']:
                        key, val = ('d', p['dsem']), p['dval']
                    else:
                        key, val = ('e', p['eng']), p['sval']
                    if need.get(key, 0) < val:
                        need[key] = val
                return need

            def emit(eng, e):
                seen = {}

                def do_waits(need):
                    for key, val in need.items():
                        if seen.get(key, 0) >= val:
                            continue
                        seen[key] = val
                        sem = dsem[key[1]] if key[0] == 'd' else esem[key[1]]
                        e.wait_ge(sem, val)

                for op in ops:
                    if op['bar']:
                        do_waits(waits_for(op['deps']))
                        continue
                    if op['eng'] != eng:
                        continue
                    need = waits_for(op['deps'])
                    if op['dma'] and op['dprev'] > 0:
                        key = ('d', op['dsem'])
                        if need.get(key, 0) < op['dprev']:
                            need[key] = op['dprev']
                    do_waits(need)
                    ins = op['fn'](e)
                    if op['dma']:
                        ins.then_inc(dsem[op['dsem']], 16)
                    elif op['sig']:
                        ins.then_inc(esem[eng], 1)
                if eng == 'sp':
                    for d in range(N_DSEM):
                        if dcnt[d] > 0 and seen.get(('d', d), 0) < dcnt[d]:
                            e.wait_ge(dsem[d], dcnt[d])
                    for en in ENGS:
                        if cnt[en] > 0 and seen.get(('e', en), 0) < cnt[en]:
                            e.wait_ge(esem[en], cnt[en])

            @block.tensor
            def _(e):
                emit('pe', e)

            @block.scalar
            def _(e):
                emit('act', e)

            @block.vector
            def _(e):
                emit('dve', e)

            @block.gpsimd
            def _(e):
                emit('pool', e)

            @block.sync
            def _(e):
                emit('sp', e)


class Arena:
    def __init__(self, nc, nbytes):
        self.t = nc.alloc_sbuf_tensor("arena", [128, nbytes], U8)
        self.cap = nbytes
        self.off = 0

    def alloc(self, shape, dtype):
        n = int(np.prod(shape)) * DSZ[dtype]
        n_al = (n + 63) // 64 * 64
        assert self.off + n_al <= self.cap, ("arena overflow", self.off, n_al, self.cap)
        ap = self.t[:, self.off:self.off + n].bitcast(dtype)
        self.off += n_al
        if len(shape) == 2:
            ap = ap.rearrange("p (a b) -> p a b", a=shape[0])
        elif len(shape) == 3:
            ap = ap.rearrange("p (a b c) -> p a b c", a=shape[0], b=shape[1])
        return ap

    def view(self, off, shape, dtype):
        n = int(np.prod(shape)) * DSZ[dtype]
        assert off + n <= self.cap
        ap = self.t[:, off:off + n].bitcast(dtype)
        if len(shape) == 2:
            ap = ap.rearrange("p (a b) -> p a b", a=shape[0])
        return ap

    def mark(self):
        return self.off

    def release(self, m):
        self.off = m


def build_program(stop_after=None, debug=False):
    nc = bass.Bass("TRN2", target_bir_lowering=False)
    sc = Sched(nc)

    def din(name, shape, dt=F32):
        return nc.dram_tensor(name, list(shape), dt, kind="ExternalInput").ap()

    def dscr(name, shape, dt):
        return nc.dram_tensor(name, list(shape), dt, kind="Internal").ap()

    x = din("x", [S, D])
    w0f = din("w0f", [D, 1664])
    w0v = din("w0v", [D, 128])
    a_sink = din("a_sink", [1, 8])
    gq_d = din("gq", [128, 3])
    gkv_d = din("gkv", [128, 2])
    wq_d = din("wq", [384, 768])
    wqs_d = din("wqs", [384, 768])
    wkv_d = din("wkv", [256, 1024])
    wo_d = [din("wo0", [D, D]), din("wo1", [D, D])]
    lnp = {n: din(n, [1, D]) for n in
           ["ln0a_g", "ln0a_b", "ln0b_g", "ln0b_b", "ln1a_g", "ln1a_b", "ln1b_g", "ln1b_b"]}
    router_d = [din("router0", [D, 16]), din("router1", [D, 16])]
    NE_ = 1 if stop_after in ('INIT', 'L0A', 'L0W', 'L0M', 'L0O') else 16
    wg_d = [din("w_gate0", [NE_, D, 2048]), din("w_gate1", [NE_, D, 2048])]
    wu_d = [din("w_up0", [NE_, D, 2048]), din("w_up1", [NE_, D, 2048])]
    wd_d = [din("w_down0", [NE_, 2048, D]), din("w_down1", [NE_, 2048, D])]
    wqkv_d = din("w_qkv1", [D, 3072])
    rpbT_d = din("rpbT", [16, 128, 16, 64])
    identb_d = din("ident_bf", [128, 128], BF16)
    identf_d = din("ident_f", [128, 128])
    cos_d = din("cos_t", [32, S])
    sin_d = din("sin_t", [32, S])
    biasw_hi_d = din("biasw_hi", [128, 8, 384], BF16)
    biasw_lo_d = din("biasw_lo", [128, 8, 384], BF16)
    namask_d = din("namask", [128, 16, 64])
    cvals_d = din("cvals", [128, 4])
    gmat_d = din("gmat", [128, 128])
    lmat_d = din("lmat", [128, 128])
    out_d = nc.dram_tensor("out", [S, D], F32, kind="ExternalOutput").ap()

    Hb = dscr("Hb", [S, D], BF16)
    Yd = dscr("Yd", [S, D], F32)
    R1 = dscr("R1", [S, D], F32)
    affd = dscr("affd", [S, 16], F32)
    cumd = dscr("cumd", [16, S], F16)
    affTd = dscr("affTd", [16, S], F32)
    mixd = dscr("mixd", [8, 128, S], BF16)
    dbg = {}
    if debug:
        for n, shp, dt in [("dbg_h1", [S, D], F32), ("dbg_mix", [8, 128, S], F32), ("dbg_aff", [S, 16], F32),
                           ("dbg_y", [S, D], F32), ("dbg_cum", [16, S], F32)]:
            dbg[n] = nc.dram_tensor(n, shp, dt, kind="ExternalOutput").ap()

    ar = Arena(nc, 200 * 1024)
    psb = [nc.alloc_psum_tensor("psb%d" % i, [128, 512], F32) for i in range(8)]

    def PS(b):
        return psb[b][:, :]

    def PSB(b):
        return psb[b][:, :].bitcast(BF16)

    def pk(b):
        return "ps%d" % b

    identb = ar.alloc([128], BF16)
    identf = ar.alloc([128], F32)
    onesb = ar.alloc([128], BF16)
    esink = ar.alloc([8], F32)
    cvals = ar.alloc([4], F32)
    eps_rms = ar.alloc([1], F32)
    eps_ln = ar.alloc([1], F32)
    afft = ar.alloc([NT, 16], F32)
    sc.dma('sp', identb, identb_d, (), ['identb'])
    sc.dma('sp', identf, identf_d, (), ['identf'])
    sc.dma('sp', cvals, cvals_d, (), ['cvals'])
    sc.dma('sp', esink, a_sink.partition_broadcast(128), (), ['esink'])
    sc.memset('pool', onesb, 1.0, ['onesb'])
    sc.memset('pool', eps_rms, 1e-6, ['eps'])
    sc.memset('pool', eps_ln, 1e-5, ['eps'])
    sc.act(esink, esink, AF.Exp, ['esink'], ['esink'])
    base_mark = ar.mark()
    if stop_after == 'INIT':
        sc.finalize()
        return nc

    def load_bcast(dst, name, key):
        sc.dma('sp', dst, lnp[name].partition_broadcast(128), (), [key])

    def ln_stats(src, srck, st, mv, sd, sfx=''):
        for hh in range(2):
            sc.add('dve', (lambda a, b: (lambda e: e.bn_stats(a, b)))(st[:, hh, :], src[:, hh * 512:(hh + 1) * 512]),
                   [srck], ['lnst' + sfx])
        sc.add('dve', lambda e: e.bn_aggr(mv, st.rearrange("p a b -> p (a b)")), ['lnst' + sfx], ['lnmv' + sfx])
        sc.act(sd, mv[:, 1:2], AF.Ln, ['lnmv' + sfx, 'eps'], ['lnsd' + sfx], bias=eps_ln)
        sc.act(sd, sd, AF.Exp, ['lnsd' + sfx], ['lnsd' + sfx], scale=-0.5)

    def ln_apply(src, srck, gB, bB, dst, dstk, mv, sd, sfx=''):
        sc.stt('dve', dst, src, mv[:, 0:1], gB, ALU.subtract, ALU.mult, [srck, 'lnmv' + sfx, 'lng'], [dstk])
        sc.stt('dve', dst, dst, sd[:, 0:1], bB, ALU.mult, ALU.add, [dstk, 'lnsd' + sfx, 'lnb'], [dstk])

    def ln_chunk(src, srck, gB, bB, dst, dstk, st, mv, sd, sfx='', pool_affine=True):
        ln_stats(src, srck, st, mv, sd, sfx)
        ln_apply(src, srck, gB, bB, dst, dstk, mv, sd, sfx)

    def post_h_moe(h, hk, tc, layer, bufs, sfx=''):
        ahb, hbf, hT, ex, ssum, rtr, affT = bufs
        rows = slice(tc * 128, (tc + 1) * 128)
        sc.act(ahb, h, AF.Copy, [hk], ['ahb' + sfx], scale=ALPHA)
        sc.dma('sp', Yd[rows, :], ahb, ['ahb' + sfx], ['Yd'])
        sc.copy('act', hbf, h, [hk], ['hbf' + sfx])
        sc.dma('sp', Hb[rows, :], hbf, ['hbf' + sfx], ['Hb'])
        tb = 4 + (tc % 2)
        for k in range(8):
            sc.tr(PSB(tb)[:, k * 128:(k + 1) * 128], hbf[:, k * 128:(k + 1) * 128], identb, ['hbf' + sfx, 'identb'], [pk(tb)])
        sc.copy('act', hT, PSB(tb).rearrange("p (k t) -> p k t", k=8), [pk(tb)], ['hT' + sfx])
        for k in range(8):
            sc.mm(PS(6)[:, 0:16], hT[:, k, :], rtr[:, k, :], k == 0, k == 7, ['hT' + sfx, 'rtr'], [pk(6)])
        sc.act(ex, PS(6)[:, 0:16], AF.Exp, [pk(6)], ['ex' + sfx, 'ssum' + sfx], accum=ssum)
        sc.recip(ssum, ssum, ['ssum' + sfx], ['ssum' + sfx])
        sc.ts('dve', afft[:, tc, :], ex, ssum[:, 0:1], None, ALU.mult, None, ['ex' + sfx, 'ssum' + sfx], ['afft%d' % tc])
        sc.tr(PS(7)[0:16, (tc % 4) * 128:(tc % 4 + 1) * 128], afft[:, tc, :], identf, ['afft%d' % tc, 'identf'], [pk(7)])
        if tc % 4 == 3:
            g4 = tc // 4
            sc.copy('act', affT[0:16, g4 * 512:(g4 + 1) * 512], PS(7)[0:16, :], [pk(7)], ['affT'])

    def moe_route(affT, rbufs):
        aff128, junk, ones1, mask, scan, cum, gmat, lmat, lo, mid, cntp, step, offs = rbufs
        sc.dma('sp', affTd, affT, ['affT'], ['affTd'])
        sc.dma('sp', aff128, affTd.rearrange("e (s c) -> (e s) c", s=8), ['affTd'], ['aff128'])
        sc.dma('sp', gmat, gmat_d, (), ['gmat'])
        sc.dma('sp', lmat, lmat_d, (), ['lmat'])
        sc.memset('pool', ones1, 1.0, ['ones1'])
        sc.memset('dve', lo, 0.0, ['lo'])
        for it in range(27):
            wk = 2.0 ** -(it + 1)
            pb = 5 + (it % 2)
            sc.ts('dve', mid, lo, wk, None, ALU.add, None, ['lo'], ['mid'])
            sc.ts('dve', junk, aff128, mid[:, 0:1], 0.0, ALU.is_ge, ALU.add, ['aff128', 'mid'], ['junk', 'cntp'], accum=cntp)
            sc.mm(PS(pb)[:, 0:1], gmat, cntp, True, True, ['gmat', 'cntp'], [pk(pb)])
            sc.ts('dve', step, PS(pb)[:, 0:1], 511.5, wk, ALU.is_ge, ALU.mult, [pk(pb)], ['step'])
            sc.tt('dve', lo, lo, step, ALU.add, ['lo', 'step'], ['lo'])
        sc.ts('dve', mask, aff128, lo[:, 0:1], 0.0, ALU.is_ge, ALU.add, ['aff128', 'lo'], ['mask', 'cntp'], accum=cntp)
        sc.add('dve', lambda e: e.tensor_tensor_scan(scan, ones1, mask, 0.0, ALU.mult, ALU.add),
               ['ones1', 'mask'], ['scan'])
        sc.mm(PS(7)[:, 0:1], lmat, cntp, True, True, ['lmat', 'cntp'], [pk(7)])
        sc.copy('dve', offs, PS(7)[:, 0:1], [pk(7)], ['offs'])
        sc.ts('dve', cum, scan, offs[:, 0:1], None, ALU.add, None, ['scan', 'offs'], ['cum'])
        sc.dma('sp', cumd.rearrange("e (s c) -> (e s) c", s=8), cum, ['cum'], ['cumd'])
        if debug:
            sc.dma('sp', dbg["dbg_cum"].rearrange("e (s c) -> (e s) c", s=8), mask, ['mask'], ['dbg_cum'])

    def moe_experts(layer, mb):
        (wring, cumB, idxf, idxi, xg, gg, xgT, hidT, ysb, sg, junkc) = mb
        NSL = len(wring)
        pieces = []
        for e in range(16):
            for fq in range(4):
                pieces.append((e, 'gu', fq))
            for dh in range(2):
                pieces.append((e, 'd', dh))
        state = dict(next_load=0)

        def load_piece(n):
            e, kind, q = pieces[n]
            slot = wring[n % NSL]
            key = 'w%d' % (n % NSL)
            if kind == 'gu':
                sc.dma('pool', slot[:, 0:8, :], wg_d[layer][e][:, q * 512:(q + 1) * 512].rearrange("(k p) f -> p k f", p=128),
                       (), [key])
                sc.dma('pool', slot[:, 8:16, :], wu_d[layer][e][:, q * 512:(q + 1) * 512].rearrange("(k p) f -> p k f", p=128),
                       (), [key])
            else:
                sc.dma('pool', slot, wd_d[layer][e][:, q * 512:(q + 1) * 512].rearrange("(f p) d -> p f d", p=128),
                       (), [key])

        def prefetch(upto):
            while state['next_load'] < min(upto, len(pieces)):
                load_piece(state['next_load'])
                state['next_load'] += 1

        def prep_a(e):
            b = e % 2
            b3 = e % 4
            g3 = e % 3
            sc.dma('sp', cumB[b], cumd[e:e + 1, :].partition_broadcast(128), ['cumd'], ['cumB%d' % b])
            for j in range(4):
                sc.ts('dve', junkc, cumB[b], cvals[:, j:j + 1], 0.0, ALU.is_le, ALU.add,
                      ['cumB%d' % b, 'cvals'], ['junkc', 'idxf%d' % b3], accum=idxf[b3][:, j:j + 1])
            sc.ts('dve', idxf[b3], idxf[b3], float(S - 1), None, ALU.min, None, ['idxf%d' % b3], ['idxf%d' % b3])
            sc.copy('dve', idxi[b3], idxf[b3], ['idxf%d' % b3], ['idxi%d' % b3])

        def prep_a2(e):
            b = e % 2
            b3 = e % 4
            g3 = e % 3
            for j in range(4):
                sc.idma(xg[b][:, j, :], None, Hb[:, :], bass.IndirectOffsetOnAxis(ap=idxi[b3][:, j:j + 1], axis=0),
                        ['idxi%d' % b3, 'Hb'], ['xg%d' % b])
                sc.idma(gg[g3][:, j, :], None, affd[:, :], bass.IndirectOffsetOnAxis(ap=idxi[b3][:, j:j + 1], axis=0),
                        ['idxi%d' % b3, 'affd'], ['gg%d' % g3])

        def prep_b(e):
            b = e % 2
            for j in range(4):
                tb = 0 if j % 2 == 0 else 7
                for k in range(8):
                    sc.tr(PSB(tb)[:, k * 128:(k + 1) * 128], xg[b][:, j, k * 128:(k + 1) * 128], identb,
                          ['xg%d' % b, 'identb'], [pk(tb)])
                sc.copy('act' if j % 2 == 0 else 'dve', xgT[b][:, :, j * 128:(j + 1) * 128],
                        PSB(tb).rearrange("p (k t) -> p k t", k=8), [pk(tb)], ['xgT%d' % b])

        prefetch(NSL)
        prep_a(0)
        prep_a2(0)
        prep_a(1)
        prep_a2(1)
        prep_b(0)
        n = 0
        for e in range(16):
            b = e % 2
            b3 = e % 4
            g3 = e % 3
            if e + 2 < 16:
                prep_a(e + 2)
            for fq in range(4):
                if fq == 2 and e + 2 < 16:
                    prep_a2(e + 2)
                if fq == 3 and e + 1 < 16:
                    prep_b(e + 1)
                slot = wring[n % NSL]
                wkey = 'w%d' % (n % NSL)
                for fcl in range(4):
                    fc = fq * 4 + fcl
                    bg, bu = 1 + (fc % 2), 3 + (fc % 2)
                    for k in range(8):
                        sc.mm(PS(bg), slot[:, k, fcl * 128:(fcl + 1) * 128], xgT[b][:, k, :], k == 0, k == 7,
                              [wkey, 'xgT%d' % b], [pk(bg)])
                    for k in range(8):
                        sc.mm(PS(bu), slot[:, 8 + k, fcl * 128:(fcl + 1) * 128], xgT[b][:, k, :], k == 0, k == 7,
                              [wkey, 'xgT%d' % b], [pk(bu)])
                    sc.act(sg[fc % 2], PS(bg), AF.Silu, [pk(bg)], ['sg%d' % (fc % 2)])
                    sc.tt('dve', hidT[:, fc, :], sg[fc % 2], PS(bu), ALU.mult, ['sg%d' % (fc % 2), pk(bu)], ['hid%d' % fc])
                n += 1
                prefetch(n + NSL)
            for dh in range(2):
                slot = wring[n % NSL]
                wkey = 'w%d' % (n % NSL)
                for j in range(4):
                    bd = 5 + (j % 2)
                    for fc in range(16):
                        sc.mm(PS(bd), hidT[:, fc, j * 128:(j + 1) * 128], slot[:, fc, :], fc == 0, fc == 15,
                              ['hid%d' % fc, wkey], [pk(bd)])
                    sc.act(ysb[:, j, dh * 512:(dh + 1) * 512], PS(bd), AF.Copy, [pk(bd), 'gg%d' % g3], ['ysb%d' % j],
                           scale=gg[g3][:, j, e:e + 1])
                n += 1
                prefetch(n + NSL)
            for j in range(4):
                sc.idma(Yd[:, :], bass.IndirectOffsetOnAxis(ap=idxi[b3][:, j:j + 1], axis=0), ysb[:, j, :], None,
                        ['ysb%d' % j, 'idxi%d' % b3] + ['Yd_%d_%d' % ((e - 1) % 2, jj) for jj in range(4)],
                        ['Yd_%d_%d' % (e % 2, j)], compute_op=ALU.add)

    def moe_phase(layer, affT):
        m0 = ar.mark()
        wring = [ar.alloc([16, 512], BF16) for _ in range(6)]
        m1 = ar.mark()
        aff128 = ar.alloc([512], F32)
        junk = ar.alloc([512], BF16)
        ones1 = ar.alloc([512], F32)
        mask = ar.alloc([512], F32)
        scan = ar.alloc([512], F32)
        cum = ar.alloc([512], F16)
        gmat = ar.alloc([128], F32)
        lmat = ar.alloc([128], F32)
        smalls = [ar.alloc([1], F32) for _ in range(5)]
        rb = (aff128, junk, ones1, mask, scan, cum, gmat, lmat) + tuple(smalls)
        assert ar.off <= 184 * 1024
        moe_route(affT[0:16], rb)
        sc.barrier()
        ar.release(m1)
        cumB = [ar.alloc([S], F16) for _ in range(2)]
        idxf = [ar.alloc([4], F32) for _ in range(4)]
        idxi = [ar.alloc([4], I32) for _ in range(4)]
        xg = [ar.alloc([4, D], BF16) for _ in range(2)]
        gg = [ar.alloc([4, 16], F32) for _ in range(3)]
        xgT = [ar.alloc([8, 512], BF16) for _ in range(2)]
        hidT = ar.alloc([16, 512], BF16)
        ysb = ar.alloc([4, D], F32)
        sg = [ar.alloc([512], F32) for _ in range(2)]
        junkc = ar.alloc([S], BF16)
        moe_experts(layer, (wring, cumB, idxf, idxi, xg, gg, xgT, hidT, ysb, sg, junkc))
        sc.barrier()
        ar.release(m0)

    def attn_norm_store(po_bank, nq, esk, mixrow, rd, mixt):
        if esk is not None:
            sc.ts('dve', rd[64:128, 0:nq], PS(po_bank)[64:128, 0:nq], esk, None, ALU.add, None,
                  [pk(po_bank), 'esink'], ['rdh'])
        else:
            sc.copy('dve', rd[64:128, 0:nq], PS(po_bank)[64:128, 0:nq], [pk(po_bank)], ['rdh'])
        sc.ts('dve', rd[0:64, 0:nq], rd[64:128, 0:nq], 1.0, None, ALU.mult, None, ['rdh'], ['rd'])
        sc.recip(rd[0:64, 0:nq], rd[0:64, 0:nq], ['rd'], ['rd'])
        sc.tt('dve', mixt[0:64, 0:nq], PS(po_bank)[0:64, 0:nq], rd[0:64, 0:nq], ALU.mult, [pk(po_bank), 'rd'], ['mixt'])
        sc.dma('sp', mixrow, mixt[0:64, 0:nq], ['mixt'], ['mixd'])

    def outproj_ln_phase(layer, wo_ap, gname, bname, resid_load, resid_fn):
        m0 = ar.mark()
        wo = ar.alloc([8, D], BF16)
        gB = ar.alloc([D], F32)
        bB = ar.alloc([D], F32)
        rtr = ar.alloc([8, 16], BF16)
        affT = ar.view(184 * 1024, [S], F32)
        mt = [ar.alloc([8, 128], BF16) for _ in range(2)]
        xc = [ar.alloc([D], F32) for _ in range(2)]
        rbuf = [ar.alloc([D], F32) for _ in range(2)]
        hbuf = [ar.alloc([D], F32) for _ in range(2)]
        ahb = [ar.alloc([D], F32) for _ in range(2)]
        hbf = [ar.alloc([D], BF16) for _ in range(2)]
        hT = [ar.alloc([8, 128], BF16) for _ in range(2)]
        ex = [ar.alloc([16], F32) for _ in range(2)]
        ssum = [ar.alloc([1], F32) for _ in range(2)]
        st = [ar.alloc([2, 6], F32) for _ in range(2)]
        mv = [ar.alloc([2], F32) for _ in range(2)]
        sd = [ar.alloc([1], F32) for _ in range(2)]
        sc.dma('pool', wo, wo_ap.rearrange("(k p) d -> p k d", p=128), (), ['wo'])
        sc.dma('pool', rtr, router_d[layer].rearrange("(k p) e -> p k e", p=128), (), ['rtr'])
        load_bcast(gB, gname, 'lng')
        load_bcast(bB, bname, 'lnb')
        def loads(tc):
            b = tc % 2
            sc.dma('sp', mt[b], mixd[:, :, tc * 128:(tc + 1) * 128].rearrange("f p t -> p f t"), ['mixd'], ['mt%d' % b])
            resid_load(tc, xc[b], 'xc%d' % b)

        def banks_of(tc):
            return (0, 1) if tc % 2 == 0 else (2, 3)

        def mms(tc):
            b = tc % 2
            for half, bank in zip((0, 1), banks_of(tc)):
                for f in range(8):
                    sc.mm(PS(bank), mt[b][:, f, :], wo[:, f, half * 512:(half + 1) * 512], f == 0, f == 7,
                          ['mt%d' % b, 'wo'], [pk(bank)])

        def resid(tc):
            b = tc % 2
            pa, pb_ = banks_of(tc)
            resid_fn(tc, pa, pb_, rbuf[b], 'rbuf%d' % b, xc[b], 'xc%d' % b)

        def stats(tc):
            b = tc % 2
            ln_stats(rbuf[b], 'rbuf%d' % b, st[b], mv[b], sd[b], str(b))

        loads(0)
        loads(1)
        mms(0)
        mms(1)
        resid(0)
        stats(0)
        for tc in range(NT):
            b = tc % 2
            sf = str(b)
            if tc + 2 < NT:
                loads(tc + 2)
                mms(tc + 2)
            if tc + 1 < NT:
                resid(tc + 1)
                stats(tc + 1)
            ln_apply(rbuf[b], 'rbuf' + sf, gB, bB, hbuf[b], 'hbuf' + sf, mv[b], sd[b], sf)
            post_h_moe(hbuf[b], 'hbuf' + sf, tc, layer, (ahb[b], hbf[b], hT[b], ex[b], ssum[b], rtr, affT), sf)
            if debug and layer == 0:
                sc.dma('sp', dbg["dbg_h1"][tc * 128:(tc + 1) * 128, :], hbuf[b], ['hbuf' + sf], ['dbg_h1'])
        sc.dma('sp', affd.rearrange("(c p) e -> p c e", p=128), afft, ['afft%d' % t for t in range(NT)], ['affd'])
        if debug and layer == 0:
            sc.dma('sp', dbg["dbg_aff"].rearrange("(c p) e -> p c e", p=128), afft, ['afft%d' % t for t in range(NT)], ['dbg_aff'])
        sc.barrier()
        ar.release(m0)
        return affT, m0

    qaT = ar.alloc([4, S], BF16)
    kaT2 = ar.alloc([2, S], BF16)
    va = ar.alloc([NT, 2, 128], BF16)
    cqn = ar.alloc([3, S], BF16)
    ckvn = ar.alloc([2, S], BF16)
    KT = [ar.alloc([S], BF16) for _ in range(2)]
    mA = ar.mark()
    w0 = ar.alloc([8, 1792], BF16)
    xb = ar.alloc([4, D], BF16)
    xT = ar.alloc([8, 512], BF16)
    cqg = ar.alloc([3, 512], F32)
    ckg = ar.alloc([2, 512], F32)
    sq = ar.alloc([5, 512], BF16)
    rq = ar.alloc([512], F32)
    rk = ar.alloc([512], F32)
    cs = ar.alloc([512], F32)
    sn = ar.alloc([512], F32)
    t1 = ar.alloc([512], F32)
    t2 = ar.alloc([512], F32)
    gq = ar.alloc([3], F32)
    gkv = ar.alloc([2], F32)
    sc.dma('pool', w0[:, :, 0:1664], w0f.rearrange("(k p) f -> p k f", p=128), (), ['w0'])
    sc.dma('pool', w0[:, :, 1664:1792], w0v.rearrange("(k p) f -> p k f", p=128), (), ['w0'])
    sc.dma('sp', gq, gq_d, (), ['gq'])
    sc.dma('sp', gkv, gkv_d, (), ['gkv'])
    sc.memset('pool', va[:, :, :, 64:128], 1.0, ['va_ones'])
    for i in range(8):
        tl = slice(i * 512, (i + 1) * 512)
        sc.dma('pool', xb, x[tl, :].rearrange("(c p) d -> p c d", p=128), (), ['xb'])
        sc.dma('sp', cs[64:96, :], cos_d[:, tl], (), ['cs'])
        sc.dma('sp', sn[64:96, :], sin_d[:, tl], (), ['sn'])
        for c in range(4):
            tb = c % 2
            for k in range(8):
                sc.tr(PSB(tb)[:, k * 128:(k + 1) * 128], xb[:, c, k * 128:(k + 1) * 128], identb, ['xb', 'identb'], [pk(tb)])
            sc.copy('act' if c % 2 == 0 else 'dve', xT[:, :, c * 128:(c + 1) * 128],
                    PSB(tb).rearrange("p (k t) -> p k t", k=8), [pk(tb)], ['xT'])
        for f in range(13):
            bank = 2 + (f % 3)
            for k in range(8):
                sc.mm(PS(bank), w0[:, k, f * 128:(f + 1) * 128], xT[:, k, :], k == 0, k == 7, ['w0', 'xT'], [pk(bank)])
            if f < 4:
                sc.copy('act', qaT[:, f, tl], PS(bank), [pk(bank)], ['qaT'])
            elif f < 6:
                sc.copy('dve', kaT2[:, f - 4, tl], PS(bank), [pk(bank)], ['kaT2'])
            elif f < 9:
                r = f - 6
                sc.act(sq[:, r, :], PS(bank), AF.Square, [pk(bank)], ['sq%d' % r, 'psord'])
                sc.ts('dve', cqg[:, r, :], PS(bank), gq[:, r:r + 1], None, ALU.mult, None, [pk(bank), 'gq', 'psord'], ['cqg%d' % r])
            elif f < 11:
                r = f - 9
                sc.act(sq[:, 3 + r, :], PS(bank), AF.Square, [pk(bank)], ['sq%d' % (3 + r), 'psord'])
                sc.ts('dve', ckg[:, r, :], PS(bank), gkv[:, r:r + 1], None, ALU.mult, None, [pk(bank), 'gkv', 'psord'], ['ckg%d' % r])
            elif f == 11:
                sc.tt('dve', t1[64:96, :], PS(bank)[64:96, :], cs[64:96, :], ALU.mult, [pk(bank), 'cs'], ['t1'])
            else:
                sc.tt('dve', t2[64:96, :], PS(bank)[64:96, :], sn[64:96, :], ALU.mult, [pk(bank), 'sn'], ['t2'])
                sc.tt('pool', KT[0][64:96, tl], t1[64:96, :], t2[64:96, :], ALU.add, ['t1', 't2'], ['KT0pe'])
                sc.copy('pool', KT[1][64:96, tl], KT[0][64:96, tl], ['KT0pe'], ['KT1pe'])
        for r in range(3):
            sc.mm(PS(5), onesb, sq[:, r, :], r == 0, r == 2, ['onesb', 'sq%d' % r], [pk(5)])
        for r in range(2):
            sc.mm(PS(6), onesb, sq[:, 3 + r, :], r == 0, r == 1, ['onesb', 'sq%d' % (3 + r)], [pk(6)])
        sc.act(rq, PS(5), AF.Sqrt, [pk(5), 'eps'], ['rq'], bias=eps_rms, scale=1.0 / 384.0)
        sc.recip(rq, rq, ['rq'], ['rq'])
        sc.act(rk, PS(6), AF.Sqrt, [pk(6), 'eps'], ['rk'], bias=eps_rms, scale=1.0 / 256.0)
        sc.recip(rk, rk, ['rk'], ['rk'])
        for r in range(3):
            sc.tt('dve', cqn[:, r, tl], cqg[:, r, :], rq, ALU.mult, ['cqg%d' % r, 'rq'], ['cqn'])
        for r in range(2):
            sc.tt('dve', ckvn[:, r, tl], ckg[:, r, :], rk, ALU.mult, ['ckg%d' % r, 'rk'], ['ckvn'])
        for c in range(4):
            for k in range(8):
                sc.mm(PS(7)[:, c * 128:(c + 1) * 128], xT[:, k, c * 128:(c + 1) * 128], w0[:, k, 1664:1792],
                      k == 0, k == 7, ['xT', 'w0'], [pk(7)])
        sc.copy('act', va[:, i * 4:(i + 1) * 4, :, 0:64], PS(7).rearrange("p (c g d) -> p c g d", c=4, g=2),
                [pk(7)], ['va'])
    sc.barrier()
    ar.release(mA)
    if stop_after == 'L0A':
        sc.finalize()
        return nc

    mW = ar.mark()
    bhi = ar.alloc([8, 384], BF16)
    blo = ar.alloc([8, 384], BF16)
    pT = [ar.alloc([384], BF16) for _ in range(4)]
    rd = ar.alloc([512], F32)
    mixt = ar.alloc([512], BF16)
    sc.dma('sp', bhi, biasw_hi_d, (), ['biasw'])
    sc.dma('sp', blo, biasw_lo_d, (), ['biasw'])
    scale_a = 64.0 ** -0.5
    for h in range(8):
        g = h // 4
        f = h // 2
        rbs = (h % 2) * 64
        pr = slice(rbs, rbs + 64)

        def q0_of(j):
            return max(0, (j - 1) * 128)

        def s_step(j):
            q0 = q0_of(j)
            q1 = min(S, (j + 2) * 128)
            n = q1 - q0
            off = q0 - (j - 1) * 128
            bank = j % 3
            sc.mm(PS(bank)[:, 0:n], identb, bhi[:, h, off:off + n], True, False, ['identb', 'biasw'], [pk(bank)])
            sc.mm(PS(bank)[:, 0:n], identb, blo[:, h, off:off + n], False, False, ['identb', 'biasw'], [pk(bank)])
            sc.mm(PS(bank)[:, 0:n], kaT2[pr, g, j * 128:(j + 1) * 128], qaT[pr, f, q0:q1], False, True,
                  ['kaT2', 'qaT'], [pk(bank)])
            sc.act(pT[j % 4][:, 0:n], PS(bank)[:, 0:n], AF.Exp, [pk(bank)], ['pT%d' % (j % 4)], scale=scale_a)

        def pv_step(i):
            pob = 3 + ((i // 4) % 2)
            js = [jj for jj in (i - 1, i, i + 1) if 0 <= jj < NT]
            for n_, jj in enumerate(js):
                c0 = i * 128 - q0_of(jj)
                sc.mm(PS(pob)[:, (i % 4) * 128:(i % 4 + 1) * 128], va[:, jj, g, :], pT[jj % 4][:, c0:c0 + 128],
                      n_ == 0, n_ == len(js) - 1, ['va', 'va_ones', 'pT%d' % (jj % 4)], [pk(pob)])
            if i % 4 == 3:
                t0 = (i // 4) * 512
                attn_norm_store(pob, 512, esink[64:128, h:h + 1], mixd[f, pr, t0:t0 + 512], rd, mixt)

        s_step(0)
        s_step(1)
        for j in range(2, NT):
            s_step(j)
            pv_step(j - 2)
        pv_step(NT - 2)
        pv_step(NT - 1)
    sc.barrier()
    ar.release(mW)
    if stop_after == 'L0W':
        sc.finalize()
        return nc

    mM = ar.mark()
    wq = ar.alloc([3, 768], BF16)
    wqs = ar.alloc([3, 768], BF16)
    wkv = ar.alloc([2, 1024], BF16)
    QT = [ar.alloc([S], BF16) for _ in range(2)]
    vm = [ar.alloc([NT, 128], BF16) for _ in range(2)]
    csq = [ar.alloc([512], F32) for _ in range(2)]
    snq = [ar.alloc([512], F32) for _ in range(2)]
    u1 = ar.alloc([512], F32)
    u2 = ar.alloc([512], F32)
    pbuf = [ar.alloc([512], BF16) for _ in range(4)]
    rd = ar.alloc([512], F32)
    mixt = ar.alloc([512], BF16)
    sc.dma('pool', wq, wq_d.rearrange("(k p) f -> p k f", p=128), (), ['wq'])
    sc.dma('pool', wqs, wqs_d.rearrange("(k p) f -> p k f", p=128), (), ['wqs'])
    sc.dma('pool', wkv, wkv_d.rearrange("(k p) f -> p k f", p=128), (), ['wkv'])
    for b in range(2):
        sc.memset('pool', vm[b][:, :, 64:128], 1.0, ['vm_ones%d' % b])
    scale_b = 96.0 ** -0.5

    def mla_proj(h):
        b = h % 2
        for i in range(8):
            tl = slice(i * 512, (i + 1) * 512)
            cb = i % 2
            sc.dma('sp', csq[cb][64:96, :], cos_d[:, tl], (), ['csq%d' % cb])
            sc.dma('sp', snq[cb][64:96, :], sin_d[:, tl], (), ['snq%d' % cb])
            for r in range(3):
                sc.mm(PS(5)[0:96, :], wq[:, r, h * 96:(h + 1) * 96], cqn[:, r, tl], r == 0, r == 2, ['wq', 'cqn'], [pk(5)])
            for r in range(3):
                sc.mm(PS(6)[0:96, :], wqs[:, r, h * 96:(h + 1) * 96], cqn[:, r, tl], r == 0, r == 2, ['wqs', 'cqn'], [pk(6)])
            sc.copy('act', QT[b][0:64, tl], PS(5)[0:64, :], [pk(5)], ['QTn%d' % b, 'psord'])
            sc.tt('dve', u1[64:96, :], PS(5)[64:96, :], csq[cb][64:96, :], ALU.mult, [pk(5), 'csq%d' % cb, 'psord'], ['u1'])
            sc.tt('dve', u2[64:96, :], PS(6)[64:96, :], snq[cb][64:96, :], ALU.mult, [pk(6), 'snq%d' % cb], ['u2'])
            sc.tt('pool', QT[b][64:96, tl], u1[64:96, :], u2[64:96, :], ALU.add, ['u1', 'u2'], ['QTp%d' % b])
            for c in range(2):
                sc.mm(PS(7)[0:64, :], wkv[:, c, h * 128:h * 128 + 64], ckvn[:, c, tl], c == 0, c == 1, ['wkv', 'ckvn'], [pk(7)])
            sc.copy('act', KT[b][0:64, tl], PS(7)[0:64, :], [pk(7)], ['KTn%d' % b])
        for g4 in range(4):
            for t8 in range(8):
                tc = g4 * 8 + t8
                for c in range(2):
                    sc.mm(PS(7)[:, t8 * 64:(t8 + 1) * 64], ckvn[:, c, tc * 128:(tc + 1) * 128],
                          wkv[:, c, h * 128 + 64:h * 128 + 128], c == 0, c == 1, ['ckvn', 'wkv'], [pk(7)])
            sc.copy('dve', vm[b][:, g4 * 8:(g4 + 1) * 8, 0:64], PS(7).rearrange("p (t d) -> p t d", t=8), [pk(7)], ['vm%d' % b])

    def mla_attn(h):
        b = h % 2
        f = 4 + h // 2
        rbs = (h % 2) * 64
        kkeys = ['KTn%d' % b, 'KT%dpe' % b]
        qkeys = ['QTn%d' % b, 'QTp%d' % b]
        cnt = 0
        for i in range(8):
            tl = slice(i * 512, (i + 1) * 512)
            pob = 3 + (i % 2)

            def s_step(kc, slot):
                bank = slot % 3
                sc.mm(PS(bank), KT[b][0:96, kc * 128:(kc + 1) * 128], QT[b][0:96, tl], True, True, kkeys + qkeys, [pk(bank)])
                sc.act(pbuf[slot % 4], PS(bank), AF.Exp, [pk(bank)], ['pb%d' % (slot % 4)], scale=scale_b)

            def pv_step(kc, slot):
                sc.mm(PS(pob), vm[b][:, kc, :], pbuf[slot % 4], kc == 0, kc == NT - 1,
                      ['vm%d' % b, 'vm_ones%d' % b, 'pb%d' % (slot % 4)], [pk(pob)])

            s_step(0, cnt)
            s_step(1, cnt + 1)
            for kc in range(NT):
                if kc + 2 < NT:
                    s_step(kc + 2, cnt + kc + 2)
                pv_step(kc, cnt + kc)
            cnt += NT
            attn_norm_store(pob, 512, None, mixd[f, rbs:rbs + 64, tl], rd, mixt)

    mla_proj(0)
    for h in range(8):
        if h + 1 < 8:
            mla_proj(h + 1)
        mla_attn(h)
    sc.barrier()
    ar.release(base_mark)

    if debug:
        mD = ar.mark()
        mtb = ar.alloc([S], BF16)
        mtf = ar.alloc([S], F32)
        for f in range(8):
            sc.dma('sp', mtb, mixd[f], ['mixd'], ['mtb'])
            sc.copy('dve', mtf, mtb, ['mtb'], ['mtf'])
            sc.dma('sp', dbg["dbg_mix"][f], mtf, ['mtf'], ['dbg_mix'])
        sc.barrier()
        ar.release(mD)

    def resid0_load(tc, xcb, xck):
        sc.dma('sp', xcb, x[tc * 128:(tc + 1) * 128, :], (), [xck])

    def resid0(tc, pa, pb_, rbuf, rbk, xcb, xck):
        for half, bank in ((0, pa), (1, pb_)):
            sc.stt('dve', rbuf[:, half * 512:(half + 1) * 512], xcb[:, half * 512:(half + 1) * 512], ALPHA, PS(bank),
                   ALU.mult, ALU.add, [xck, pk(bank)], [rbk])

    affT, _ = outproj_ln_phase(0, wo_d[0], "ln0a_g", "ln0a_b", resid0_load, resid0)
    if stop_after == 'L0O':
        sc.finalize()
        return nc

    moe_phase(0, affT)
    ar.release(base_mark)
    if debug:
        mD = ar.mark()
        yb = ar.alloc([D], F32)
        for tc in range(NT):
            sc.dma('sp', yb, Yd[tc * 128:(tc + 1) * 128, :], ['Yd'], ['yb'])
            sc.dma('sp', dbg["dbg_y"][tc * 128:(tc + 1) * 128, :], yb, ['yb'], ['dbg_y'])
        sc.barrier()
        ar.release(mD)
    if stop_after == 'MOE0':
        sc.finalize()
        return nc

    h2T = ar.alloc([8, S], BF16)
    mL1 = ar.mark()
    gB = ar.alloc([D], F32)
    bB = ar.alloc([D], F32)
    yc = [ar.alloc([D], F32) for _ in range(2)]
    hbuf = [ar.alloc([D], F32) for _ in range(2)]
    ahb = [ar.alloc([D], F32) for _ in range(2)]
    hbf = [ar.alloc([D], BF16) for _ in range(2)]
    st = [ar.alloc([2, 6], F32) for _ in range(2)]
    mv = [ar.alloc([2], F32) for _ in range(2)]
    sd = [ar.alloc([1], F32) for _ in range(2)]
    load_bcast(gB, "ln0b_g", 'lng')
    load_bcast(bB, "ln0b_b", 'lnb')
    sc.dma('sp', yc[0], Yd[0:128, :], ['Yd'], ['yc0'])
    sc.dma('sp', yc[1], Yd[128:256, :], ['Yd'], ['yc1'])
    ln_stats(yc[0], 'yc0', st[0], mv[0], sd[0], '0')
    for tc in range(NT):
        b = tc % 2
        sf = str(b)
        rows = slice(tc * 128, (tc + 1) * 128)
        if tc + 1 < NT:
            ln_stats(yc[1 - b], 'yc%d' % (1 - b), st[1 - b], mv[1 - b], sd[1 - b], str(1 - b))
        ln_apply(yc[b], 'yc%d' % b, gB, bB, hbuf[b], 'hbuf' + sf, mv[b], sd[b], sf)
        if tc + 2 < NT:
            sc.dma('sp', yc[b], Yd[(tc + 2) * 128:(tc + 3) * 128, :], ['Yd'], ['yc%d' % b])
        sc.act(ahb[b], hbuf[b], AF.Copy, ['hbuf' + sf], ['ahb' + sf], scale=ALPHA)
        sc.dma('sp', R1[rows, :], ahb[b], ['ahb' + sf], ['R1'])
        sc.copy('act', hbf[b], hbuf[b], ['hbuf' + sf], ['hbf' + sf])
        tb = tc % 2
        for k in range(8):
            sc.tr(PSB(tb)[:, k * 128:(k + 1) * 128], hbf[b][:, k * 128:(k + 1) * 128], identb, ['hbf' + sf, 'identb'], [pk(tb)])
        sc.copy('act', h2T[:, :, rows], PSB(tb).rearrange("p (k t) -> p k t", k=8), [pk(tb)], ['h2T'])
    sc.barrier()
    ar.release(mL1)

    wp = [ar.alloc([8, 384], BF16) for _ in range(2)]
    QTp = [ar.alloc([S], BF16) for _ in range(2)]
    KTp = [ar.alloc([S], BF16) for _ in range(2)]
    Vp = [ar.alloc([NT, 2, 128], BF16) for _ in range(2)]
    TTp = [ar.alloc([2, 16, 64], F32) for _ in range(2)]
    TTb = [ar.alloc([2, 16, 64], BF16) for _ in range(2)]
    namask = ar.alloc([16, 64], F32)
    tbn = [ar.alloc([5, 64], F32) for _ in range(4)]
    pTn = [ar.alloc([5, 64], BF16) for _ in range(5)]
    rd = ar.alloc([512], F32)
    mixt = ar.alloc([512], BF16)
    sc.dma('sp', namask, namask_d, (), ['namask'])
    for b in range(2):
        sc.memset('pool', Vp[b][:, :, :, 64:128], 1.0, ['vp_ones%d' % b])
    scale_c = 64.0 ** -0.5

    def na_proj(hp):
        b = hp % 2
        for part in range(3):
            sc.dma('pool', wp[b][:, :, part * 128:(part + 1) * 128],
                   wqkv_d[:, part * 1024 + hp * 128: part * 1024 + (hp + 1) * 128].rearrange("(k p) f -> p k f", p=128),
                   (), ['wp%d' % b])
        for g in range(2):
            sc.dma('sp', TTp[b][:, g], rpbT_d[hp * 2 + g], (), ['TT%d_%d' % (b, g)])
            sc.tt('pool', TTp[b][:, g], TTp[b][:, g], namask, ALU.add, ['TT%d_%d' % (b, g), 'namask'], ['TT%d_%d' % (b, g)])
            sc.ts('pool', TTb[b][:, g], TTp[b][:, g], 1.0 / scale_c, None, ALU.mult, None, ['TT%d_%d' % (b, g)], ['TTb%d_%d' % (b, g)])
        for i in range(8):
            tl = slice(i * 512, (i + 1) * 512)
            for k in range(8):
                sc.mm(PS(5), wp[b][:, k, 0:128], h2T[:, k, tl], k == 0, k == 7, ['wp%d' % b, 'h2T'], [pk(5)])
            sc.copy('act', QTp[b][:, tl], PS(5), [pk(5)], ['QTp%d' % b])
            for k in range(8):
                sc.mm(PS(6), wp[b][:, k, 128:256], h2T[:, k, tl], k == 0, k == 7, ['wp%d' % b, 'h2T'], [pk(6)])
            sc.copy('dve', KTp[b][:, tl], PS(6), [pk(6)], ['KTp%d' % b])
            for c in range(4):
                tc = i * 4 + c
                for k in range(8):
                    sc.mm(PS(5)[:, c * 128:(c + 1) * 128], h2T[:, k, tc * 128:(tc + 1) * 128], wp[b][:, k, 256:384],
                          k == 0, k == 7, ['h2T', 'wp%d' % b], [pk(5)])
            sc.copy('act', Vp[b][:, i * 4:(i + 1) * 4, :, 0:64], PS(5).rearrange("p (c g d) -> p c g d", c=4, g=2),
                    [pk(5)], ['Vp%d' % b])

    def na_attn(hp):
        b = hp % 2
        tasks = [(g, r) for g in range(2) for r in range(64)]

        def geom(r):
            rs = min(max(r - 4, 0), 56)
            odd = rs % 2
            kr0 = rs - odd
            nch = 5 if odd else 4
            return odd, kr0, nch, kr0 - r + 8

        def s_part(t):
            g, r = tasks[t]
            pr = slice(g * 64, g * 64 + 64)
            odd, kr0, nch, u0 = geom(r)
            bank = (0, 1, 2, 7)[t % 4]
            sc.mm(PS(bank)[:, 0:nch * 64].rearrange("p (c q) -> p c q", c=nch), identb,
                  TTb[b][:, g, u0:u0 + 2 * nch - 1:2, :], True, False, ['identb', 'TTb%d_%d' % (b, g)], [pk(bank)])
            for c in range(nch):
                kc = kr0 // 2 + c
                sc.mm(PS(bank)[:, c * 64:(c + 1) * 64], KTp[b][pr, kc * 128:(kc + 1) * 128], QTp[b][pr, r * 64:(r + 1) * 64],
                      False, c == nch - 1, ['KTp%d' % b, 'QTp%d' % b], [pk(bank)])
            sc.act(pTn[t % 5][:, 0:nch, :], PS(bank)[:, 0:nch * 64].rearrange("p (c q) -> p c q", c=nch), AF.Exp,
                   [pk(bank)], ['pTn%d' % (t % 5)], scale=scale_c)

        def pv_part(t):
            g, r = tasks[t]
            pr = slice(g * 64, g * 64 + 64)
            odd, kr0, nch, u0 = geom(r)
            pob = 3 + ((r // 8) % 2)
            pkk = 'pTn%d' % (t % 5)
            for c in range(nch):
                kc = kr0 // 2 + c
                if odd and c == 0:
                    ps_ = slice(64, 128)
                elif odd and c == nch - 1:
                    ps_ = slice(0, 64)
                else:
                    ps_ = slice(0, 128)
                sc.mm(PS(pob)[:, (r % 8) * 64:(r % 8 + 1) * 64], Vp[b][ps_, kc, g, :], pTn[t % 5][ps_, c, :],
                      c == 0, c == nch - 1, ['Vp%d' % b, 'vp_ones%d' % b, pkk], [pk(pob)])
            if r % 8 == 7:
                t0 = (r // 8) * 512
                attn_norm_store(pob, 512, None, mixd[hp, pr, t0:t0 + 512], rd, mixt)

        LA = 3
        for t in range(min(LA, len(tasks))):
            s_part(t)
        for t in range(len(tasks)):
            if t + LA < len(tasks):
                s_part(t + LA)
            pv_part(t)

    na_proj(0)
    for hp in range(8):
        if hp + 1 < 8:
            na_proj(hp + 1)
        na_attn(hp)
    sc.barrier()
    ar.release(base_mark)

    def resid1_load(tc, xcb, xck):
        sc.dma('sp', xcb, R1[tc * 128:(tc + 1) * 128, :], ['R1'], [xck])

    def resid1(tc, pa, pb_, rbuf, rbk, xcb, xck):
        for half, bank in ((0, pa), (1, pb_)):
            sc.tt('dve', rbuf[:, half * 512:(half + 1) * 512], xcb[:, half * 512:(half + 1) * 512], PS(bank),
                  ALU.add, [xck, pk(bank)], [rbk])

    affT, _ = outproj_ln_phase(1, wo_d[1], "ln1a_g", "ln1a_b", resid1_load, resid1)
    moe_phase(1, affT)
    ar.release(base_mark)

    gB = ar.alloc([D], F32)
    bB = ar.alloc([D], F32)
    yc = [ar.alloc([D], F32) for _ in range(2)]
    hb2 = [ar.alloc([D], F32) for _ in range(2)]
    st = [ar.alloc([2, 6], F32) for _ in range(2)]
    mv = [ar.alloc([2], F32) for _ in range(2)]
    sd = [ar.alloc([1], F32) for _ in range(2)]
    load_bcast(gB, "ln1b_g", 'lng')
    load_bcast(bB, "ln1b_b", 'lnb')
    sc.dma('sp', yc[0], Yd[0:128, :], ['Yd'], ['yc0'])
    sc.dma('sp', yc[1], Yd[128:256, :], ['Yd'], ['yc1'])
    ln_stats(yc[0], 'yc0', st[0], mv[0], sd[0], '0')
    for tc in range(NT):
        b = tc % 2
        rows = slice(tc * 128, (tc + 1) * 128)
        if tc + 1 < NT:
            ln_stats(yc[1 - b], 'yc%d' % (1 - b), st[1 - b], mv[1 - b], sd[1 - b], str(1 - b))
        ln_apply(yc[b], 'yc%d' % b, gB, bB, hb2[b], 'hb%d' % b, mv[b], sd[b], str(b))
        if tc + 2 < NT:
            sc.dma('sp', yc[b], Yd[(tc + 2) * 128:(tc + 3) * 128, :], ['Yd'], ['yc%d' % b])
        sc.dma('sp', out_d[rows, :], hb2[b], ['hb%d' % b], ['out'])
    sc.finalize()
    return nc


def _consts():
    c = {}
    c["ident_bf"] = np.eye(128, dtype=np.float32).astype(ml_dtypes.bfloat16)
    c["ident_f"] = np.eye(128, dtype=np.float32)
    half = 16
    inv = (10000.0 ** (-np.arange(half, dtype=np.float32) / half)).astype(np.float32)
    ang = np.arange(S, dtype=np.float32)[None, :] * inv[:, None]
    cos = np.cos(ang).astype(np.float32)
    sin = np.sin(ang).astype(np.float32)
    c["cos_t"] = np.concatenate([cos, cos], 0)
    c["sin_t"] = np.concatenate([-sin, sin], 0)
    k = np.arange(128)[:, None]
    qp = np.arange(384)[None, :]
    dist = np.abs(qp - 128 - k).astype(np.float32)
    slopes = 2.0 ** (-8.0 * (np.arange(8, dtype=np.float32) + 1.0) / 8)
    bw = np.where(dist[:, None, :] <= 128, -slopes[None, :, None] * dist[:, None, :], NEG).astype(np.float32)
    bws = (bw.astype(np.float64) / (64.0 ** -0.5)).astype(np.float32)
    hi = bws.astype(ml_dtypes.bfloat16)
    lo = (bws - hi.astype(np.float32)).astype(ml_dtypes.bfloat16)
    c["biasw_hi"] = np.ascontiguousarray(hi)
    c["biasw_lo"] = np.ascontiguousarray(lo)
    cols = np.arange(64)
    cstart = np.clip(cols - 8, 0, 48)
    valid = (cols[None, :] >= cstart[:, None]) & (cols[None, :] < cstart[:, None] + 16)
    m = np.where(valid.T, 0.0, NEG).astype(np.float32)
    m2 = np.concatenate([m, m], 0)
    c["namask"] = np.ascontiguousarray(np.broadcast_to(m2[:, None, :], (128, 16, 64)))
    c["cvals"] = (np.arange(4)[None, :] * 128 + np.arange(128)[:, None]).astype(np.float32)
    pi = np.arange(128)
    same = (pi[:, None] // 8) == (pi[None, :] // 8)
    c["gmat"] = same.astype(np.float32)
    c["lmat"] = (same & (pi[:, None] < pi[None, :])).astype(np.float32)
    return c


def _prep_shared(inp):
    f32 = np.float32
    d = {}
    w_in0 = np.asarray(inp["w_in0"], f32)
    qa = w_in0[:, 0:512]
    ka0, ka1 = w_in0[:, 512:576], w_in0[:, 576:640]
    vaw = w_in0[:, 640:768]
    cq = w_in0[:, 768:1152]
    ckv = w_in0[:, 1152:1408]
    kr = w_in0[:, 1408:1440]
    krs = np.concatenate([kr[:, 16:32], kr[:, 0:16]], 1)
    d["w0f"] = np.ascontiguousarray(np.concatenate(
        [qa, ka0, ka0, ka1, ka1, cq, ckv, kr, kr, kr, kr, krs, krs, krs, krs], 1))
    d["w0v"] = np.ascontiguousarray(vaw)
    d["a_sink"] = np.asarray(inp["a_sink"], f32).reshape(1, 8)
    d["gq"] = np.ascontiguousarray(np.asarray(inp["mla_q_norm"], f32).reshape(3, 128).T)
    d["gkv"] = np.ascontiguousarray(np.asarray(inp["mla_kv_norm"], f32).reshape(2, 128).T)
    wq = np.asarray(inp["w_q_up"], f32)
    d["wq"] = wq
    wq3 = wq.reshape(384, 8, 96)
    d["wqs"] = np.ascontiguousarray(
        np.concatenate([wq3[:, :, 0:64], wq3[:, :, 80:96], wq3[:, :, 64:80]], 2).reshape(384, 768))
    d["wkv"] = np.asarray(inp["w_kv_up"], f32)
    d["wo0"] = np.asarray(inp["w_out0"], f32)
    d["wo1"] = np.asarray(inp["w_out1"], f32)
    for n in ["ln0a_g", "ln0a_b", "ln0b_g", "ln0b_b", "ln1a_g", "ln1a_b", "ln1b_g", "ln1b_b"]:
        d[n] = np.asarray(inp[n], f32).reshape(1, D)
    for n in ["router0", "router1", "w_gate0", "w_gate1", "w_up0", "w_up1", "w_down0", "w_down1", "w_qkv1"]:
        d[n] = np.asarray(inp[n], f32)
    rpb = np.asarray(inp["na_rpb"], f32)
    cols = np.arange(64)
    dc = np.clip(cols[None, :] - cols[:, None] + 15, 0, 30)
    u = np.arange(16)
    out = np.empty((16, 2, 64, 16, 64), f32)
    for kr2 in range(2):
        dr = np.clip(u + kr2 - 8, -7, 7) + 7
        g_ = rpb[:, dr[:, None, None], dc[None, :, :]]
        out[:, kr2] = np.transpose(g_, (0, 3, 1, 2))
    d["rpbT"] = np.ascontiguousarray(out.reshape(16, 128, 16, 64))
    d.update(_consts())
    return d


_CACHE = {}


def kernel(**inputs):
    x = np.asarray(inputs["x"], np.float32)
    shared = _prep_shared(inputs)
    if "nc" not in _CACHE:
        _CACHE["nc"] = build_program()
    nc = _CACHE["nc"]
    in_maps = []
    for c in range(N_CORES):
        m = dict(shared)
        m["x"] = np.ascontiguousarray(x[c])
        in_maps.append(m)
    res = run_bass_kernel_spmd(nc, in_maps, core_ids=list(range(N_CORES)))
    return np.stack([np.asarray(r["out"], np.float32) for r in res.results], 0)
```

```python
import math
import numpy as np
import ml_dtypes
import concourse.bass as bass
import concourse.mybir as mybir
from concourse.bass_utils import run_bass_kernel_spmd

F32 = mybir.dt.float32
BF16 = mybir.dt.bfloat16
F16 = mybir.dt.float16
I32 = mybir.dt.int32
U8 = mybir.dt.uint8
ALU = mybir.AluOpType
AF = mybir.ActivationFunctionType
DSZ = {F32: 4, BF16: 2, F16: 2, I32: 4, U8: 1}

S = 4096
D = 1024
NT = 32
NEG = -1.0e30
ALPHA = 4.0 ** 0.25
N_CORES = 8
ENGS = ['pe', 'act', 'dve', 'pool', 'sp']
N_DSEM = 48


class Sched:
    def __init__(self, nc):
        self.nc = nc
        self.ops = []

    def add(self, eng, fn, r=(), w=(), dma=False):
        self.ops.append(dict(eng=eng, fn=fn, r=list(r), w=list(w), dma=dma, sig=dma, bar=False))

    def barrier(self):
        self.ops.append(dict(eng=None, bar=True))

    def mm(self, out, lhsT, rhs, start, stop, r, w):
        self.add('pe', lambda e: e.matmul(out, lhsT, rhs, start=start, stop=stop), r, w)

    def tr(self, out, in_, ident, r, w):
        self.add('pe', lambda e: e.transpose(out, in_, ident), r, w)

    def act(self, out, in_, func, r, w, bias=None, scale=None, accum=None):
        kw = {}
        if bias is not None:
            kw['bias'] = bias
        if scale is not None:
            kw['scale'] = scale
        if accum is not None:
            kw['accum_out'] = accum
        self.add('act', lambda e: e.activation(out, in_, func, **kw), r, w)

    def ts(self, eng, out, in0, s1, s2, op0, op1, r, w, accum=None):
        if accum is not None:
            self.add(eng, lambda e: e.tensor_scalar(out, in0, s1, s2, op0, op1, accum_out=accum), r, w)
        elif op1 is None:
            self.add(eng, lambda e: e.tensor_scalar(out, in0, s1, None, op0), r, w)
        else:
            self.add(eng, lambda e: e.tensor_scalar(out, in0, s1, s2, op0, op1), r, w)

    def tt(self, eng, out, in0, in1, op, r, w):
        self.add(eng, lambda e: e.tensor_tensor(out, in0, in1, op), r, w)

    def stt(self, eng, out, in0, scalar, in1, op0, op1, r, w):
        self.add(eng, lambda e: e.scalar_tensor_tensor(out, in0, scalar, in1, op0, op1), r, w)

    def copy(self, eng, out, in_, r, w):
        if eng == 'act':
            self.add('act', lambda e: e.copy(out, in_), r, w)
        else:
            self.add(eng, lambda e: e.tensor_copy(out, in_), r, w)

    def recip(self, out, in_, r, w):
        self.add('dve', lambda e: e.reciprocal(out, in_), r, w)

    def memset(self, eng, ap, val, w):
        self.add(eng, lambda e: e.memset(ap, val), (), w)

    def dma(self, eng, out, in_, r, w, **kw):
        self.add(eng, lambda e: e.dma_start(out=out, in_=in_, **kw), r, w, dma=True)

    def idma(self, out, out_off, in_, in_off, r, w, **kw):
        self.add('pool', lambda e: e.indirect_dma_start(out=out, out_offset=out_off, in_=in_,
                                                        in_offset=in_off, **kw), r, w, dma=True)

    def finalize(self):
        nc = self.nc
        ops = self.ops
        last_w, readers = {}, {}
        eng_last = {e: None for e in ENGS}
        outstanding = []
        for i, op in enumerate(ops):
            if op['bar']:
                op['deps'] = [v for v in eng_last.values() if v is not None] + outstanding
                outstanding = []
                last_w.clear()
                readers.clear()
                for d in op['deps']:
                    ops[d]['sig'] = True
                continue
            deps = {}
            for k in op['r']:
                if k in last_w:
                    deps[last_w[k]] = 'raw'
            for k in op['w']:
                if k in last_w:
                    deps.setdefault(last_w[k], 'waw')
                for rr in readers.get(k, ()):
                    deps.setdefault(rr, 'war')
            deps.pop(i, None)
            keep = []
            for d, kind in deps.items():
                p = ops[d]
                if p['eng'] == op['eng'] and not p['dma']:
                    if op['eng'] == 'pe':
                        continue
                keep.append(d)
            op['deps'] = keep
            for d in keep:
                ops[d]['sig'] = True
            for k in op['r']:
                lst = readers.setdefault(k, [])
                if not op['dma']:
                    lst[:] = [q for q in lst if ops[q]['dma'] or ops[q]['eng'] != op['eng']]
                lst.append(i)
            for k in op['w']:
                last_w[k] = i
                readers[k] = []
            if op['dma']:
                outstanding.append(i)
            else:
                eng_last[op['eng']] = i
        cnt = {e: 0 for e in ENGS}
        dcnt = [0] * N_DSEM
        ndq = [0, 0]
        for op in ops:
            if op['bar']:
                continue
            if op['dma']:
                half = N_DSEM // 2
                qi = 0 if op['eng'] == 'sp' else 1
                d = qi * half + (ndq[qi] % half)
                ndq[qi] += 1
                op['dsem'] = d
                op['dprev'] = dcnt[d]
                dcnt[d] += 16
                op['dval'] = dcnt[d]
            elif op['sig']:
                cnt[op['eng']] += 1
                op['sval'] = cnt[op['eng']]
        import contextlib
        with contextlib.ExitStack() as st:
            esem = {e: st.enter_context(nc.semaphore("s_" + e)) for e in ENGS}
            dsem = [st.enter_context(nc.semaphore("d_%d" % i)) for i in range(N_DSEM)]
            block = st.enter_context(nc.Block())

            def waits_for(deps):
                need = {}
                for d in deps:
                    p = ops[d]
                    if p['dma']:
                        key, val = ('d', p['dsem']), p['dval']
                    else:
                        key, val = ('e', p['eng']), p['sval']
                    if need.get(key, 0) < val:
                        need[key] = val
                return need

            def emit(eng, e):
                seen = {}

                def do_waits(need):
                    for key, val in need.items():
                        if seen.get(key, 0) >= val:
                            continue
                        seen[key] = val
                        sem = dsem[key[1]] if key[0] == 'd' else esem[key[1]]
                        e.wait_ge(sem, val)

                for op in ops:
                    if op['bar']:
                        do_waits(waits_for(op['deps']))
                        continue
                    if op['eng'] != eng:
                        continue
                    need = waits_for(op['deps'])
                    if op['dma'] and op['dprev'] > 0:
                        key = ('d', op['dsem'])
                        if need.get(key, 0) < op['dprev']:
                            need[key] = op['dprev']
                    do_waits(need)
                    ins = op['fn'](e)
                    if op['dma']:
                        ins.then_inc(dsem[op['dsem']], 16)
                    elif op['sig']:
                        ins.then_inc(esem[eng], 1)
                if eng == 'sp':
                    for d in range(N_DSEM):
                        if dcnt[d] > 0 and seen.get(('d', d), 0) < dcnt[d]:
                            e.wait_ge(dsem[d], dcnt[d])
                    for en in ENGS:
                        if cnt[en] > 0 and seen.get(('e', en), 0) < cnt[en]:
                            e.wait_ge(esem[en], cnt[en])

            @block.tensor
            def _(e):
                emit('pe', e)

            @block.scalar
            def _(e):
                emit('act', e)

            @block.vector
            def _(e):
                emit('dve', e)

            @block.gpsimd
            def _(e):
                emit('pool', e)

            @block.sync
            def _(e):
                emit('sp', e)


class Arena:
    def __init__(self, nc, nbytes):
        self.t = nc.alloc_sbuf_tensor("arena", [128, nbytes], U8)
        self.cap = nbytes
        self.off = 0

    def alloc(self, shape, dtype):
        n = int(np.prod(shape)) * DSZ[dtype]
        n_al = (n + 63) // 64 * 64
        assert self.off + n_al <= self.cap, ("arena overflow", self.off, n_al, self.cap)
        ap = self.t[:, self.off:self.off + n].bitcast(dtype)
        self.off += n_al
        if len(shape) == 2:
            ap = ap.rearrange("p (a b) -> p a b", a=shape[0])
        elif len(shape) == 3:
            ap = ap.rearrange("p (a b c) -> p a b c", a=shape[0], b=shape[1])
        return ap

    def view(self, off, shape, dtype):
        n = int(np.prod(shape)) * DSZ[dtype]
        assert off + n <= self.cap
        ap = self.t[:, off:off + n].bitcast(dtype)
        if len(shape) == 2:
            ap = ap.rearrange("p (a b) -> p a b", a=shape[0])
        return ap

    def mark(self):
        return self.off

    def release(self, m):
        self.off = m


def build_program(stop_after=None, debug=False):
    nc = bass.Bass("TRN2", target_bir_lowering=False)
    sc = Sched(nc)

    def din(name, shape, dt=F32):
        return nc.dram_tensor(name, list(shape), dt, kind="ExternalInput").ap()

    def dscr(name, shape, dt):
        return nc.dram_tensor(name, list(shape), dt, kind="Internal").ap()

    x = din("x", [S, D])
    w0f = din("w0f", [D, 1664])
    w0v = din("w0v", [D, 128])
    a_sink = din("a_sink", [1, 8])
    gq_d = din("gq", [128, 3])
    gkv_d = din("gkv", [128, 2])
    wq_d = din("wq", [384, 768])
    wqs_d = din("wqs", [384, 768])
    wkv_d = din("wkv", [256, 1024])
    wo_d = [din("wo0", [D, D]), din("wo1", [D, D])]
    lnp = {n: din(n, [1, D]) for n in
           ["ln0a_g", "ln0a_b", "ln0b_g", "ln0b_b", "ln1a_g", "ln1a_b", "ln1b_g", "ln1b_b"]}
    router_d = [din("router0", [D, 16]), din("router1", [D, 16])]
    NE_ = 1 if stop_after in ('INIT', 'L0A', 'L0W', 'L0M', 'L0O') else 16
    wg_d = [din("w_gate0", [NE_, D, 2048]), din("w_gate1", [NE_, D, 2048])]
    wu_d = [din("w_up0", [NE_, D, 2048]), din("w_up1", [NE_, D, 2048])]
    wd_d = [din("w_down0", [NE_, 2048, D]), din("w_down1", [NE_, 2048, D])]
    wqkv_d = din("w_qkv1", [D, 3072])
    rpbT_d = din("rpbT", [16, 128, 16, 64])
    identb_d = din("ident_bf", [128, 128], BF16)
    identf_d = din("ident_f", [128, 128])
    cos_d = din("cos_t", [32, S])
    sin_d = din("sin_t", [32, S])
    biasw_hi_d = din("biasw_hi", [128, 8, 384], BF16)
    biasw_lo_d = din("biasw_lo", [128, 8, 384], BF16)
    namask_d = din("namask", [128, 16, 64])
    cvals_d = din("cvals", [128, 4])
    gmat_d = din("gmat", [128, 128])
    lmat_d = din("lmat", [128, 128])
    out_d = nc.dram_tensor("out", [S, D], F32, kind="ExternalOutput").ap()

    Hb = dscr("Hb", [S, D], BF16)
    Yd = dscr("Yd", [S, D], F32)
    R1 = dscr("R1", [S, D], F32)
    affd = dscr("affd", [S, 16], F32)
    cumd = dscr("cumd", [16, S], F16)
    affTd = dscr("affTd", [16, S], F32)
    mixd = dscr("mixd", [8, 128, S], BF16)
    dbg = {}
    if debug:
        for n, shp, dt in [("dbg_h1", [S, D], F32), ("dbg_mix", [8, 128, S], F32), ("dbg_aff", [S, 16], F32),
                           ("dbg_y", [S, D], F32), ("dbg_cum", [16, S], F32)]:
            dbg[n] = nc.dram_tensor(n, shp, dt, kind="ExternalOutput").ap()

    ar = Arena(nc, 200 * 1024)
    psb = [nc.alloc_psum_tensor("psb%d" % i, [128, 512], F32) for i in range(8)]

    def PS(b):
        return psb[b][:, :]

    def PSB(b):
        return psb[b][:, :].bitcast(BF16)

    def pk(b):
        return "ps%d" % b

    identb = ar.alloc([128], BF16)
    identf = ar.alloc([128], F32)
    onesb = ar.alloc([128], BF16)
    esink = ar.alloc([8], F32)
    cvals = ar.alloc([4], F32)
    eps_rms = ar.alloc([1], F32)
    eps_ln = ar.alloc([1], F32)
    afft = ar.alloc([NT, 16], F32)
    sc.dma('sp', identb, identb_d, (), ['identb'])
    sc.dma('sp', identf, identf_d, (), ['identf'])
    sc.dma('sp', cvals, cvals_d, (), ['cvals'])
    sc.dma('sp', esink, a_sink.partition_broadcast(128), (), ['esink'])
    sc.memset('pool', onesb, 1.0, ['onesb'])
    sc.memset('pool', eps_rms, 1e-6, ['eps'])
    sc.memset('pool', eps_ln, 1e-5, ['eps'])
    sc.act(esink, esink, AF.Exp, ['esink'], ['esink'])
    base_mark = ar.mark()
    if stop_after == 'INIT':
        sc.finalize()
        return nc

    def load_bcast(dst, name, key):
        sc.dma('sp', dst, lnp[name].partition_broadcast(128), (), [key])

    def ln_stats(src, srck, st, mv, sd, sfx=''):
        for hh in range(2):
            sc.add('dve', (lambda a, b: (lambda e: e.bn_stats(a, b)))(st[:, hh, :], src[:, hh * 512:(hh + 1) * 512]),
                   [srck], ['lnst' + sfx])
        sc.add('dve', lambda e: e.bn_aggr(mv, st.rearrange("p a b -> p (a b)")), ['lnst' + sfx], ['lnmv' + sfx])
        sc.act(sd, mv[:, 1:2], AF.Ln, ['lnmv' + sfx, 'eps'], ['lnsd' + sfx], bias=eps_ln)
        sc.act(sd, sd, AF.Exp, ['lnsd' + sfx], ['lnsd' + sfx], scale=-0.5)

    def ln_apply(src, srck, gB, bB, dst, dstk, mv, sd, sfx=''):
        sc.stt('dve', dst, src, mv[:, 0:1], gB, ALU.subtract, ALU.mult, [srck, 'lnmv' + sfx, 'lng'], [dstk])
        sc.stt('dve', dst, dst, sd[:, 0:1], bB, ALU.mult, ALU.add, [dstk, 'lnsd' + sfx, 'lnb'], [dstk])

    def ln_chunk(src, srck, gB, bB, dst, dstk, st, mv, sd, sfx='', pool_affine=True):
        ln_stats(src, srck, st, mv, sd, sfx)
        ln_apply(src, srck, gB, bB, dst, dstk, mv, sd, sfx)

    def post_h_moe(h, hk, tc, layer, bufs, sfx=''):
        ahb, hbf, hT, ex, ssum, rtr, affT = bufs
        rows = slice(tc * 128, (tc + 1) * 128)
        sc.act(ahb, h, AF.Copy, [hk], ['ahb' + sfx], scale=ALPHA)
        sc.dma('sp', Yd[rows, :], ahb, ['ahb' + sfx], ['Yd'])
        sc.copy('act', hbf, h, [hk], ['hbf' + sfx])
        sc.dma('sp', Hb[rows, :], hbf, ['hbf' + sfx], ['Hb'])
        tb = 4 + (tc % 2)
        for k in range(8):
            sc.tr(PSB(tb)[:, k * 128:(k + 1) * 128], hbf[:, k * 128:(k + 1) * 128], identb, ['hbf' + sfx, 'identb'], [pk(tb)])
        sc.copy('act', hT, PSB(tb).rearrange("p (k t) -> p k t", k=8), [pk(tb)], ['hT' + sfx])
        for k in range(8):
            sc.mm(PS(6)[:, 0:16], hT[:, k, :], rtr[:, k, :], k == 0, k == 7, ['hT' + sfx, 'rtr'], [pk(6)])
        sc.act(ex, PS(6)[:, 0:16], AF.Exp, [pk(6)], ['ex' + sfx, 'ssum' + sfx], accum=ssum)
        sc.recip(ssum, ssum, ['ssum' + sfx], ['ssum' + sfx])
        sc.ts('dve', afft[:, tc, :], ex, ssum[:, 0:1], None, ALU.mult, None, ['ex' + sfx, 'ssum' + sfx], ['afft%d' % tc])
        sc.tr(PS(7)[0:16, (tc % 4) * 128:(tc % 4 + 1) * 128], afft[:, tc, :], identf, ['afft%d' % tc, 'identf'], [pk(7)])
        if tc % 4 == 3:
            g4 = tc // 4
            sc.copy('act', affT[0:16, g4 * 512:(g4 + 1) * 512], PS(7)[0:16, :], [pk(7)], ['affT'])

    def moe_route(affT, rbufs):
        aff128, junk, ones1, mask, scan, cum, gmat, lmat, lo, mid, cntp, step, offs = rbufs
        sc.dma('sp', affTd, affT, ['affT'], ['affTd'])
        sc.dma('sp', aff128, affTd.rearrange("e (s c) -> (e s) c", s=8), ['affTd'], ['aff128'])
        sc.dma('sp', gmat, gmat_d, (), ['gmat'])
        sc.dma('sp', lmat, lmat_d, (), ['lmat'])
        sc.memset('pool', ones1, 1.0, ['ones1'])
        sc.memset('dve', lo, 0.0, ['lo'])
        for it in range(27):
            wk = 2.0 ** -(it + 1)
            pb = 5 + (it % 2)
            sc.ts('dve', mid, lo, wk, None, ALU.add, None, ['lo'], ['mid'])
            sc.ts('dve', junk, aff128, mid[:, 0:1], 0.0, ALU.is_ge, ALU.add, ['aff128', 'mid'], ['junk', 'cntp'], accum=cntp)
            sc.mm(PS(pb)[:, 0:1], gmat, cntp, True, True, ['gmat', 'cntp'], [pk(pb)])
            sc.ts('dve', step, PS(pb)[:, 0:1], 511.5, wk, ALU.is_ge, ALU.mult, [pk(pb)], ['step'])
            sc.tt('dve', lo, lo, step, ALU.add, ['lo', 'step'], ['lo'])
        sc.ts('dve', mask, aff128, lo[:, 0:1], 0.0, ALU.is_ge, ALU.add, ['aff128', 'lo'], ['mask', 'cntp'], accum=cntp)
        sc.add('dve', lambda e: e.tensor_tensor_scan(scan, ones1, mask, 0.0, ALU.mult, ALU.add),
               ['ones1', 'mask'], ['scan'])
        sc.mm(PS(7)[:, 0:1], lmat, cntp, True, True, ['lmat', 'cntp'], [pk(7)])
        sc.copy('dve', offs, PS(7)[:, 0:1], [pk(7)], ['offs'])
        sc.ts('dve', cum, scan, offs[:, 0:1], None, ALU.add, None, ['scan', 'offs'], ['cum'])
        sc.dma('sp', cumd.rearrange("e (s c) -> (e s) c", s=8), cum, ['cum'], ['cumd'])
        if debug:
            sc.dma('sp', dbg["dbg_cum"].rearrange("e (s c) -> (e s) c", s=8), mask, ['mask'], ['dbg_cum'])

    def moe_experts(layer, mb):
        (wring, cumB, idxf, idxi, xg, gg, xgT, hidT, ysb, sg, junkc) = mb
        NSL = len(wring)
        pieces = []
        for e in range(16):
            for fq in range(4):
                pieces.append((e, 'gu', fq))
            for dh in range(2):
                pieces.append((e, 'd', dh))
        state = dict(next_load=0)

        def load_piece(n):
            e, kind, q = pieces[n]
            slot = wring[n % NSL]
            key = 'w%d' % (n % NSL)
            if kind == 'gu':
                sc.dma('pool', slot[:, 0:8, :], wg_d[layer][e][:, q * 512:(q + 1) * 512].rearrange("(k p) f -> p k f", p=128),
                       (), [key])
                sc.dma('pool', slot[:, 8:16, :], wu_d[layer][e][:, q * 512:(q + 1) * 512].rearrange("(k p) f -> p k f", p=128),
                       (), [key])
            else:
                sc.dma('pool', slot, wd_d[layer][e][:, q * 512:(q + 1) * 512].rearrange("(f p) d -> p f d", p=128),
                       (), [key])

        def prefetch(upto):
            while state['next_load'] < min(upto, len(pieces)):
                load_piece(state['next_load'])
                state['next_load'] += 1

        def prep_a(e):
            b = e % 2
            b3 = e % 4
            g3 = e % 3
            sc.dma('sp', cumB[b], cumd[e:e + 1, :].partition_broadcast(128), ['cumd'], ['cumB%d' % b])
            for j in range(4):
                sc.ts('dve', junkc, cumB[b], cvals[:, j:j + 1], 0.0, ALU.is_le, ALU.add,
                      ['cumB%d' % b, 'cvals'], ['junkc', 'idxf%d' % b3], accum=idxf[b3][:, j:j + 1])
            sc.ts('dve', idxf[b3], idxf[b3], float(S - 1), None, ALU.min, None, ['idxf%d' % b3], ['idxf%d' % b3])
            sc.copy('dve', idxi[b3], idxf[b3], ['idxf%d' % b3], ['idxi%d' % b3])

        def prep_a2(e, js=(0, 1, 2, 3)):
            b = e % 2
            b3 = e % 4
            g3 = e % 3
            for j in js:
                sc.idma(xg[b][:, j, :], None, Hb[:, :], bass.IndirectOffsetOnAxis(ap=idxi[b3][:, j:j + 1], axis=0),
                        ['idxi%d' % b3, 'Hb'], ['xg%d' % b])
                sc.idma(gg[g3][:, j, :], None, affd[:, :], bass.IndirectOffsetOnAxis(ap=idxi[b3][:, j:j + 1], axis=0),
                        ['idxi%d' % b3, 'affd'], ['gg%d' % g3])

        def prep_b(e):
            b = e % 2
            for j in range(4):
                tb = 0 if j % 2 == 0 else 7
                for k in range(8):
                    sc.tr(PSB(tb)[:, k * 128:(k + 1) * 128], xg[b][:, j, k * 128:(k + 1) * 128], identb,
                          ['xg%d' % b, 'identb'], [pk(tb)])
                sc.copy('act' if j % 2 == 0 else 'dve', xgT[b][:, :, j * 128:(j + 1) * 128],
                        PSB(tb).rearrange("p (k t) -> p k t", k=8), [pk(tb)], ['xgT%d' % b])

        prefetch(NSL)
        prep_a(0)
        prep_a2(0)
        prep_a(1)
        prep_a2(1)
        prep_b(0)
        n = 0
        for e in range(16):
            b = e % 2
            b3 = e % 4
            g3 = e % 3
            if e + 2 < 16:
                prep_a(e + 2)
            for fq in range(4):
                if fq == 3 and e + 1 < 16:
                    prep_b(e + 1)
                slot = wring[n % NSL]
                wkey = 'w%d' % (n % NSL)
                for fcl in range(4):
                    fc = fq * 4 + fcl
                    bg, bu = 1 + (fc % 2), 3 + (fc % 2)
                    for k in range(8):
                        sc.mm(PS(bg), slot[:, k, fcl * 128:(fcl + 1) * 128], xgT[b][:, k, :], k == 0, k == 7,
                              [wkey, 'xgT%d' % b], [pk(bg)])
                    for k in range(8):
                        sc.mm(PS(bu), slot[:, 8 + k, fcl * 128:(fcl + 1) * 128], xgT[b][:, k, :], k == 0, k == 7,
                              [wkey, 'xgT%d' % b], [pk(bu)])
                    sc.act(sg[fc % 2], PS(bg), AF.Silu, [pk(bg)], ['sg%d' % (fc % 2)])
                    sc.tt('dve', hidT[:, fc, :], sg[fc % 2], PS(bu), ALU.mult, ['sg%d' % (fc % 2), pk(bu)], ['hid%d' % fc])
                n += 1
                prefetch(n + NSL)
                if fq >= 1 and e + 2 < 16:
                    prep_a2(e + 2, (fq - 1,))
            for dh in range(2):
                slot = wring[n % NSL]
                wkey = 'w%d' % (n % NSL)
                for j in range(4):
                    bd = 5 + (j % 2)
                    for fc in range(16):
                        sc.mm(PS(bd), hidT[:, fc, j * 128:(j + 1) * 128], slot[:, fc, :], fc == 0, fc == 15,
                              ['hid%d' % fc, wkey], [pk(bd)])
                    sc.act(ysb[:, j, dh * 512:(dh + 1) * 512], PS(bd), AF.Copy, [pk(bd), 'gg%d' % g3], ['ysb%d' % j],
                           scale=gg[g3][:, j, e:e + 1])
                n += 1
                prefetch(n + NSL)
                if dh == 0 and e + 2 < 16:
                    prep_a2(e + 2, (3,))
            for j in range(4):
                sc.idma(Yd[:, :], bass.IndirectOffsetOnAxis(ap=idxi[b3][:, j:j + 1], axis=0), ysb[:, j, :], None,
                        ['ysb%d' % j, 'idxi%d' % b3] + ['Yd_%d_%d' % ((e - 1) % 2, jj) for jj in range(4)],
                        ['Yd_%d_%d' % (e % 2, j)], compute_op=ALU.add)

    def moe_phase(layer, affT):
        m0 = ar.mark()
        wring = [ar.alloc([16, 512], BF16) for _ in range(6)]
        m1 = ar.mark()
        aff128 = ar.alloc([512], F32)
        junk = ar.alloc([512], BF16)
        ones1 = ar.alloc([512], F32)
        mask = ar.alloc([512], F32)
        scan = ar.alloc([512], F32)
        cum = ar.alloc([512], F16)
        gmat = ar.alloc([128], F32)
        lmat = ar.alloc([128], F32)
        smalls = [ar.alloc([1], F32) for _ in range(5)]
        rb = (aff128, junk, ones1, mask, scan, cum, gmat, lmat) + tuple(smalls)
        assert ar.off <= 184 * 1024
        moe_route(affT[0:16], rb)
        sc.barrier()
        ar.release(m1)
        cumB = [ar.alloc([S], F16) for _ in range(2)]
        idxf = [ar.alloc([4], F32) for _ in range(4)]
        idxi = [ar.alloc([4], I32) for _ in range(4)]
        xg = [ar.alloc([4, D], BF16) for _ in range(2)]
        gg = [ar.alloc([4, 16], F32) for _ in range(3)]
        xgT = [ar.alloc([8, 512], BF16) for _ in range(2)]
        hidT = ar.alloc([16, 512], BF16)
        ysb = ar.alloc([4, D], F32)
        sg = [ar.alloc([512], F32) for _ in range(2)]
        junkc = ar.alloc([S], BF16)
        moe_experts(layer, (wring, cumB, idxf, idxi, xg, gg, xgT, hidT, ysb, sg, junkc))
        sc.barrier()
        ar.release(m0)

    def attn_norm_store(po_bank, nq, esk, mixrow, rd, mixt):
        if esk is not None:
            sc.ts('dve', rd[64:128, 0:nq], PS(po_bank)[64:128, 0:nq], esk, None, ALU.add, None,
                  [pk(po_bank), 'esink'], ['rdh'])
        else:
            sc.copy('dve', rd[64:128, 0:nq], PS(po_bank)[64:128, 0:nq], [pk(po_bank)], ['rdh'])
        sc.ts('dve', rd[0:64, 0:nq], rd[64:128, 0:nq], 1.0, None, ALU.mult, None, ['rdh'], ['rd'])
        sc.recip(rd[0:64, 0:nq], rd[0:64, 0:nq], ['rd'], ['rd'])
        sc.tt('dve', mixt[0:64, 0:nq], PS(po_bank)[0:64, 0:nq], rd[0:64, 0:nq], ALU.mult, [pk(po_bank), 'rd'], ['mixt'])
        sc.dma('sp', mixrow, mixt[0:64, 0:nq], ['mixt'], ['mixd'])

    def outproj_ln_phase(layer, wo_ap, gname, bname, resid_load, resid_fn):
        m0 = ar.mark()
        wo = ar.alloc([8, D], BF16)
        gB = ar.alloc([D], F32)
        bB = ar.alloc([D], F32)
        rtr = ar.alloc([8, 16], BF16)
        affT = ar.view(184 * 1024, [S], F32)
        mt = [ar.alloc([8, 128], BF16) for _ in range(2)]
        xc = [ar.alloc([D], F32) for _ in range(2)]
        rbuf = [ar.alloc([D], F32) for _ in range(2)]
        hbuf = [ar.alloc([D], F32) for _ in range(2)]
        ahb = [ar.alloc([D], F32) for _ in range(2)]
        hbf = [ar.alloc([D], BF16) for _ in range(2)]
        hT = [ar.alloc([8, 128], BF16) for _ in range(2)]
        ex = [ar.alloc([16], F32) for _ in range(2)]
        ssum = [ar.alloc([1], F32) for _ in range(2)]
        st = [ar.alloc([2, 6], F32) for _ in range(2)]
        mv = [ar.alloc([2], F32) for _ in range(2)]
        sd = [ar.alloc([1], F32) for _ in range(2)]
        sc.dma('pool', wo, wo_ap.rearrange("(k p) d -> p k d", p=128), (), ['wo'])
        sc.dma('pool', rtr, router_d[layer].rearrange("(k p) e -> p k e", p=128), (), ['rtr'])
        load_bcast(gB, gname, 'lng')
        load_bcast(bB, bname, 'lnb')
        def loads(tc):
            b = tc % 2
            sc.dma('sp', mt[b], mixd[:, :, tc * 128:(tc + 1) * 128].rearrange("f p t -> p f t"), ['mixd'], ['mt%d' % b])
            resid_load(tc, xc[b], 'xc%d' % b)

        def banks_of(tc):
            return (0, 1) if tc % 2 == 0 else (2, 3)

        def mms(tc):
            b = tc % 2
            for half, bank in zip((0, 1), banks_of(tc)):
                for f in range(8):
                    sc.mm(PS(bank), mt[b][:, f, :], wo[:, f, half * 512:(half + 1) * 512], f == 0, f == 7,
                          ['mt%d' % b, 'wo'], [pk(bank)])

        def resid(tc):
            b = tc % 2
            pa, pb_ = banks_of(tc)
            resid_fn(tc, pa, pb_, rbuf[b], 'rbuf%d' % b, xc[b], 'xc%d' % b)

        def stats(tc):
            b = tc % 2
            ln_stats(rbuf[b], 'rbuf%d' % b, st[b], mv[b], sd[b], str(b))

        loads(0)
        loads(1)
        mms(0)
        mms(1)
        resid(0)
        stats(0)
        for tc in range(NT):
            b = tc % 2
            sf = str(b)
            if tc + 2 < NT:
                loads(tc + 2)
                mms(tc + 2)
            if tc + 1 < NT:
                resid(tc + 1)
                stats(tc + 1)
            ln_apply(rbuf[b], 'rbuf' + sf, gB, bB, hbuf[b], 'hbuf' + sf, mv[b], sd[b], sf)
            post_h_moe(hbuf[b], 'hbuf' + sf, tc, layer, (ahb[b], hbf[b], hT[b], ex[b], ssum[b], rtr, affT), sf)
            if debug and layer == 0:
                sc.dma('sp', dbg["dbg_h1"][tc * 128:(tc + 1) * 128, :], hbuf[b], ['hbuf' + sf], ['dbg_h1'])
        sc.dma('sp', affd.rearrange("(c p) e -> p c e", p=128), afft, ['afft%d' % t for t in range(NT)], ['affd'])
        if debug and layer == 0:
            sc.dma('sp', dbg["dbg_aff"].rearrange("(c p) e -> p c e", p=128), afft, ['afft%d' % t for t in range(NT)], ['dbg_aff'])
        sc.barrier()
        ar.release(m0)
        return affT, m0

    qaT = ar.alloc([4, S], BF16)
    kaT2 = ar.alloc([2, S], BF16)
    va = ar.alloc([NT, 2, 128], BF16)
    cqn = ar.alloc([3, S], BF16)
    ckvn = ar.alloc([2, S], BF16)
    KT = [ar.alloc([S], BF16) for _ in range(2)]
    mA = ar.mark()
    w0 = ar.alloc([8, 1792], BF16)
    xb = ar.alloc([4, D], BF16)
    xT = ar.alloc([8, 512], BF16)
    cqg = ar.alloc([3, 512], F32)
    ckg = ar.alloc([2, 512], F32)
    sq = ar.alloc([5, 512], BF16)
    rq = ar.alloc([512], F32)
    rk = ar.alloc([512], F32)
    cs = ar.alloc([512], F32)
    sn = ar.alloc([512], F32)
    t1 = ar.alloc([512], F32)
    t2 = ar.alloc([512], F32)
    gq = ar.alloc([3], F32)
    gkv = ar.alloc([2], F32)
    sc.dma('pool', w0[:, :, 0:1664], w0f.rearrange("(k p) f -> p k f", p=128), (), ['w0'])
    sc.dma('pool', w0[:, :, 1664:1792], w0v.rearrange("(k p) f -> p k f", p=128), (), ['w0'])
    sc.dma('sp', gq, gq_d, (), ['gq'])
    sc.dma('sp', gkv, gkv_d, (), ['gkv'])
    sc.memset('pool', va[:, :, :, 64:128], 1.0, ['va_ones'])
    for i in range(8):
        tl = slice(i * 512, (i + 1) * 512)
        sc.dma('pool', xb, x[tl, :].rearrange("(c p) d -> p c d", p=128), (), ['xb'])
        sc.dma('sp', cs[64:96, :], cos_d[:, tl], (), ['cs'])
        sc.dma('sp', sn[64:96, :], sin_d[:, tl], (), ['sn'])
        for c in range(4):
            tb = c % 2
            for k in range(8):
                sc.tr(PSB(tb)[:, k * 128:(k + 1) * 128], xb[:, c, k * 128:(k + 1) * 128], identb, ['xb', 'identb'], [pk(tb)])
            sc.copy('act' if c % 2 == 0 else 'dve', xT[:, :, c * 128:(c + 1) * 128],
                    PSB(tb).rearrange("p (k t) -> p k t", k=8), [pk(tb)], ['xT'])
        for f in range(13):
            bank = 2 + (f % 3)
            for k in range(8):
                sc.mm(PS(bank), w0[:, k, f * 128:(f + 1) * 128], xT[:, k, :], k == 0, k == 7, ['w0', 'xT'], [pk(bank)])
            if f < 4:
                sc.copy('act', qaT[:, f, tl], PS(bank), [pk(bank)], ['qaT'])
            elif f < 6:
                sc.copy('dve', kaT2[:, f - 4, tl], PS(bank), [pk(bank)], ['kaT2'])
            elif f < 9:
                r = f - 6
                sc.act(sq[:, r, :], PS(bank), AF.Square, [pk(bank)], ['sq%d' % r, 'psord'])
                sc.ts('dve', cqg[:, r, :], PS(bank), gq[:, r:r + 1], None, ALU.mult, None, [pk(bank), 'gq', 'psord'], ['cqg%d' % r])
            elif f < 11:
                r = f - 9
                sc.act(sq[:, 3 + r, :], PS(bank), AF.Square, [pk(bank)], ['sq%d' % (3 + r), 'psord'])
                sc.ts('dve', ckg[:, r, :], PS(bank), gkv[:, r:r + 1], None, ALU.mult, None, [pk(bank), 'gkv', 'psord'], ['ckg%d' % r])
            elif f == 11:
                sc.tt('dve', t1[64:96, :], PS(bank)[64:96, :], cs[64:96, :], ALU.mult, [pk(bank), 'cs'], ['t1'])
            else:
                sc.tt('dve', t2[64:96, :], PS(bank)[64:96, :], sn[64:96, :], ALU.mult, [pk(bank), 'sn'], ['t2'])
                sc.tt('pool', KT[0][64:96, tl], t1[64:96, :], t2[64:96, :], ALU.add, ['t1', 't2'], ['KT0pe'])
                sc.copy('pool', KT[1][64:96, tl], KT[0][64:96, tl], ['KT0pe'], ['KT1pe'])
        for r in range(3):
            sc.mm(PS(5), onesb, sq[:, r, :], r == 0, r == 2, ['onesb', 'sq%d' % r], [pk(5)])
        for r in range(2):
            sc.mm(PS(6), onesb, sq[:, 3 + r, :], r == 0, r == 1, ['onesb', 'sq%d' % (3 + r)], [pk(6)])
        sc.act(rq, PS(5), AF.Sqrt, [pk(5), 'eps'], ['rq'], bias=eps_rms, scale=1.0 / 384.0)
        sc.recip(rq, rq, ['rq'], ['rq'])
        sc.act(rk, PS(6), AF.Sqrt, [pk(6), 'eps'], ['rk'], bias=eps_rms, scale=1.0 / 256.0)
        sc.recip(rk, rk, ['rk'], ['rk'])
        for r in range(3):
            sc.tt('dve', cqn[:, r, tl], cqg[:, r, :], rq, ALU.mult, ['cqg%d' % r, 'rq'], ['cqn'])
        for r in range(2):
            sc.tt('dve', ckvn[:, r, tl], ckg[:, r, :], rk, ALU.mult, ['ckg%d' % r, 'rk'], ['ckvn'])
        for c in range(4):
            for k in range(8):
                sc.mm(PS(7)[:, c * 128:(c + 1) * 128], xT[:, k, c * 128:(c + 1) * 128], w0[:, k, 1664:1792],
                      k == 0, k == 7, ['xT', 'w0'], [pk(7)])
        sc.copy('act', va[:, i * 4:(i + 1) * 4, :, 0:64], PS(7).rearrange("p (c g d) -> p c g d", c=4, g=2),
                [pk(7)], ['va'])
    sc.barrier()
    ar.release(mA)
    if stop_after == 'L0A':
        sc.finalize()
        return nc

    mW = ar.mark()
    bhi = ar.alloc([8, 384], BF16)
    blo = ar.alloc([8, 384], BF16)
    pT = [ar.alloc([384], BF16) for _ in range(4)]
    rd = ar.alloc([512], F32)
    mixt = ar.alloc([512], BF16)
    sc.dma('sp', bhi, biasw_hi_d, (), ['biasw'])
    sc.dma('sp', blo, biasw_lo_d, (), ['biasw'])
    scale_a = 64.0 ** -0.5
    for h in range(8):
        g = h // 4
        f = h // 2
        rbs = (h % 2) * 64
        pr = slice(rbs, rbs + 64)

        def q0_of(j):
            return max(0, (j - 1) * 128)

        def s_step(j):
            q0 = q0_of(j)
            q1 = min(S, (j + 2) * 128)
            n = q1 - q0
            off = q0 - (j - 1) * 128
            bank = j % 3
            sc.mm(PS(bank)[:, 0:n], identb, bhi[:, h, off:off + n], True, False, ['identb', 'biasw'], [pk(bank)])
            sc.mm(PS(bank)[:, 0:n], identb, blo[:, h, off:off + n], False, False, ['identb', 'biasw'], [pk(bank)])
            sc.mm(PS(bank)[:, 0:n], kaT2[pr, g, j * 128:(j + 1) * 128], qaT[pr, f, q0:q1], False, True,
                  ['kaT2', 'qaT'], [pk(bank)])
            sc.act(pT[j % 4][:, 0:n], PS(bank)[:, 0:n], AF.Exp, [pk(bank)], ['pT%d' % (j % 4)], scale=scale_a)

        def pv_step(i):
            pob = 3 + ((i // 4) % 2)
            js = [jj for jj in (i - 1, i, i + 1) if 0 <= jj < NT]
            for n_, jj in enumerate(js):
                c0 = i * 128 - q0_of(jj)
                sc.mm(PS(pob)[:, (i % 4) * 128:(i % 4 + 1) * 128], va[:, jj, g, :], pT[jj % 4][:, c0:c0 + 128],
                      n_ == 0, n_ == len(js) - 1, ['va', 'va_ones', 'pT%d' % (jj % 4)], [pk(pob)])
            if i % 4 == 3:
                t0 = (i // 4) * 512
                attn_norm_store(pob, 512, esink[64:128, h:h + 1], mixd[f, pr, t0:t0 + 512], rd, mixt)

        s_step(0)
        s_step(1)
        for j in range(2, NT):
            s_step(j)
            pv_step(j - 2)
        pv_step(NT - 2)
        pv_step(NT - 1)
    sc.barrier()
    ar.release(mW)
    if stop_after == 'L0W':
        sc.finalize()
        return nc

    mM = ar.mark()
    wq = ar.alloc([3, 768], BF16)
    wqs = ar.alloc([3, 768], BF16)
    wkv = ar.alloc([2, 1024], BF16)
    QT = [ar.alloc([S], BF16) for _ in range(2)]
    vm = [ar.alloc([NT, 128], BF16) for _ in range(2)]
    csq = [ar.alloc([512], F32) for _ in range(2)]
    snq = [ar.alloc([512], F32) for _ in range(2)]
    u1 = ar.alloc([512], F32)
    u2 = ar.alloc([512], F32)
    pbuf = [ar.alloc([512], BF16) for _ in range(4)]
    rd = ar.alloc([512], F32)
    mixt = ar.alloc([512], BF16)
    sc.dma('pool', wq, wq_d.rearrange("(k p) f -> p k f", p=128), (), ['wq'])
    sc.dma('pool', wqs, wqs_d.rearrange("(k p) f -> p k f", p=128), (), ['wqs'])
    sc.dma('pool', wkv, wkv_d.rearrange("(k p) f -> p k f", p=128), (), ['wkv'])
    for b in range(2):
        sc.memset('pool', vm[b][:, :, 64:128], 1.0, ['vm_ones%d' % b])
    scale_b = 96.0 ** -0.5

    def mla_proj(h):
        b = h % 2
        for i in range(8):
            tl = slice(i * 512, (i + 1) * 512)
            cb = i % 2
            sc.dma('sp', csq[cb][64:96, :], cos_d[:, tl], (), ['csq%d' % cb])
            sc.dma('sp', snq[cb][64:96, :], sin_d[:, tl], (), ['snq%d' % cb])
            for r in range(3):
                sc.mm(PS(5)[0:96, :], wq[:, r, h * 96:(h + 1) * 96], cqn[:, r, tl], r == 0, r == 2, ['wq', 'cqn'], [pk(5)])
            for r in range(3):
                sc.mm(PS(6)[0:96, :], wqs[:, r, h * 96:(h + 1) * 96], cqn[:, r, tl], r == 0, r == 2, ['wqs', 'cqn'], [pk(6)])
            sc.copy('act', QT[b][0:64, tl], PS(5)[0:64, :], [pk(5)], ['QTn%d' % b, 'psord'])
            sc.tt('dve', u1[64:96, :], PS(5)[64:96, :], csq[cb][64:96, :], ALU.mult, [pk(5), 'csq%d' % cb, 'psord'], ['u1'])
            sc.tt('dve', u2[64:96, :], PS(6)[64:96, :], snq[cb][64:96, :], ALU.mult, [pk(6), 'snq%d' % cb], ['u2'])
            sc.tt('pool', QT[b][64:96, tl], u1[64:96, :], u2[64:96, :], ALU.add, ['u1', 'u2'], ['QTp%d' % b])
            for c in range(2):
                sc.mm(PS(7)[0:64, :], wkv[:, c, h * 128:h * 128 + 64], ckvn[:, c, tl], c == 0, c == 1, ['wkv', 'ckvn'], [pk(7)])
            sc.copy('act', KT[b][0:64, tl], PS(7)[0:64, :], [pk(7)], ['KTn%d' % b])
        for g4 in range(4):
            for t8 in range(8):
                tc = g4 * 8 + t8
                for c in range(2):
                    sc.mm(PS(7)[:, t8 * 64:(t8 + 1) * 64], ckvn[:, c, tc * 128:(tc + 1) * 128],
                          wkv[:, c, h * 128 + 64:h * 128 + 128], c == 0, c == 1, ['ckvn', 'wkv'], [pk(7)])
            sc.copy('dve', vm[b][:, g4 * 8:(g4 + 1) * 8, 0:64], PS(7).rearrange("p (t d) -> p t d", t=8), [pk(7)], ['vm%d' % b])

    def mla_attn(h):
        b = h % 2
        f = 4 + h // 2
        rbs = (h % 2) * 64
        kkeys = ['KTn%d' % b, 'KT%dpe' % b]
        qkeys = ['QTn%d' % b, 'QTp%d' % b]
        cnt = 0
        for i in range(8):
            tl = slice(i * 512, (i + 1) * 512)
            pob = 3 + (i % 2)

            def s_step(kc, slot):
                bank = slot % 3
                sc.mm(PS(bank), KT[b][0:96, kc * 128:(kc + 1) * 128], QT[b][0:96, tl], True, True, kkeys + qkeys, [pk(bank)])
                sc.act(pbuf[slot % 4], PS(bank), AF.Exp, [pk(bank)], ['pb%d' % (slot % 4)], scale=scale_b)

            def pv_step(kc, slot):
                sc.mm(PS(pob), vm[b][:, kc, :], pbuf[slot % 4], kc == 0, kc == NT - 1,
                      ['vm%d' % b, 'vm_ones%d' % b, 'pb%d' % (slot % 4)], [pk(pob)])

            s_step(0, cnt)
            s_step(1, cnt + 1)
            for kc in range(NT):
                if kc + 2 < NT:
                    s_step(kc + 2, cnt + kc + 2)
                pv_step(kc, cnt + kc)
            cnt += NT
            attn_norm_store(pob, 512, None, mixd[f, rbs:rbs + 64, tl], rd, mixt)

    mla_proj(0)
    for h in range(8):
        if h + 1 < 8:
            mla_proj(h + 1)
        mla_attn(h)
    sc.barrier()
    ar.release(base_mark)

    if debug:
        mD = ar.mark()
        mtb = ar.alloc([S], BF16)
        mtf = ar.alloc([S], F32)
        for f in range(8):
            sc.dma('sp', mtb, mixd[f], ['mixd'], ['mtb'])
            sc.copy('dve', mtf, mtb, ['mtb'], ['mtf'])
            sc.dma('sp', dbg["dbg_mix"][f], mtf, ['mtf'], ['dbg_mix'])
        sc.barrier()
        ar.release(mD)

    def resid0_load(tc, xcb, xck):
        sc.dma('sp', xcb, x[tc * 128:(tc + 1) * 128, :], (), [xck])

    def resid0(tc, pa, pb_, rbuf, rbk, xcb, xck):
        for half, bank in ((0, pa), (1, pb_)):
            sc.stt('dve', rbuf[:, half * 512:(half + 1) * 512], xcb[:, half * 512:(half + 1) * 512], ALPHA, PS(bank),
                   ALU.mult, ALU.add, [xck, pk(bank)], [rbk])

    affT, _ = outproj_ln_phase(0, wo_d[0], "ln0a_g", "ln0a_b", resid0_load, resid0)
    if stop_after == 'L0O':
        sc.finalize()
        return nc

    moe_phase(0, affT)
    ar.release(base_mark)
    if debug:
        mD = ar.mark()
        yb = ar.alloc([D], F32)
        for tc in range(NT):
            sc.dma('sp', yb, Yd[tc * 128:(tc + 1) * 128, :], ['Yd'], ['yb'])
            sc.dma('sp', dbg["dbg_y"][tc * 128:(tc + 1) * 128, :], yb, ['yb'], ['dbg_y'])
        sc.barrier()
        ar.release(mD)
    if stop_after == 'MOE0':
        sc.finalize()
        return nc

    h2T = ar.alloc([8, S], BF16)
    mL1 = ar.mark()
    gB = ar.alloc([D], F32)
    bB = ar.alloc([D], F32)
    yc = [ar.alloc([D], F32) for _ in range(2)]
    hbuf = [ar.alloc([D], F32) for _ in range(2)]
    ahb = [ar.alloc([D], F32) for _ in range(2)]
    hbf = [ar.alloc([D], BF16) for _ in range(2)]
    st = [ar.alloc([2, 6], F32) for _ in range(2)]
    mv = [ar.alloc([2], F32) for _ in range(2)]
    sd = [ar.alloc([1], F32) for _ in range(2)]
    load_bcast(gB, "ln0b_g", 'lng')
    load_bcast(bB, "ln0b_b", 'lnb')
    sc.dma('sp', yc[0], Yd[0:128, :], ['Yd'], ['yc0'])
    sc.dma('sp', yc[1], Yd[128:256, :], ['Yd'], ['yc1'])
    ln_stats(yc[0], 'yc0', st[0], mv[0], sd[0], '0')
    for tc in range(NT):
        b = tc % 2
        sf = str(b)
        rows = slice(tc * 128, (tc + 1) * 128)
        if tc + 1 < NT:
            ln_stats(yc[1 - b], 'yc%d' % (1 - b), st[1 - b], mv[1 - b], sd[1 - b], str(1 - b))
        ln_apply(yc[b], 'yc%d' % b, gB, bB, hbuf[b], 'hbuf' + sf, mv[b], sd[b], sf)
        if tc + 2 < NT:
            sc.dma('sp', yc[b], Yd[(tc + 2) * 128:(tc + 3) * 128, :], ['Yd'], ['yc%d' % b])
        sc.act(ahb[b], hbuf[b], AF.Copy, ['hbuf' + sf], ['ahb' + sf], scale=ALPHA)
        sc.dma('sp', R1[rows, :], ahb[b], ['ahb' + sf], ['R1'])
        sc.copy('act', hbf[b], hbuf[b], ['hbuf' + sf], ['hbf' + sf])
        tb = tc % 2
        for k in range(8):
            sc.tr(PSB(tb)[:, k * 128:(k + 1) * 128], hbf[b][:, k * 128:(k + 1) * 128], identb, ['hbf' + sf, 'identb'], [pk(tb)])
        sc.copy('act', h2T[:, :, rows], PSB(tb).rearrange("p (k t) -> p k t", k=8), [pk(tb)], ['h2T'])
    sc.barrier()
    ar.release(mL1)

    wp = [ar.alloc([8, 384], BF16) for _ in range(2)]
    QTp = [ar.alloc([S], BF16) for _ in range(2)]
    KTp = [ar.alloc([S], BF16) for _ in range(2)]
    Vp = [ar.alloc([NT, 2, 128], BF16) for _ in range(2)]
    TTp = [ar.alloc([2, 16, 64], F32) for _ in range(2)]
    TTb = [ar.alloc([2, 16, 64], BF16) for _ in range(2)]
    namask = ar.alloc([16, 64], F32)
    tbn = [ar.alloc([5, 64], F32) for _ in range(4)]
    pTn = [ar.alloc([5, 64], BF16) for _ in range(5)]
    rd = ar.alloc([512], F32)
    mixt = ar.alloc([512], BF16)
    sc.dma('sp', namask, namask_d, (), ['namask'])
    for b in range(2):
        sc.memset('pool', Vp[b][:, :, :, 64:128], 1.0, ['vp_ones%d' % b])
    scale_c = 64.0 ** -0.5

    def na_proj(hp):
        b = hp % 2
        for part in range(3):
            sc.dma('pool', wp[b][:, :, part * 128:(part + 1) * 128],
                   wqkv_d[:, part * 1024 + hp * 128: part * 1024 + (hp + 1) * 128].rearrange("(k p) f -> p k f", p=128),
                   (), ['wp%d' % b])
        for g in range(2):
            sc.dma('sp', TTp[b][:, g], rpbT_d[hp * 2 + g], (), ['TT%d_%d' % (b, g)])
            sc.tt('pool', TTp[b][:, g], TTp[b][:, g], namask, ALU.add, ['TT%d_%d' % (b, g), 'namask'], ['TT%d_%d' % (b, g)])
            sc.ts('pool', TTb[b][:, g], TTp[b][:, g], 1.0 / scale_c, None, ALU.mult, None, ['TT%d_%d' % (b, g)], ['TTb%d_%d' % (b, g)])
        for i in range(8):
            tl = slice(i * 512, (i + 1) * 512)
            for k in range(8):
                sc.mm(PS(5), wp[b][:, k, 0:128], h2T[:, k, tl], k == 0, k == 7, ['wp%d' % b, 'h2T'], [pk(5)])
            sc.copy('act', QTp[b][:, tl], PS(5), [pk(5)], ['QTp%d' % b])
            for k in range(8):
                sc.mm(PS(6), wp[b][:, k, 128:256], h2T[:, k, tl], k == 0, k == 7, ['wp%d' % b, 'h2T'], [pk(6)])
            sc.copy('dve', KTp[b][:, tl], PS(6), [pk(6)], ['KTp%d' % b])
            for c in range(4):
                tc = i * 4 + c
                for k in range(8):
                    sc.mm(PS(5)[:, c * 128:(c + 1) * 128], h2T[:, k, tc * 128:(tc + 1) * 128], wp[b][:, k, 256:384],
                          k == 0, k == 7, ['h2T', 'wp%d' % b], [pk(5)])
            sc.copy('act', Vp[b][:, i * 4:(i + 1) * 4, :, 0:64], PS(5).rearrange("p (c g d) -> p c g d", c=4, g=2),
                    [pk(5)], ['Vp%d' % b])

    def na_attn(hp):
        b = hp % 2
        tasks = [(g, r) for g in range(2) for r in range(64)]

        def geom(r):
            rs = min(max(r - 4, 0), 56)
            odd = rs % 2
            kr0 = rs - odd
            nch = 5 if odd else 4
            return odd, kr0, nch, kr0 - r + 8

        def s_part(t):
            g, r = tasks[t]
            pr = slice(g * 64, g * 64 + 64)
            odd, kr0, nch, u0 = geom(r)
            bank = (0, 1, 2, 7)[t % 4]
            sc.mm(PS(bank)[:, 0:nch * 64].rearrange("p (c q) -> p c q", c=nch), identb,
                  TTb[b][:, g, u0:u0 + 2 * nch - 1:2, :], True, False, ['identb', 'TTb%d_%d' % (b, g)], [pk(bank)])
            for c in range(nch):
                kc = kr0 // 2 + c
                sc.mm(PS(bank)[:, c * 64:(c + 1) * 64], KTp[b][pr, kc * 128:(kc + 1) * 128], QTp[b][pr, r * 64:(r + 1) * 64],
                      False, c == nch - 1, ['KTp%d' % b, 'QTp%d' % b], [pk(bank)])
            sc.act(pTn[t % 5][:, 0:nch, :], PS(bank)[:, 0:nch * 64].rearrange("p (c q) -> p c q", c=nch), AF.Exp,
                   [pk(bank)], ['pTn%d' % (t % 5)], scale=scale_c)

        def pv_part(t):
            g, r = tasks[t]
            pr = slice(g * 64, g * 64 + 64)
            odd, kr0, nch, u0 = geom(r)
            pob = 3 + ((r // 8) % 2)
            pkk = 'pTn%d' % (t % 5)
            for c in range(nch):
                kc = kr0 // 2 + c
                if odd and c == 0:
                    ps_ = slice(64, 128)
                elif odd and c == nch - 1:
                    ps_ = slice(0, 64)
                else:
                    ps_ = slice(0, 128)
                sc.mm(PS(pob)[:, (r % 8) * 64:(r % 8 + 1) * 64], Vp[b][ps_, kc, g, :], pTn[t % 5][ps_, c, :],
                      c == 0, c == nch - 1, ['Vp%d' % b, 'vp_ones%d' % b, pkk], [pk(pob)])
            if r % 8 == 7:
                t0 = (r // 8) * 512
                attn_norm_store(pob, 512, None, mixd[hp, pr, t0:t0 + 512], rd, mixt)

        LA = 3
        for t in range(min(LA, len(tasks))):
            s_part(t)
        for t in range(len(tasks)):
            if t + LA < len(tasks):
                s_part(t + LA)
            pv_part(t)

    na_proj(0)
    for hp in range(8):
        if hp + 1 < 8:
            na_proj(hp + 1)
        na_attn(hp)
    sc.barrier()
    ar.release(base_mark)

    def resid1_load(tc, xcb, xck):
        sc.dma('sp', xcb, R1[tc * 128:(tc + 1) * 128, :], ['R1'], [xck])

    def resid1(tc, pa, pb_, rbuf, rbk, xcb, xck):
        for half, bank in ((0, pa), (1, pb_)):
            sc.tt('dve', rbuf[:, half * 512:(half + 1) * 512], xcb[:, half * 512:(half + 1) * 512], PS(bank),
                  ALU.add, [xck, pk(bank)], [rbk])

    affT, _ = outproj_ln_phase(1, wo_d[1], "ln1a_g", "ln1a_b", resid1_load, resid1)
    moe_phase(1, affT)
    ar.release(base_mark)

    gB = ar.alloc([D], F32)
    bB = ar.alloc([D], F32)
    yc = [ar.alloc([D], F32) for _ in range(2)]
    hb2 = [ar.alloc([D], F32) for _ in range(2)]
    st = [ar.alloc([2, 6], F32) for _ in range(2)]
    mv = [ar.alloc([2], F32) for _ in range(2)]
    sd = [ar.alloc([1], F32) for _ in range(2)]
    load_bcast(gB, "ln1b_g", 'lng')
    load_bcast(bB, "ln1b_b", 'lnb')
    sc.dma('sp', yc[0], Yd[0:128, :], ['Yd'], ['yc0'])
    sc.dma('sp', yc[1], Yd[128:256, :], ['Yd'], ['yc1'])
    ln_stats(yc[0], 'yc0', st[0], mv[0], sd[0], '0')
    for tc in range(NT):
        b = tc % 2
        rows = slice(tc * 128, (tc + 1) * 128)
        if tc + 1 < NT:
            ln_stats(yc[1 - b], 'yc%d' % (1 - b), st[1 - b], mv[1 - b], sd[1 - b], str(1 - b))
        ln_apply(yc[b], 'yc%d' % b, gB, bB, hb2[b], 'hb%d' % b, mv[b], sd[b], str(b))
        if tc + 2 < NT:
            sc.dma('sp', yc[b], Yd[(tc + 2) * 128:(tc + 3) * 128, :], ['Yd'], ['yc%d' % b])
        sc.dma('sp', out_d[rows, :], hb2[b], ['hb%d' % b], ['out'])
    sc.finalize()
    return nc


def _consts():
    c = {}
    c["ident_bf"] = np.eye(128, dtype=np.float32).astype(ml_dtypes.bfloat16)
    c["ident_f"] = np.eye(128, dtype=np.float32)
    half = 16
    inv = (10000.0 ** (-np.arange(half, dtype=np.float32) / half)).astype(np.float32)
    ang = np.arange(S, dtype=np.float32)[None, :] * inv[:, None]
    cos = np.cos(ang).astype(np.float32)
    sin = np.sin(ang).astype(np.float32)
    c["cos_t"] = np.concatenate([cos, cos], 0)
    c["sin_t"] = np.concatenate([-sin, sin], 0)
    k = np.arange(128)[:, None]
    qp = np.arange(384)[None, :]
    dist = np.abs(qp - 128 - k).astype(np.float32)
    slopes = 2.0 ** (-8.0 * (np.arange(8, dtype=np.float32) + 1.0) / 8)
    bw = np.where(dist[:, None, :] <= 128, -slopes[None, :, None] * dist[:, None, :], NEG).astype(np.float32)
    bws = (bw.astype(np.float64) / (64.0 ** -0.5)).astype(np.float32)
    hi = bws.astype(ml_dtypes.bfloat16)
    lo = (bws - hi.astype(np.float32)).astype(ml_dtypes.bfloat16)
    c["biasw_hi"] = np.ascontiguousarray(hi)
    c["biasw_lo"] = np.ascontiguousarray(lo)
    cols = np.arange(64)
    cstart = np.clip(cols - 8, 0, 48)
    valid = (cols[None, :] >= cstart[:, None]) & (cols[None, :] < cstart[:, None] + 16)
    m = np.where(valid.T, 0.0, NEG).astype(np.float32)
    m2 = np.concatenate([m, m], 0)
    c["namask"] = np.ascontiguousarray(np.broadcast_to(m2[:, None, :], (128, 16, 64)))
    c["cvals"] = (np.arange(4)[None, :] * 128 + np.arange(128)[:, None]).astype(np.float32)
    pi = np.arange(128)
    same = (pi[:, None] // 8) == (pi[None, :] // 8)
    c["gmat"] = same.astype(np.float32)
    c["lmat"] = (same & (pi[:, None] < pi[None, :])).astype(np.float32)
    return c


def _prep_shared(inp):
    f32 = np.float32
    d = {}
    w_in0 = np.asarray(inp["w_in0"], f32)
    qa = w_in0[:, 0:512]
    ka0, ka1 = w_in0[:, 512:576], w_in0[:, 576:640]
    vaw = w_in0[:, 640:768]
    cq = w_in0[:, 768:1152]
    ckv = w_in0[:, 1152:1408]
    kr = w_in0[:, 1408:1440]
    krs = np.concatenate([kr[:, 16:32], kr[:, 0:16]], 1)
    d["w0f"] = np.ascontiguousarray(np.concatenate(
        [qa, ka0, ka0, ka1, ka1, cq, ckv, kr, kr, kr, kr, krs, krs, krs, krs], 1))
    d["w0v"] = np.ascontiguousarray(vaw)
    d["a_sink"] = np.asarray(inp["a_sink"], f32).reshape(1, 8)
    d["gq"] = np.ascontiguousarray(np.asarray(inp["mla_q_norm"], f32).reshape(3, 128).T)
    d["gkv"] = np.ascontiguousarray(np.asarray(inp["mla_kv_norm"], f32).reshape(2, 128).T)
    wq = np.asarray(inp["w_q_up"], f32)
    d["wq"] = wq
    wq3 = wq.reshape(384, 8, 96)
    d["wqs"] = np.ascontiguousarray(
        np.concatenate([wq3[:, :, 0:64], wq3[:, :, 80:96], wq3[:, :, 64:80]], 2).reshape(384, 768))
    d["wkv"] = np.asarray(inp["w_kv_up"], f32)
    d["wo0"] = np.asarray(inp["w_out0"], f32)
    d["wo1"] = np.asarray(inp["w_out1"], f32)
    for n in ["ln0a_g", "ln0a_b", "ln0b_g", "ln0b_b", "ln1a_g", "ln1a_b", "ln1b_g", "ln1b_b"]:
        d[n] = np.asarray(inp[n], f32).reshape(1, D)
    for n in ["router0", "router1", "w_gate0", "w_gate1", "w_up0", "w_up1", "w_down0", "w_down1", "w_qkv1"]:
        d[n] = np.asarray(inp[n], f32)
    rpb = np.asarray(inp["na_rpb"], f32)
    cols = np.arange(64)
    dc = np.clip(cols[None, :] - cols[:, None] + 15, 0, 30)
    u = np.arange(16)
    out = np.empty((16, 2, 64, 16, 64), f32)
    for kr2 in range(2):
        dr = np.clip(u + kr2 - 8, -7, 7) + 7
        g_ = rpb[:, dr[:, None, None], dc[None, :, :]]
        out[:, kr2] = np.transpose(g_, (0, 3, 1, 2))
    d["rpbT"] = np.ascontiguousarray(out.reshape(16, 128, 16, 64))
    d.update(_consts())
    return d


_CACHE = {}


def kernel(**inputs):
    x = np.asarray(inputs["x"], np.float32)
    shared = _prep_shared(inputs)
    if "nc" not in _CACHE:
        _CACHE["nc"] = build_program()
    nc = _CACHE["nc"]
    in_maps = []
    for c in range(N_CORES):
        m = dict(shared)
        m["x"] = np.ascontiguousarray(x[c])
        in_maps.append(m)
    res = run_bass_kernel_spmd(nc, in_maps, core_ids=list(range(N_CORES)))
    return np.stack([np.asarray(r["out"], np.float32) for r in res.results], 0)
```
